# Optimizing a Trainium2 kernel written in Bass

```python
import math
import jax, jax.numpy as jnp
from jax import lax
import numpy as np

D_MODEL = 1024
BATCH = 16
SEQ = 2048
DEPTH = 2
DEC_BATCH = 8
DEC_SEQ = 4096
PAST_LEN = 128

GRID_W = 64
HEAD_DIM = 64
NA_HEADS = D_MODEL // (2 * HEAD_DIM)
GQA_HEADS = D_MODEL // (2 * HEAD_DIM)
GQA_KV_HEADS = GQA_HEADS // 4
DIFF_HEADS = D_MODEL // (2 * HEAD_DIM)
NA_WIN_H = 8
NA_WIN_W = 16
Q_BLOCK = 128
ROPE_THETA = 10000.0
D_FF = -(-8 * D_MODEL // (3 * 256)) * 256
NA_W = NA_HEADS * HEAD_DIM
GQ_W = GQA_HEADS * HEAD_DIM
GKV_W = GQA_KV_HEADS * HEAD_DIM
IN0_W = 3 * NA_W + GQ_W + 2 * GKV_W
IN1_W = 3 * D_MODEL
LN_EPS = 1e-5
RMS_EPS = 1e-6
SUBLN_EPS = 1e-5

kernel_name = 'hybrid_na_gqa_diffattn_encoder'


def layer_norm(x, g, b):
    xf = x.astype(jnp.float32)
    mu = jnp.mean(xf, -1, keepdims=True)
    var = jnp.mean(jnp.square(xf - mu), -1, keepdims=True)
    return ((xf - mu) * lax.rsqrt(var + LN_EPS) * g + b).astype(x.dtype)


def rms_norm(x, g, eps):
    xf = x.astype(jnp.float32)
    return (xf * lax.rsqrt(jnp.mean(jnp.square(xf), -1, keepdims=True) + eps) * g).astype(x.dtype)


def rope_cos_sin(pos, dim):
    inv = ROPE_THETA ** (-jnp.arange(0, dim, 2, dtype=jnp.float32) / dim)
    ang = pos.astype(jnp.float32)[:, None] * inv[None, :]
    return jnp.cos(ang), jnp.sin(ang)


def apply_rope(x, cos, sin):
    xf = x.astype(jnp.float32)
    x1, x2 = jnp.split(xf, 2, axis=-1)
    c = cos[:, None, :]
    s = sin[:, None, :]
    return jnp.concatenate([x1 * c - x2 * s, x1 * s + x2 * c], axis=-1).astype(x.dtype)


def apply_axial_rope(x, row_cs, col_cs):
    half = x.shape[-1] // 2
    return jnp.concatenate([apply_rope(x[..., :half], *row_cs), apply_rope(x[..., half:], *col_cs)], axis=-1)


def sweep_query_blocks(fn, q):
    B, L = q.shape[0], q.shape[1]
    nb = L // Q_BLOCK
    qb = jnp.moveaxis(q.reshape((B, nb, Q_BLOCK) + q.shape[2:]), 1, 0)
    out = lax.map(fn, qb)
    return jnp.moveaxis(out, 0, 1).reshape((B, L) + out.shape[3:])


def neighbourhood_attention(q, k, v, rpb):
    B, L, H, dh = q.shape
    rows = L // GRID_W
    kh = min(NA_WIN_H, rows)
    kw = NA_WIN_W
    qg = (q * (dh ** -0.5)).reshape(B, rows, GRID_W, H, dh)
    kg = k.reshape(B, rows, GRID_W, H, dh)
    vg = v.reshape(B, rows, GRID_W, H, dh)
    cols = jnp.arange(GRID_W)
    col_start = jnp.clip(cols - kw // 2, 0, GRID_W - kw)
    col_idx = col_start[:, None] + jnp.arange(kw)[None, :]
    col_bias_idx = col_idx - cols[:, None] + (NA_WIN_W - 1)

    def row_block(args):
        r, q_row = args
        r0 = jnp.clip(r - kh // 2, 0, rows - kh)
        k_rows = lax.dynamic_slice_in_dim(kg, r0, kh, axis=1)
        v_rows = lax.dynamic_slice_in_dim(vg, r0, kh, axis=1)
        k_win = jnp.take(k_rows, col_idx, axis=2)
        v_win = jnp.take(v_rows, col_idx, axis=2)
        s = jnp.einsum('bqhd,bjqkhd->bhqjk', q_row, k_win).astype(jnp.float32)
        row_bias_idx = r0 + jnp.arange(kh) - r + (NA_WIN_H - 1)
        bias = rpb[:, row_bias_idx[:, None, None], col_bias_idx[None, :, :]]
        s = s + jnp.transpose(bias, (0, 2, 1, 3))[None].astype(jnp.float32)
        p = jax.nn.softmax(s.reshape(B, H, GRID_W, kh * kw), axis=-1)
        p = p.reshape(B, H, GRID_W, kh, kw).astype(v.dtype)
        return jnp.einsum('bhqjk,bjqkhd->bqhd', p, v_win)

    out = lax.map(row_block, (jnp.arange(rows), jnp.moveaxis(qg, 1, 0)))
    return jnp.moveaxis(out, 0, 1).reshape(B, L, H * dh)


def gqa_axial_attention(q, k, v, g_q, g_k, row_cs, col_cs):
    B, L = q.shape[0], q.shape[1]
    q = apply_axial_rope(rms_norm(q, g_q, RMS_EPS), row_cs, col_cs)
    k = apply_axial_rope(rms_norm(k, g_k, RMS_EPS), row_cs, col_cs)
    group = GQA_HEADS // GQA_KV_HEADS
    q = (q * (HEAD_DIM ** -0.5)).reshape(B, L, GQA_KV_HEADS, group, HEAD_DIM)

    def block(qb):
        s = jnp.einsum('bqkgd,bskd->bkgqs', qb, k).astype(jnp.float32)
        p = jax.nn.softmax(s, axis=-1).astype(v.dtype)
        return jnp.einsum('bkgqs,bskd->bqkgd', p, v)

    return sweep_query_blocks(block, q).reshape(B, L, GQ_W)


def differential_attention(q, k, v, lq1, lk1, lq2, lk2, g_subln, lambda_init, seq_cs):
    B, L = q.shape[0], q.shape[1]
    q = apply_rope(q, *seq_cs)
    k = apply_rope(k, *seq_cs)
    q = (q * (HEAD_DIM ** -0.5)).reshape(B, L, DIFF_HEADS, 2, HEAD_DIM)
    k = k.reshape(B, L, DIFF_HEADS, 2, HEAD_DIM)
    lam = (jnp.exp(jnp.sum(lq1.astype(jnp.float32) * lk1.astype(jnp.float32)))
           - jnp.exp(jnp.sum(lq2.astype(jnp.float32) * lk2.astype(jnp.float32))) + lambda_init)

    def block(qb):
        s = jnp.einsum('bqhcd,bshcd->bhcqs', qb, k).astype(jnp.float32)
        p = jax.nn.softmax(s, axis=-1)
        a = (p[:, :, 0] - lam * p[:, :, 1]).astype(v.dtype)
        return jnp.einsum('bhqs,bshe->bqhe', a, v)

    o = sweep_query_blocks(block, q)
    o = rms_norm(o, g_subln, SUBLN_EPS) * (1.0 - lambda_init)
    return o.reshape(B, L, D_MODEL)


def na_gqa_mixer(x, w_in, rpb, g_q, g_k, w_out, row_cs, col_cs):
    B, L, _ = x.shape
    h = x @ w_in
    cuts = [NA_W, 2 * NA_W, 3 * NA_W, 3 * NA_W + GQ_W, 3 * NA_W + GQ_W + GKV_W]
    na_q, na_k, na_v, gq, gk, gv = jnp.split(h, cuts, axis=-1)
    a_out = neighbourhood_attention(na_q.reshape(B, L, NA_HEADS, HEAD_DIM),
                                    na_k.reshape(B, L, NA_HEADS, HEAD_DIM),
                                    na_v.reshape(B, L, NA_HEADS, HEAD_DIM), rpb)
    b_out = gqa_axial_attention(gq.reshape(B, L, GQA_HEADS, HEAD_DIM),
                                gk.reshape(B, L, GQA_KV_HEADS, HEAD_DIM),
                                gv.reshape(B, L, GQA_KV_HEADS, HEAD_DIM), g_q, g_k, row_cs, col_cs)
    return jnp.concatenate([a_out, b_out], axis=-1) @ w_out


def diff_mixer(x, w_in, lq1, lk1, lq2, lk2, g_subln, w_out, lambda_init, seq_cs):
    B, L, _ = x.shape
    q, k, v = jnp.split(x @ w_in, 3, axis=-1)
    o = differential_attention(q.reshape(B, L, 2 * DIFF_HEADS, HEAD_DIM),
                               k.reshape(B, L, 2 * DIFF_HEADS, HEAD_DIM),
                               v.reshape(B, L, DIFF_HEADS, 2 * HEAD_DIM),
                               lq1, lk1, lq2, lk2, g_subln, lambda_init, seq_cs)
    return o @ w_out


def swiglu(x, wg, wu, wd):
    return (jax.nn.silu(x @ wg) * (x @ wu)) @ wd


def encoder_trunk(x, w_in_mix0, rpb_na, g_q_gqa, g_k_gqa, w_out_mix0,
                  w_in_mix1, lam_q1, lam_k1, lam_q2, lam_k2, g_subln, w_out_mix1,
                  ln_mix_g, ln_mix_b, w_ffn_gate, w_ffn_up, w_ffn_down, ln_ffn_g, ln_ffn_b):
    L = x.shape[1]
    t = jnp.arange(L)
    row_cs = rope_cos_sin(t // GRID_W, HEAD_DIM // 2)
    col_cs = rope_cos_sin(t % GRID_W, HEAD_DIM // 2)
    seq_cs = rope_cos_sin(t, HEAD_DIM)
    alpha = (2.0 * DEPTH) ** 0.25
    for i in range(DEPTH):
        j = i // 2
        if i % 2 == 0:
            mix = na_gqa_mixer(x, w_in_mix0[j], rpb_na[j], g_q_gqa[j], g_k_gqa[j], w_out_mix0[j], row_cs, col_cs)
        else:
            lambda_init = 0.8 - 0.6 * math.exp(-0.3 * i)
            mix = diff_mixer(x, w_in_mix1[j], lam_q1[j], lam_k1[j], lam_q2[j], lam_k2[j], g_subln[j],
                             w_out_mix1[j], lambda_init, seq_cs)
        x = layer_norm(alpha * x + mix, ln_mix_g[i], ln_mix_b[i])
        x = layer_norm(alpha * x + swiglu(x, w_ffn_gate[i], w_ffn_up[i], w_ffn_down[i]), ln_ffn_g[i], ln_ffn_b[i])
    return x


def setup_inputs(seed: int = 0) -> dict:
    key = jax.random.key(seed)
    ks = jax.random.split(key, 24)
    n_even = (DEPTH + 1) // 2
    n_odd = DEPTH // 2
    beta = (8.0 * DEPTH) ** -0.25
    f32 = jnp.float32
    nrm = lambda k, shape, s: jax.random.normal(k, shape, f32) * s
    return {
        'x_prompt': nrm(ks[0], (BATCH, SEQ, D_MODEL), 1.0),
        'x_sample': nrm(ks[1], (DEC_BATCH, DEC_SEQ, D_MODEL), 1.0),
        'w_in_mix0': nrm(ks[2], (n_even, D_MODEL, IN0_W), D_MODEL ** -0.5),
        'rpb_na': nrm(ks[3], (n_even, NA_HEADS, 2 * NA_WIN_H - 1, 2 * NA_WIN_W - 1), 0.1),
        'g_q_gqa': 1.0 + nrm(ks[4], (n_even, HEAD_DIM), 0.1),
        'g_k_gqa': 1.0 + nrm(ks[5], (n_even, HEAD_DIM), 0.1),
        'w_out_mix0': nrm(ks[6], (n_even, D_MODEL, D_MODEL), beta * D_MODEL ** -0.5),
        'w_in_mix1': nrm(ks[7], (n_odd, D_MODEL, IN1_W), D_MODEL ** -0.5),
        'lam_q1': nrm(ks[8], (n_odd, HEAD_DIM), 0.1),
        'lam_k1': nrm(ks[9], (n_odd, HEAD_DIM), 0.1),
        'lam_q2': nrm(ks[10], (n_odd, HEAD_DIM), 0.1),
        'lam_k2': nrm(ks[11], (n_odd, HEAD_DIM), 0.1),
        'g_subln': 1.0 + nrm(ks[12], (n_odd, 2 * HEAD_DIM), 0.1),
        'w_out_mix1': nrm(ks[13], (n_odd, D_MODEL, D_MODEL), beta * D_MODEL ** -0.5),
        'ln_mix_g': 1.0 + nrm(ks[14], (DEPTH, D_MODEL), 0.05),
        'ln_mix_b': nrm(ks[15], (DEPTH, D_MODEL), 0.02),
        'w_ffn_gate': nrm(ks[16], (DEPTH, D_MODEL, D_FF), D_MODEL ** -0.5),
        'w_ffn_up': nrm(ks[17], (DEPTH, D_MODEL, D_FF), D_MODEL ** -0.5),
        'w_ffn_down': nrm(ks[18], (DEPTH, D_FF, D_MODEL), beta * D_FF ** -0.5),
        'ln_ffn_g': 1.0 + nrm(ks[19], (DEPTH, D_MODEL), 0.05),
        'ln_ffn_b': nrm(ks[20], (DEPTH, D_MODEL), 0.02),
    }


def reference(x_prompt, x_sample, w_in_mix0, rpb_na, g_q_gqa, g_k_gqa, w_out_mix0,
              w_in_mix1, lam_q1, lam_k1, lam_q2, lam_k2, g_subln, w_out_mix1,
              ln_mix_g, ln_mix_b, w_ffn_gate, w_ffn_up, w_ffn_down, ln_ffn_g, ln_ffn_b):
    y_prompt = encoder_trunk(x_prompt, w_in_mix0, rpb_na, g_q_gqa, g_k_gqa, w_out_mix0,
                             w_in_mix1, lam_q1, lam_k1, lam_q2, lam_k2, g_subln, w_out_mix1,
                             ln_mix_g, ln_mix_b, w_ffn_gate, w_ffn_up, w_ffn_down, ln_ffn_g, ln_ffn_b)
    y_sample = encoder_trunk(x_sample, w_in_mix0, rpb_na, g_q_gqa, g_k_gqa, w_out_mix0,
                             w_in_mix1, lam_q1, lam_k1, lam_q2, lam_k2, g_subln, w_out_mix1,
                             ln_mix_g, ln_mix_b, w_ffn_gate, w_ffn_up, w_ffn_down, ln_ffn_g, ln_ffn_b)
    return (y_prompt, y_sample)
```

```python
import math
from contextlib import ExitStack

import numpy as np
import concourse.bass as bass
import concourse.mybir as mybir
from concourse.bass_utils import run_bass_kernel_spmd

F32 = mybir.dt.float32
BF16 = mybir.dt.bfloat16
U8 = mybir.dt.uint8
AF = mybir.ActivationFunctionType
ALU = mybir.AluOpType
AX = mybir.AxisListType

D = 1024
DFF = 2816
NCH = DFF // 128
GRID_W = 64
IN0_W = 2304
IN1_W = 3072
ALPHA = (2.0 * 2) ** 0.25
LAMBDA_INIT = 0.8 - 0.6 * math.exp(-0.3 * 1)
LN_EPS = 1e-5
RMS_EPS = 1e-6
SUBLN_EPS = 1e-5
NEG = -30000.0
N_CORES = 8
SEQS = (2048, 2048, 4096)

COMPUTE = ("pe", "act", "dve", "pool")


class Res:
    __slots__ = ("name", "w", "r")

    def __init__(self, name=""):
        self.name = name
        self.w = None
        self.r = {}


class Sched:
    def __init__(self, nc, ring=12):
        self.nc = nc
        self.engs = ("pe", "act", "dve", "pool", "sp")
        self.streams = {e: [] for e in self.engs}
        self.cnt = {}
        self.seen = {e: {} for e in self.engs}
        self.ring = ring
        self.dma_k = {e: 0 for e in self.engs}
        self.semkeys = list(COMPUTE)
        for e in ("sp", "pool"):
            for i in range(ring):
                self.semkeys.append(("d", e, i))
        for k in self.semkeys:
            self.cnt[k] = 0
        self.n_ins = 0
        self.n_wait = 0

    def _need(self, eng, tickets):
        seen = self.seen[eng]
        best = {}
        for (k, v) in tickets:
            if v <= seen.get(k, 0):
                continue
            if v > best.get(k, 0):
                best[k] = v
        out = []
        for k, v in best.items():
            seen[k] = v
            out.append((k, v))
        return out

    def op(self, eng, items, reads=(), writes=(), dma=False):
        tickets = []
        for r in reads:
            if r.w is not None:
                if dma or r.w[0] != eng or eng != "pe":
                    tickets.append(r.w)
        for w in writes:
            if w.w is not None and (dma or w.w[0] != eng):
                tickets.append(w.w)
            for k, v in w.r.items():
                if dma or k != eng:
                    tickets.append((k, v))
        if dma:
            i = self.dma_k[eng]
            self.dma_k[eng] = i + 1
            key = ("d", eng, i % self.ring)
            if self.cnt[key] > 0:
                tickets.append((key, self.cnt[key]))
            self.cnt[key] += 16
            ticket = (key, self.cnt[key])
            inc = (key, 16)
        else:
            self.cnt[eng] += 1
            ticket = (eng, self.cnt[eng])
            inc = (eng, 1)
        waits = self._need(eng, tickets)
        self.n_wait += len(waits)
        self.n_ins += len(items)
        self.streams[eng].append((waits, items, inc))
        k, v = ticket
        for r in reads:
            if v > r.r.get(k, 0):
                r.r[k] = v
        for w in writes:
            w.w = ticket
            w.r = {}
        return ticket

    def ins(self, eng, name, reads=(), writes=(), **kw):
        return self.op(eng, [(name, kw)], reads, writes)

    def dma(self, eng, out, in_, reads=(), writes=()):
        return self.op(eng, [("dma_start", dict(out=out, in_=in_))], reads, writes, dma=True)

    def barrier(self, final=False):
        tickets = [(k, v) for k, v in self.cnt.items() if v > 0 and (final or not (isinstance(k, tuple) and k[1] == "pool"))]
        for e in self.engs:
            waits = self._need(e, [t for t in tickets if t[0] != e])
            self.streams[e].append((waits, None, None))

    def emit(self, sems, block):
        streams = self.streams

        def body_for(ename):
            def body(e):
                for waits, items, inc in streams[ename]:
                    for (k, v) in waits:
                        e.wait_ge(sems[k], v)
                    if items is None:
                        continue
                    ins = None
                    for name, kw in items:
                        ins = getattr(e, name)(**kw)
                    ins.then_inc(sems[inc[0]], inc[1])
            return body

        block.tensor(body_for("pe"))
        block.scalar(body_for("act"))
        block.vector(body_for("dve"))
        block.gpsimd(body_for("pool"))
        block.sync(body_for("sp"))


class Ring:
    def __init__(self, tiles):
        self.tiles = tiles
        self.res = [Res() for _ in tiles]
        self.i = 0

    def next(self):
        n = len(self.tiles)
        t, r = self.tiles[self.i % n], self.res[self.i % n]
        self.i += 1
        return t, r


class Arena:
    def __init__(self, nc, base, size):
        self.nc, self.base, self.size = nc, base, size
        self.top = 0
        self.marks = []
        self.uid = 0
        self.peak = 0

    def alloc(self, name, shape, dt):
        esz = {F32: 4, BF16: 2, U8: 1}[dt]
        nbytes = esz
        for s in shape[1:]:
            nbytes *= s
        off = (self.top + 31) // 32 * 32
        self.uid += 1
        t = self.nc.alloc_sbuf_tensor_at(f"{name}_{self.uid}", list(shape), dt, offset=self.base + off)
        self.top = off + nbytes
        self.peak = max(self.peak, self.top)
        assert self.top <= self.size, (name, self.top, self.size)
        return t

    def ring(self, name, n, shape, dt):
        return Ring([self.alloc(f"{name}{i}", shape, dt) for i in range(n)])

    def push(self):
        self.marks.append(self.top)

    def pop(self):
        self.top = self.marks.pop()


class K:
    pass


def build(seqs=SEQS, dbg=False):
    T = sum(seqs)
    assert T % 512 == 0
    nc = bass.Bass("TRN2", target_bir_lowering=False)
    k = K()
    k.nc = nc
    k.T = T
    k.seqs = seqs

    def din(name, shape, dt=F32):
        return nc.dram_tensor(name, list(shape), dt, kind="ExternalInput").ap()

    def dscr(name, shape, dt):
        kind = "ExternalOutput" if dbg else "Internal"
        return nc.dram_tensor(name, list(shape), dt, kind=kind).ap()

    k.x = din("x", [T, D])
    k.w_in0 = din("w_in0", [D, IN0_W])
    k.w_out0 = din("w_out0", [D, D])
    k.w_in1 = din("w_in1", [D, IN1_W])
    k.w_out1 = din("w_out1", [D, D])
    k.wg = din("wg", [2, NCH, 128, 8 * 128])
    k.wu = din("wu", [2, NCH, 128, 8 * 128])
    k.wd = din("wd", [2, DFF, D])
    k.tpm = din("tpm", [8, 15, 64, 64])
    k.g_q = din("g_q", [64])
    k.g_k = din("g_k", [64])
    k.lq1 = din("lq1", [64])
    k.lk1 = din("lk1", [64])
    k.lq2 = din("lq2", [64])
    k.lk2 = din("lk2", [64])
    k.g_sub = din("g_sub", [128])
    k.ln_mix_g = din("ln_mix_g", [2, D])
    k.ln_mix_b = din("ln_mix_b", [2, D])
    k.ln_ffn_g = din("ln_ffn_g", [2, D])
    k.ln_ffn_b = din("ln_ffn_b", [2, D])
    k.ident = din("ident", [128, 128])
    k.rope_ax = din("rope_ax", [4096, 64])
    k.rope_seq = din("rope_seq", [4096, 64])
    k.y = nc.dram_tensor("y", [T, D], F32, kind="ExternalOutput").ap()

    k.w_in0_b = dscr("w_in0_b", [D, IN0_W], BF16)
    k.w_out0_b = dscr("w_out0_b", [D, D], BF16)
    k.w_in1_b = dscr("w_in1_b", [D, IN1_W], BF16)
    k.w_out1_b = dscr("w_out1_b", [D, D], BF16)
    k.wg_b = dscr("wg_b", [2, NCH, 128, 1024], BF16)
    k.wu_b = dscr("wu_b", [2, NCH, 128, 1024], BF16)
    k.wd_b = dscr("wd_b", [2, DFF, D], BF16)
    k.naT_d = dscr("naT_d", [1024, T], BF16)
    k.gT_d = dscr("gT_d", [640, T], BF16)
    k.vna_d = dscr("vna_d", [T, 520], BF16)
    k.vg_d = dscr("vg_d", [T, 130], BF16)
    k.ao_d = dscr("ao_d", [T, D], BF16)
    k.x2_d = dscr("x2_d", [T, D], F32)
    k.qT_d = dscr("qT_d", [1024, T], BF16)
    k.kT_d = dscr("kT_d", [1024, T], BF16)
    k.v1_d = dscr("v1_d", [T, 1032], BF16)
    k.ao1_d = dscr("ao1_d", [T, D], BF16)
    k.mb_d = dscr("mb_d", [128, 8 * 14 * 64], BF16)
    k.x1_d = dscr("x1_d", [T, D], F32)
    k.dbg = dbg
    if dbg:
        k.dbg_x1 = dscr("dbg_x1", [2, T, D], F32)
        k.dbg_z = dscr("dbg_z", [2, T, D], F32)
        k.dbg_h = dscr("dbg_h", [2, DFF, T], BF16)
        k.dbg_s = dscr("dbg_s", [4, T, 1], F32)
        k.dbg_zn = dscr("dbg_zn", [T, D], F32)
    k.dbg_tok = None

    S = Sched(nc)
    k.S = S
    with ExitStack() as es:
        ARENA = 207000
        arena_t = es.enter_context(nc.sbuf_tensor("arena", [128, ARENA], U8))
        base = nc.sbuf_base - ARENA
        A = Arena(nc, base, ARENA)
        k.A = A
        k.ps = es.enter_context(nc.psum_tensor("ps", [128, 8, 512], F32))
        sems = {key: es.enter_context(nc.semaphore(f"s{i}")) for i, key in enumerate(S.semkeys)}
        block = es.enter_context(nc.Block())

        setup(k)
        phase_A0(k)
        A.pop()
        cast_late(k)
        tok0 = 0
        for L in seqs:
            phase_B0_na(k, tok0, L)
            phase_B0_gqa(k, tok0, L)
            tok0 += L
        phase_C(k, 0, k.x, k.ao_d, k.x2_d)
        phase_A1(k)
        phase_B1(k)
        phase_C(k, 1, k.x2_d, k.ao1_d, k.y)
        S.barrier(final=True)
        S.emit(sems, block)
    k.stats = dict(n_ins=S.n_ins, n_wait=S.n_wait, peak=A.peak)
    return nc, k


def cast_weights(k, casts):
    S = k.S
    for name, src, dst in casts:
        r = Res(name)
        k.r_w[name] = r
        s2 = src if len(src.shape) == 2 else src.rearrange("c p n -> (c p) n")
        d2 = dst if len(dst.shape) == 2 else dst.rearrange("c p n -> (c p) n")
        if s2.shape[1] > 2048:
            half = s2.shape[1] // 2
            s2 = s2.rearrange("r (a n) -> (r a) n", n=half)
            d2 = d2.rearrange("r (a n) -> (r a) n", n=half)
        rows = s2.shape[0]
        step = 1024
        for r0 in range(0, rows, step):
            r1 = min(rows, r0 + step)
            chain = k.cast_chain[k.cast_i % 4]
            k.cast_i += 1
            S.dma("pool", out=d2[r0:r1], in_=s2[r0:r1], writes=[r, chain])


def cast_late(k):
    casts = [("w_out0", k.w_out0, k.w_out0_b)]
    for l in range(2):
        if l == 1:
            casts += [("w_in1", k.w_in1, k.w_in1_b), ("w_out1", k.w_out1, k.w_out1_b)]
        casts += [(f"wg{l}", k.wg[l], k.wg_b[l]), (f"wu{l}", k.wu[l], k.wu_b[l]), (f"wd{l}", k.wd[l], k.wd_b[l])]
    cast_weights(k, casts)


def bcast_rows(ap, n=128):
    return ap.partition_broadcast(n)


def setup(k):
    S, A = k.S, k.A
    k.r_w = {}
    k.cast_chain = [Res() for _ in range(4)]
    k.cast_i = 0
    cast_weights(k, [("w_in0", k.w_in0, k.w_in0_b)])

    k.ident_f = A.alloc("ident_f", [128, 128], F32)
    k.ident_b = A.alloc("ident_b", [128, 128], BF16)
    k.r_id = Res("ident")
    S.dma("sp", out=k.ident_f[:], in_=k.ident, writes=[k.r_id])
    S.ins("dve", "tensor_copy", reads=[k.r_id], writes=[k.r_id], out=k.ident_b[:], in_=k.ident_f[:])
    k.G0 = A.alloc("G0", [128, 10, 64], F32)
    k.G1 = A.alloc("G1", [128, 128], F32)
    k.lam = A.alloc("lam", [128, 2], F32)
    k.cst = A.alloc("cst", [128, 8], F32)
    k.r_c = Res("consts")
    S.ins("dve", "memset", writes=[k.r_c], ap=k.cst[:, 0:1], constant=-0.5)
    S.ins("dve", "memset", writes=[k.r_c], ap=k.cst[:, 1:2], constant=LN_EPS)
    S.ins("dve", "memset", writes=[k.r_c], ap=k.cst[:, 2:3], constant=RMS_EPS)
    A.push()
    MBt = A.alloc("MB", [128, 8, 14, 64], BF16)
    v = A.alloc("vecs", [128, 6, 64], F32)
    g1 = A.alloc("g1t", [128, 128], F32)
    stage = A.alloc("mbst", [128, 8, 14, 64], F32)
    r_v = Res()
    for i, src in enumerate((k.g_q, k.g_k, k.lq1, k.lk1, k.lq2, k.lk2)):
        S.dma("sp", out=v[:, i, :], in_=bcast_rows(src), writes=[r_v])
    S.dma("sp", out=g1[:], in_=bcast_rows(k.g_sub), writes=[r_v])
    S.ins("dve", "tensor_scalar", reads=[r_v], writes=[k.r_c], out=k.G0[:, 0:8, :],
          in0=v[:, 0:1, :].to_broadcast([128, 8, 64]), scalar1=0.125, scalar2=None, op0=ALU.mult)
    S.ins("dve", "tensor_copy", reads=[r_v], writes=[k.r_c], out=k.G0[:, 8:10, :],
          in_=v[:, 1:2, :].to_broadcast([128, 2, 64]))
    S.ins("dve", "tensor_scalar", reads=[r_v], writes=[k.r_c], out=k.G1[:], in0=g1[:],
          scalar1=1.0 - LAMBDA_INIT, scalar2=None, op0=ALU.mult)
    pr = A.alloc("pr", [128, 2, 64], F32)
    sm = A.alloc("sm", [128, 2], F32)
    r_p = Res()
    S.ins("dve", "tensor_tensor", reads=[r_v], writes=[r_p], out=pr[:, 0, :], in0=v[:, 2, :], in1=v[:, 3, :], op=ALU.mult)
    S.ins("dve", "tensor_tensor", reads=[r_v], writes=[r_p], out=pr[:, 1, :], in0=v[:, 4, :], in1=v[:, 5, :], op=ALU.mult)
    S.ins("dve", "tensor_reduce", reads=[r_p], writes=[r_p], out=sm[:], in_=pr[:], axis=AX.X, op=ALU.add)
    S.ins("act", "activation", reads=[r_p], writes=[r_p], out=sm[:], in_=sm[:], func=AF.Exp)
    S.ins("dve", "tensor_tensor", reads=[r_p], writes=[k.r_c], out=k.lam[:, 0:1], in0=sm[:, 0:1], in1=sm[:, 1:2], op=ALU.subtract)
    S.ins("dve", "tensor_scalar", reads=[k.r_c], writes=[k.r_c], out=k.lam[:, 0:1], in0=k.lam[:, 0:1],
          scalar1=LAMBDA_INIT, scalar2=None, op0=ALU.add)
    S.ins("dve", "tensor_scalar", reads=[k.r_c], writes=[k.r_c], out=k.lam[:, 1:2], in0=k.lam[:, 0:1],
          scalar1=-1.0, scalar2=None, op0=ALU.mult)
    r_st = Res()
    for b in range(2):
        for h in range(8):
            S.dma("sp", out=stage[b * 64:(b + 1) * 64, h, :, :],
                  in_=k.tpm[h, b:b + 14].rearrange("r k q -> k r q"), writes=[r_st])
    r_mb = Res()
    S.ins("act", "activation", reads=[r_st], writes=[r_mb], out=MBt[:].rearrange("p h m q -> p (h m q)"),
          in_=stage[:].rearrange("p h m q -> p (h m q)"), func=AF.Exp)
    S.dma("sp", out=k.mb_d, in_=MBt[:].rearrange("p h m q -> p (h m q)"), reads=[r_mb])


def rsqrt_pool(k, out, in_, scale, eps, reads, writes):
    S = k.S
    S.ins("pool", "tensor_scalar", reads=reads, writes=writes, out=out, in0=in_, scalar1=scale, scalar2=eps,
          op0=ALU.mult, op1=ALU.add)
    S.ins("pool", "tensor_tensor", reads=list(writes) + [k.r_c], writes=writes, out=out, in0=out,
          in1=k.cst[:, 0:1].to_broadcast(list(out.shape)), op=ALU.pow)


def rsqrt_act(k, out, in_, scale, eps, reads, writes):
    S = k.S
    col = {LN_EPS: 1, RMS_EPS: 2}[eps]
    S.ins("act", "activation", reads=list(reads) + [k.r_c], writes=writes, out=out, in_=in_, func=AF.Sqrt,
          bias=k.cst[:, col:col + 1], scale=scale)
    S.ins("dve", "reciprocal", reads=writes, writes=writes, out=out, in_=out)


def seq_pos(k, tok):
    t0 = 0
    for L in k.seqs:
        if tok < t0 + L:
            return tok - t0
        t0 += L
    raise AssertionError


def layer_norm_tile(k, z, r_z, gt, bt, r_ln, out, r_out, st, r_st):
    S = k.S
    stats, mv, rstd, nmr = st
    S.ins("dve", "bn_stats", reads=[r_z], writes=[r_st], out=stats[:, 0, :], in_=z[:, 0:512])
    S.ins("dve", "bn_stats", reads=[r_z], writes=[r_st], out=stats[:, 1, :], in_=z[:, 512:1024])
    S.ins("dve", "bn_aggr", reads=[r_st], writes=[r_st], out=mv[:], in_=stats[:].rearrange("p a b -> p (a b)"))
    rsqrt_pool(k, rstd[:], mv[:, 1:2], 1.0, LN_EPS, [r_st], [r_st])
    S.ins("dve", "tensor_scalar", reads=[r_z, r_st], writes=[r_z], out=z[:], in0=z[:], scalar1=mv[:, 0:1],
          scalar2=rstd[:, 0:1], op0=ALU.subtract, op1=ALU.mult)
    if k.dbg and k.dbg_tok is not None:
        tok = k.dbg_tok
        S.dma("sp", out=k.dbg_s[0, tok:tok + 128, :], in_=mv[:, 0:1], reads=[r_st])
        S.dma("sp", out=k.dbg_s[1, tok:tok + 128, :], in_=mv[:, 1:2], reads=[r_st])
        S.dma("sp", out=k.dbg_s[2, tok:tok + 128, :], in_=rstd[:], reads=[r_st])
        S.dma("sp", out=k.dbg_s[3, tok:tok + 128, :], in_=nmr[:], reads=[r_st])
        S.dma("sp", out=k.dbg_zn[tok:tok + 128, :], in_=z[:], reads=[r_z])
    S.ins("pool", "tensor_tensor", reads=[r_z, r_ln], writes=[r_z], out=z[:], in0=z[:], in1=gt[:], op=ALU.mult)
    S.ins("pool", "tensor_tensor", reads=[r_z, r_ln], writes=[r_out], out=out, in0=z[:], in1=bt[:], op=ALU.add)


def phase_A0(k):
    S, A, ps, T = k.S, k.A, k.ps, k.T
    A.push()
    w = A.alloc("w_in0", [128, 8, IN0_W], BF16)
    r_w = Res()
    S.dma("sp", out=w[:], in_=k.w_in0_b.rearrange("(k p) n -> p k n", p=128), reads=[k.r_w["w_in0"]], writes=[r_w])
    xs = A.ring("xs", 4, [128, D], F32)
    cs = A.ring("cs", 6, [128, 64], F32)
    xT = A.ring("xT", 2, [128, 8, 512], BF16)
    sq = A.ring("sq", 2, [128, 640], F32)
    xsb = A.ring("xsb", 3, [128, 640], F32)
    sst = A.ring("sst", 3, [128, 10], F32)
    tmpA = A.ring("tmpA", 2, [128, 2, 320], F32)
    tmpB = A.ring("tmpB", 2, [128, 2, 320], F32)
    qkr = A.ring("qkr", 4, [128, 640], BF16)
    vna = A.ring("vna", 2, [128, 8, 65], BF16)
    vg = A.ring("vg", 2, [128, 2, 65], BF16)
    gst = A.ring("gst", 2, [128, 5, 512], BF16)
    nst = A.ring("nst", 2, [128, 8, 512], BF16)
    for t_, r_ in zip(vna.tiles + vg.tiles, vna.res + vg.res):
        S.ins("pool", "memset", writes=[r_], ap=t_[:, :, 64:65], constant=1.0)
    pT = ps[:, 0, :].bitcast(BF16).rearrange("p (a c) -> p a c", c=128)
    r_pT = Res()
    pSec = Ring([ps[:, 1:3, :], ps[:, 3:5, :]])
    pFMr = Ring([ps[:, 5, :], ps[:, 6, :]])
    xb16 = A.ring("xb16", 2, [128, D], BF16)
    pTq = ps[:, 7, :].bitcast(BF16).rearrange("p (a c) -> p a c", c=128)
    r_pTq = Res()
    pending = []
    q3 = []

    def flush():
        while pending:
            pending.pop(0)()

    loaded = {}
    NTILE = T // 128

    def prefetch(upto):
        for ti in range(len(loaded), min(upto + 1, NTILE)):
            tok_ = ti * 128
            pos_ = seq_pos(k, tok_)
            xt_, r_xt_ = xs.next()
            ct_, r_ct_ = cs.next()
            S.dma("sp", out=xt_[:], in_=k.x[tok_:tok_ + 128, :], writes=[r_xt_])
            S.dma("sp", out=ct_[:], in_=k.rope_ax[pos_:pos_ + 128, :], writes=[r_ct_])
            loaded[ti] = (xt_, r_xt_, ct_, r_ct_)

    for g in range(T // 512):
        xTt, r_xT = xT.next()
        gs, r_gs = gst.next()
        ns, r_ns = nst.next()
        for t in range(4):
            tok = g * 512 + t * 128
            pos = seq_pos(k, tok)
            prefetch(g * 4 + t + 2)
            xt, r_xt, ct, r_ct = loaded[g * 4 + t]
            xh, r_xh = xb16.next()
            S.ins("act", "activation", reads=[r_xt], writes=[r_xh], out=xh[:], in_=xt[:], func=AF.Copy)
            S.op("pe", [("transpose", dict(out=pT[:, kk, :], in_=xh[:, kk * 128:(kk + 1) * 128], identity=k.ident_b[:]))
                        for kk in range(8)], reads=[r_xh, k.r_id], writes=[r_pT])
            S.ins("act", "activation", reads=[r_pT], writes=[r_xT], out=xTt[:, :, t * 128:(t + 1) * 128], in_=pT,
                  func=AF.Copy)
            p0, r_p0 = pSec.next()
            p0f = p0.rearrange("p a c -> p (a c)")
            items = []
            for (c0, c1) in ((0, 512), (512, 640)):
                for kk in range(8):
                    items.append(("matmul", dict(out=p0f[:, c0:c1], lhsT=xTt[:, kk, t * 128:(t + 1) * 128],
                                                 rhs=w[:, kk, 1024 + c0:1024 + c1], start=(kk == 0), stop=(kk == 7))))
            S.op("pe", items, reads=[r_xT, r_w], writes=[r_p0])
            sqt, r_sq = sq.next()
            xb, r_xsb = xsb.next()
            ss, r_ss = sst.next()
            S.ins("act", "activation", reads=[r_p0], writes=[r_xsb], out=xb[:], in_=p0f[:, 0:640], func=AF.Copy)
            S.ins("dve", "tensor_tensor", reads=[r_xsb], writes=[r_sq], out=sqt[:], in0=xb[:], in1=xb[:], op=ALU.mult)
            S.ins("dve", "tensor_reduce", reads=[r_sq], writes=[r_ss], out=ss[:],
                  in_=sqt[:].rearrange("p (h d) -> p h d", d=64), axis=AX.X, op=ALU.add)
            S.ins("act", "activation", reads=[r_ss, k.r_c], writes=[r_ss], out=ss[:], in_=ss[:], func=AF.Sqrt,
                  bias=k.cst[:, 2:3], scale=1.0 / 64)

            def stage2(xb=xb, r_xsb=r_xsb, ss=ss, r_ss=r_ss, ct=ct, r_ct=r_ct, gs=gs, r_gs=r_gs, t=t, g=g):
                S.ins("dve", "reciprocal", reads=[r_ss], writes=[r_ss], out=ss[:], in_=ss[:])
                xv = xb[:].rearrange("p (h d) -> p h d", d=64)
                S.ins("dve", "tensor_tensor", reads=[r_xsb, r_ss], writes=[r_xsb], out=xv, in0=xv,
                      in1=ss[:].unsqueeze(2).to_broadcast([128, 10, 64]), op=ALU.mult)
                S.ins("dve", "tensor_tensor", reads=[r_xsb, k.r_c], writes=[r_xsb], out=xv, in0=xv, in1=k.G0[:], op=ALU.mult)
                x5 = xb[:].rearrange("p (h a b f) -> p h a b f", a=2, b=2, f=16)
                x1, x2 = x5[:, :, :, 0, :], x5[:, :, :, 1, :]
                c4 = ct[:].rearrange("p (s a f) -> p s a f", s=2, a=2)
                cosb = c4[:, 0:1, :, :].to_broadcast([128, 10, 2, 16])
                sinb = c4[:, 1:2, :, :].to_broadcast([128, 10, 2, 16])
                qk_t, r_qk = qkr.next()
                r_qa, r_qb = Res(), Res()
                o5 = qk_t[:].rearrange("p (h a b f) -> p h a b f", a=2, b=2, f=16)
                ta, r_ta = tmpA.next()
                tb, r_tb = tmpB.next()
                tva = [ta[:, i, :].rearrange("p (h a f) -> p h a f", a=2, f=16) for i in range(2)]
                tvb = [tb[:, i, :].rearrange("p (h a f) -> p h a f", a=2, f=16) for i in range(2)]
                S.ins("dve", "tensor_tensor", reads=[r_xsb, r_ct], writes=[r_ta], out=tva[0], in0=x1, in1=cosb, op=ALU.mult)
                S.ins("dve", "tensor_tensor", reads=[r_xsb, r_ct], writes=[r_ta], out=tva[1], in0=x2, in1=sinb, op=ALU.mult)
                S.ins("dve", "tensor_tensor", reads=[r_ta, r_qk], writes=[r_qa], out=o5[:, :, :, 0, :], in0=tva[0],
                      in1=tva[1], op=ALU.subtract)
                S.ins("pool", "tensor_tensor", reads=[r_xsb, r_ct], writes=[r_tb], out=tvb[0], in0=x1, in1=sinb, op=ALU.mult)
                S.ins("pool", "tensor_tensor", reads=[r_xsb, r_ct], writes=[r_tb], out=tvb[1], in0=x2, in1=cosb, op=ALU.mult)
                S.ins("pool", "tensor_tensor", reads=[r_tb, r_qk], writes=[r_qb], out=o5[:, :, :, 1, :], in0=tvb[0],
                      in1=tvb[1], op=ALU.add)

                def back():
                    S.op("pe", [("transpose", dict(out=pTq[:, j, :], in_=qk_t[:, j * 128:(j + 1) * 128], identity=k.ident_b[:]))
                                for j in range(5)], reads=[r_qa, r_qb, k.r_id], writes=[r_pTq])
                    S.ins("act", "activation", reads=[r_pTq, r_qa, r_qb], writes=[r_gs, r_qk], out=gs[:, :, t * 128:(t + 1) * 128],
                          in_=pTq[:, 0:5, :], func=AF.Copy)
                    if t == 3:
                        S.dma("sp", out=k.gT_d.rearrange("(c p) t -> p c t", p=128)[:, :, g * 512:(g + 1) * 512], in_=gs[:],
                              reads=[r_gs])
                q3.append(back)
            p1, r_p1 = pSec.next()
            p1f = p1.rearrange("p a c -> p (a c)")
            items = []
            for (c0, c1) in ((0, 512), (512, 640)):
                for kk in range(8):
                    items.append(("matmul", dict(out=p1f[:, c0:c1], lhsT=xTt[:, kk, t * 128:(t + 1) * 128],
                                                 rhs=w[:, kk, 1664 + c0:1664 + c1], start=(kk == 0), stop=(kk == 7))))
            S.op("pe", items, reads=[r_xT, r_w], writes=[r_p1])
            while q3:
                q3.pop(0)()
            while pending:
                pending.pop(0)()
            pending.append(stage2)
            vn, r_vn = vna.next()
            vgt, r_vg = vg.next()
            S.ins("dve", "tensor_copy", reads=[r_p1], writes=[r_vn], out=vn[:, :, 0:64],
                  in_=p1f[:, 0:512].rearrange("p (h d) -> p h d", d=64))
            S.ins("dve", "tensor_copy", reads=[r_p1], writes=[r_vg], out=vgt[:, :, 0:64],
                  in_=p1f[:, 512:640].rearrange("p (h d) -> p h d", d=64))
            S.dma("sp", out=k.vna_d[tok:tok + 128, :], in_=vn[:].rearrange("p h e -> p (h e)"), reads=[r_vn])
            S.dma("sp", out=k.vg_d[tok:tok + 128, :], in_=vgt[:].rearrange("p h e -> p (h e)"), reads=[r_vg])
        for oc in range(8):
            pFM, r_pFM = pFMr.next()
            S.op("pe", [("matmul", dict(out=pFM, lhsT=w[:, kk, oc * 128:(oc + 1) * 128], rhs=xTt[:, kk, :],
                                        start=(kk == 0), stop=(kk == 7))) for kk in range(8)],
                 reads=[r_xT, r_w], writes=[r_pFM])
            S.ins("act", "activation", reads=[r_pFM], writes=[r_ns], out=ns[:, oc, :], in_=pFM, func=AF.Copy)
        S.dma("sp", out=k.naT_d.rearrange("(c p) t -> p c t", p=128)[:, :, g * 512:(g + 1) * 512], in_=ns[:],
              reads=[r_ns])
    while pending or q3:
        q3_now = list(q3)
        del q3[:]
        for f in q3_now:
            f()
        if pending:
            pending.pop(0)()
    S.barrier()
    A.pop()


def phase_B0_na(k, tok0, L):
    S, A, ps = k.S, k.A, k.ps
    R = L // GRID_W
    NT = L // 128
    A.push()
    KT = A.alloc("KTna", [128, 4, L], BF16)
    QT = A.alloc("QTna", [128, 4, L], BF16)
    Ve = A.alloc("Ve", [128, NT, 520], BF16)
    Vo = A.alloc("Vo", [128, NT - 1, 520], BF16)
    MB = A.alloc("MBna", [128, 8, 14, 64], BF16)
    r_in = Res()
    S.dma("sp", out=MB[:].rearrange("p h m q -> p (h m q)"), in_=k.mb_d, writes=[r_in])
    nav = k.naT_d.rearrange("(c p) t -> p c t", p=128)
    S.dma("sp", out=QT[:], in_=nav[:, 0:4, tok0:tok0 + L], writes=[r_in])
    S.dma("sp", out=KT[:], in_=nav[:, 4:8, tok0:tok0 + L], writes=[r_in])
    S.dma("sp", out=Ve[:], in_=k.vna_d[tok0:tok0 + L, :].rearrange("(n p) f -> p n f", p=128), writes=[r_in])
    S.dma("sp", out=Vo[:], in_=k.vna_d[tok0 + 64:tok0 + 64 + (NT - 1) * 128, :].rearrange("(n p) f -> p n f", p=128),
          writes=[r_in])
    E = A.ring("Ena", 4, [128, 2, 256], BF16)
    Ost = A.ring("Ona", 2, [128, 512], BF16)
    rc = A.ring("rcna", 2, [128, 8], F32)
    pS = Ring([ps[:, 0:2, 0:256], ps[:, 2:4, 0:256]])
    pO = Ring([ps[:, 4:6, :], ps[:, 6:8, :]])
    steps = []
    for i in range(NT):
        tst = {"started": {}}
        for rr in (0, 1):
            r = 2 * i + rr
            r0 = min(max(r - 4, 0), R - 8)
            for hp in range(4):
                st = {}

                def qk(st=st, r=r, r0=r0, hp=hp):
                    st["ps"] = pS.next()
                    p_s, r_ps = st["ps"]
                    items = []
                    for a in (0, 1):
                        for c in range(4):
                            k0 = (r0 + 2 * c) * 64
                            items.append(("matmul", dict(out=p_s[:, a, c * 64:(c + 1) * 64],
                                                         lhsT=KT[a * 64:(a + 1) * 64, hp, k0:k0 + 128],
                                                         rhs=QT[a * 64:(a + 1) * 64, hp, r * 64:(r + 1) * 64],
                                                         start=True, stop=True, skip_group_check=True)))
                    S.op("pe", items, reads=[r_in], writes=[r_ps])

                def mid(st=st, r=r, r0=r0, hp=hp):
                    p_s, r_ps = st["ps"]
                    st["e"] = E.next()
                    e_t, r_e = st["e"]
                    S.ins("act", "activation", reads=[r_ps], writes=[r_e], out=e_t[:], in_=p_s, func=AF.Exp, scale=0.125)
                    m0 = r0 - r + 7
                    ev = e_t[:].rearrange("p a (c q) -> p a c q", q=64)
                    S.ins("dve", "tensor_tensor", reads=[r_e, r_in], writes=[r_e], out=ev, in0=ev,
                          in1=MB[:, 2 * hp:2 * hp + 2, m0:m0 + 7:2, :], op=ALU.mult)

                def pv(st=st, tst=tst, rr=rr, r0=r0, hp=hp):
                    if rr == 0 and hp == 0:
                        tst["po"] = pO.next()
                    po, r_po = tst["po"]
                    e_t, r_e = st["e"]
                    Vt = Ve if r0 % 2 == 0 else Vo
                    kc0 = r0 // 2
                    items = []
                    for a in (0, 1):
                        h = 2 * hp + a
                        bank = h // 4
                        for c in range(4):
                            stt_ = not tst["started"].get((bank, rr), False)
                            tst["started"][(bank, rr)] = True
                            items.append(("matmul", dict(out=po[rr * 64:(rr + 1) * 64, bank, (h % 4) * 65:(h % 4 + 1) * 65],
                                                         lhsT=e_t[:, a, c * 64:(c + 1) * 64],
                                                         rhs=Vt[:, kc0 + c, h * 65:(h + 1) * 65],
                                                         start=stt_, stop=(c == 3), skip_group_check=True)))
                    S.op("pe", items, reads=[r_e, r_in], writes=[r_po])

                fin = None
                if rr == 1 and hp == 3:
                    def fin(tst=tst, i=i):
                        po, r_po = tst["po"]
                        pov = po[:, :, 0:260].rearrange("p b (h e) -> p b h e", e=65)
                        rct, r_rc = rc.next()
                        ot, r_ot = Ost.next()
                        rc4 = rct[:].rearrange("p (b h e) -> p b h e", b=2, e=1)
                        S.ins("dve", "reciprocal", reads=[r_po], writes=[r_rc], out=rc4, in_=pov[:, :, :, 64:65])
                        S.ins("dve", "tensor_tensor", reads=[r_po, r_rc], writes=[r_ot],
                              out=ot[:].rearrange("p (b h d) -> p b h d", b=2, d=64), in0=pov[:, :, :, 0:64],
                              in1=rc4.to_broadcast([128, 2, 4, 64]), op=ALU.mult)
                        t0 = tok0 + i * 128
                        S.dma("sp", out=k.ao_d[t0:t0 + 128, 0:512], in_=ot[:], reads=[r_ot])
                steps.append((qk, mid, pv, fin))
    run_pipeline(steps, skew=2)
    S.barrier()
    A.pop()


def qgroups(NT, gmax):
    out = []
    q = 0
    rem = NT
    while rem > 0:
        if rem > gmax + 1 or rem == gmax:
            n = gmax
        elif rem == gmax + 1 and gmax > 2:
            n = gmax - 1
        else:
            n = min(rem, gmax)
        out.append((q, n))
        q += n
        rem -= n
    return out


def run_pipeline(steps, skew=2):
    n = len(steps)
    for j in range(min(skew, n)):
        steps[j][0]()
    for i in range(n):
        steps[i][1]()
        if i + skew < n:
            steps[i + skew][0]()
        steps[i][2]()
        if steps[i][3] is not None:
            steps[i][3]()


def phase_B0_gqa(k, tok0, L):
    S, A, ps = k.S, k.A, k.ps
    NT = L // 128
    A.push()
    KT = A.alloc("KTg", [128, L], BF16)
    QT = A.alloc("QTg", [128, 4, L], BF16)
    V = A.alloc("Vg", [128, NT, 130], BF16)
    r_in = Res()
    gv = k.gT_d.rearrange("(c p) t -> p c t", p=128)
    S.dma("sp", out=QT[:], in_=gv[:, 0:4, tok0:tok0 + L], writes=[r_in])
    S.dma("sp", out=KT[:], in_=k.gT_d[512:640, tok0:tok0 + L], writes=[r_in])
    S.dma("sp", out=V[:], in_=k.vg_d[tok0:tok0 + L, :].rearrange("(n p) f -> p n f", p=128), writes=[r_in])
    E = A.ring("Eg", 3, [128, 2, 512], BF16)
    Ost = A.ring("Og", 2, [128, 4, 512], BF16)
    rc = A.ring("rcg", 2, [128, 2, 4], F32)
    pS = Ring([ps[:, 0:2, :], ps[:, 2:4, :]])
    pO = Ring([ps[:, 4:6, :], ps[:, 6:8, :]])
    r_out = Res()
    steps = []
    for g in range(L // 512):
        gst = {}
        for j in range(4):
            hst = {}
            for kc in range(NT):
                st = {}

                def qk(st=st, g=g, j=j, kc=kc):
                    st["ps"] = pS.next()
                    p_s, r_ps = st["ps"]
                    items = []
                    for a in (0, 1):
                        items.append(("matmul", dict(out=p_s[:, a, :], lhsT=KT[a * 64:(a + 1) * 64, kc * 128:(kc + 1) * 128],
                                                     rhs=QT[a * 64:(a + 1) * 64, j, g * 512:(g + 1) * 512],
                                                     start=True, stop=True, skip_group_check=True)))
                    S.op("pe", items, reads=[r_in], writes=[r_ps])

                def mid(st=st):
                    p_s, r_ps = st["ps"]
                    st["e"] = E.next()
                    e_t, r_e = st["e"]
                    S.ins("act", "activation", reads=[r_ps], writes=[r_e], out=e_t[:], in_=p_s, func=AF.Exp)

                def pv(st=st, hst=hst, kc=kc):
                    if kc == 0:
                        hst["po"] = pO.next()
                    po, r_po = hst["po"]
                    e_t, r_e = st["e"]
                    items = []
                    for a in (0, 1):
                        for qt in range(4):
                            items.append(("matmul", dict(out=po[:, a, qt * 65:(qt + 1) * 65],
                                                         lhsT=e_t[:, a, qt * 128:(qt + 1) * 128],
                                                         rhs=V[:, kc, a * 65:(a + 1) * 65],
                                                         start=(kc == 0 and qt == 0), stop=(kc == NT - 1),
                                                         skip_group_check=True)))
                    S.op("pe", items, reads=[r_e, r_in], writes=[r_po])

                fin = None
                if kc == NT - 1:
                    def fin(hst=hst, gst=gst, g=g, j=j):
                        if j == 0:
                            gst["ot"] = Ost.next()
                        ot, r_ot = gst["ot"]
                        po, r_po = hst["po"]
                        pov = po[:, :, 0:260].rearrange("p a (q e) -> p a q e", e=65)
                        rct, r_rc = rc.next()
                        S.ins("dve", "reciprocal", reads=[r_po], writes=[r_rc], out=rct[:].unsqueeze(3), in_=pov[:, :, :, 64:65])
                        for a in (0, 1):
                            h = j + 4 * a
                            S.ins("dve", "tensor_tensor", reads=[r_po, r_rc], writes=[r_ot], out=ot[:, :, h * 64:(h + 1) * 64],
                                  in0=pov[:, a, :, 0:64], in1=rct[:, a, :].unsqueeze(2).to_broadcast([128, 4, 64]), op=ALU.mult)
                        if j == 3:
                            t0 = tok0 + g * 512
                            S.dma("sp", out=k.ao_d[t0:t0 + 512, 512:1024].rearrange("(q p) f -> p q f", p=128), in_=ot[:],
                                  reads=[r_ot])
                steps.append((qk, mid, pv, fin))
    run_pipeline(steps, skew=2)
    S.barrier()
    A.pop()


def phase_C(k, layer, src, ao, dst):
    S, A, ps, T = k.S, k.A, k.ps, k.T
    NG = T // 512
    A.push()
    Wo = A.alloc("Wo", [128, 8, D], BF16)
    Wd = A.alloc("Wd", [128, NCH, D], BF16)
    lnp = [A.alloc(f"ln{i}", [128, D], F32) for i in range(4)]
    r_wo, r_wd, r_ln = Res(), Res(), Res()
    wo_b = k.w_out0_b if layer == 0 else k.w_out1_b
    S.dma("sp", out=Wo[:], in_=wo_b.rearrange("(k p) n -> p k n", p=128), reads=[k.r_w[f"w_out{layer}"]], writes=[r_wo])
    for i, src_v in enumerate((k.ln_mix_g, k.ln_mix_b, k.ln_ffn_g, k.ln_ffn_b)):
        S.dma("sp", out=lnp[i][:], in_=bcast_rows(src_v[layer]), writes=[r_ln])
    o_in = A.ring("o_in", 2, [128, D], BF16)
    oT = A.ring("oT", 2, [128, 8, 128], BF16)
    x_in = A.ring("x_in", 2, [128, D], F32)
    z = A.ring("z", 2, [128, D], F32)
    x1t = A.ring("x1t", 2, [128, D], F32)
    x1r = A.ring("x1r", 2, [128, D], F32)
    x1b = A.ring("x1b", 2, [128, D], BF16)
    x1T = A.ring("x1T", 2, [128, 8, 512], BF16)
    hT = A.alloc("hT", [128, NCH, 512], BF16)
    r_hT = Res()
    wgu = A.ring("wgu", 6, [128, 2, 1024], BF16)
    sg = A.ring("sg", 2, [128, 512], BF16)
    yo = A.ring("yo", 2, [128, D], F32)
    stt = [A.alloc("stats", [128, 2, 6], F32), A.alloc("mv", [128, 2], F32), A.alloc("rstd", [128, 1], F32),
           A.alloc("nmr", [128, 1], F32)]
    stt2 = [A.alloc("stats2", [128, 2, 6], F32), A.alloc("mv2", [128, 2], F32), A.alloc("rstd2", [128, 1], F32),
            A.alloc("nmr2", [128, 1], F32)]
    r_st, r_st2 = Res(), Res()
    pTb = Ring([ps[:, 0, :].bitcast(BF16).rearrange("p (a c) -> p a c", c=128),
                ps[:, 1, :].bitcast(BF16).rearrange("p (a c) -> p a c", c=128)])
    pOut, r_pOut = ps[:, 2:4, :], Res()
    pGU = Ring([ps[:, 4:6, :], ps[:, 6:8, :]])
    r_wsrc = [k.r_w[f"wg{layer}"], k.r_w[f"wu{layer}"]]
    r_x1d = [Res() for _ in range(T // 128)]
    S.dma("sp", out=Wd[:], in_=k.wd_b[layer].rearrange("(c p) n -> p c n", p=128), reads=[k.r_w[f"wd{layer}"]],
          writes=[r_wd])

    wq = {}

    def load_w(idx):
        g_, c_ = divmod(idx, NCH)
        if g_ >= NG or idx in wq:
            return
        wt, r_wt = wgu.next()
        S.dma("sp", out=wt[:, 0, :], in_=k.wg_b[layer, c_], reads=[r_wsrc[0]], writes=[r_wt])
        S.dma("sp", out=wt[:, 1, :], in_=k.wu_b[layer, c_], reads=[r_wsrc[1]], writes=[r_wt])
        wq[idx] = (wt, r_wt)

    ld = {}

    def load_c1(ti):
        if ti >= T // 128 or ti in ld:
            return
        tok = ti * 128
        oi, r_oi = o_in.next()
        xi, r_xi = x_in.next()
        S.dma("sp", out=oi[:], in_=ao[tok:tok + 128, :], writes=[r_oi])
        S.dma("sp", out=xi[:], in_=src[tok:tok + 128, :], writes=[r_xi])
        ld[ti] = (oi, r_oi, xi, r_xi)

    xTs = {}
    c1st = {}
    deferred = []

    def flush_deferred():
        while deferred:
            deferred.pop(0)()

    def c1a(ti):
        g_, t = divmod(ti, 4)
        tok = ti * 128
        load_c1(ti)
        oi, r_oi, xi, r_xi = ld[ti]
        if t == 0:
            xTs[g_] = x1T.next()
        pt, r_pt = pTb.next()
        S.op("pe", [("transpose", dict(out=pt[:, kk, :], in_=oi[:, kk * 128:(kk + 1) * 128], identity=k.ident_b[:]))
                    for kk in range(8)], reads=[r_oi, k.r_id], writes=[r_pt])
        ott, r_oT = oT.next()
        S.ins("act", "activation", reads=[r_pt], writes=[r_oT], out=ott[:], in_=pt, func=AF.Copy)
        items = []
        for half in (0, 1):
            for kk in range(8):
                items.append(("matmul", dict(out=pOut[:, half, :], lhsT=ott[:, kk, :],
                                             rhs=Wo[:, kk, half * 512:(half + 1) * 512], start=(kk == 0), stop=(kk == 7))))
        S.op("pe", items, reads=[r_oT, r_wo], writes=[r_pOut])
        zt, r_z = z.next()
        S.ins("dve", "scalar_tensor_tensor", reads=[r_xi, r_pOut], writes=[r_z], out=zt[:], in0=xi[:], scalar=ALPHA,
              in1=pOut.rearrange("p a c -> p (a c)"), op0=ALU.mult, op1=ALU.add)
        xt1, r_xt1 = x1t.next()
        layer_norm_tile(k, zt, r_z, lnp[0], lnp[1], r_ln, xt1[:], r_xt1, stt, r_st)
        deferred.append(lambda: S.dma("sp", out=k.x1_d[tok:tok + 128, :], in_=xt1[:], reads=[r_xt1], writes=[r_x1d[ti]]))
        xb, r_xb = x1b.next()
        S.ins("pool", "tensor_copy", reads=[r_xt1], writes=[r_xb], out=xb[:], in_=xt1[:])
        c1st[ti] = (xb, r_xb)

    def c1b(ti):
        g_, t = divmod(ti, 4)
        flush_deferred()
        xb, r_xb = c1st.pop(ti)
        xT_t, r_xT = xTs[g_]
        pt, r_pt = pTb.next()
        S.op("pe", [("transpose", dict(out=pt[:, kk, :], in_=xb[:, kk * 128:(kk + 1) * 128], identity=k.ident_b[:]))
                    for kk in range(8)], reads=[r_xb, k.r_id], writes=[r_pt])
        S.ins("act", "activation", reads=[r_pt], writes=[r_xT], out=xT_t[:, :, t * 128:(t + 1) * 128], in_=pt, func=AF.Copy)

    load_c1(0)
    load_c1(1)
    for i in range(5):
        load_w(i)
    for t in range(4):
        c1a(t)
        load_c1(t + 2)
        c1b(t)
    ydef = []
    A_AT = {1: 0, 6: 1, 11: 2, 16: 3}
    B_AT = {5: 0, 10: 1, 15: 2, 20: 3}
    for g in range(NG):
        xT_t, r_xT = xTs[g]
        for c in range(NCH):
            load_w(g * NCH + c + 5)
            wt, r_wt = wq.pop(g * NCH + c)
            pgu, r_pgu = pGU.next()
            items = []
            for m in (0, 1):
                for kk in range(8):
                    items.append(("matmul", dict(out=pgu[:, m, :], lhsT=wt[:, m, kk * 128:(kk + 1) * 128], rhs=xT_t[:, kk, :],
                                                 start=(kk == 0), stop=(kk == 7))))
            S.op("pe", items, reads=[r_wt, r_xT], writes=[r_pgu])
            sgt, r_sg = sg.next()
            S.ins("act", "activation", reads=[r_pgu], writes=[r_sg], out=sgt[:], in_=pgu[:, 0, :], func=AF.Silu)
            S.ins("dve", "tensor_tensor", reads=[r_sg, r_pgu], writes=[r_hT], out=hT[:, c, :], in0=sgt[:], in1=pgu[:, 1, :],
                  op=ALU.mult)
            if c == 2:
                while ydef:
                    ydef.pop(0)()
            if g + 1 < NG:
                if c in A_AT:
                    ti = (g + 1) * 4 + A_AT[c]
                    c1a(ti)
                    load_c1(ti + 2 if A_AT[c] < 2 else -1 + 10 ** 9)
                if c in B_AT:
                    c1b((g + 1) * 4 + B_AT[c])
                if c == 0:
                    load_c1((g + 1) * 4)
                    load_c1((g + 1) * 4 + 1)
        flush_deferred()
        xr = {}

        def load_x1r(t, g=g, xr=xr):
            ti = g * 4 + t
            xt_, r_xt_ = x1r.next()
            S.dma("sp", out=xt_[:], in_=k.x1_d[ti * 128:(ti + 1) * 128, :], reads=[r_x1d[ti]], writes=[r_xt_])
            xr[t] = (xt_, r_xt_)

        load_x1r(0)
        load_x1r(1)
        for t in range(4):
            tok = g * 512 + t * 128
            po, r_po = (pOut, r_pOut) if t % 2 == 0 else pGU.next()
            items = []
            for half in (0, 1):
                for c in range(NCH):
                    items.append(("matmul", dict(out=po[:, half, :], lhsT=hT[:, c, t * 128:(t + 1) * 128],
                                                 rhs=Wd[:, c, half * 512:(half + 1) * 512], start=(c == 0), stop=(c == NCH - 1))))
            S.op("pe", items, reads=[r_hT, r_wd], writes=[r_po])
            xt_, r_xt_ = xr[t]
            zt, r_z = z.next()
            S.ins("dve", "scalar_tensor_tensor", reads=[r_xt_, r_po], writes=[r_z], out=zt[:], in0=xt_[:],
                  scalar=ALPHA, in1=po.rearrange("p a c -> p (a c)"), op0=ALU.mult, op1=ALU.add)
            if t + 2 < 4:
                load_x1r(t + 2)
            yt, r_yt = yo.next()
            layer_norm_tile(k, zt, r_z, lnp[2], lnp[3], r_ln, yt[:], r_yt, stt2, r_st2)
            ydef.append(lambda tok=tok, yt=yt, r_yt=r_yt: S.dma("sp", out=dst[tok:tok + 128, :], in_=yt[:], reads=[r_yt]))
            if len(ydef) > 1:
                ydef.pop(0)()
    while ydef:
        ydef.pop(0)()
    S.barrier()
    A.pop()


def phase_A1(k):
    S, A, ps, T = k.S, k.A, k.ps, k.T
    A.push()
    w = A.alloc("w_in1", [128, 8, IN1_W], BF16)
    r_w = Res()
    S.dma("sp", out=w[:], in_=k.w_in1_b.rearrange("(k p) n -> p k n", p=128), reads=[k.r_w["w_in1"]], writes=[r_w])
    xs = A.ring("xs", 4, [128, D], F32)
    cs = A.ring("cs", 6, [128, 64], F32)
    xT = A.ring("xT", 2, [128, 8, 128], BF16)
    xsb = A.ring("xsb", 6, [128, D], F32)
    tmpA = A.ring("tmpA", 2, [128, 2, 512], F32)
    tmpB = A.ring("tmpB", 2, [128, 2, 512], F32)
    qkr = A.ring("qkr", 7, [128, D], BF16)
    vst = A.ring("vst", 2, [128, 8, 129], BF16)
    stq = A.ring("stq", 2, [128, 8, 512], BF16)
    stk = A.ring("stk", 2, [128, 8, 512], BF16)
    for t_, r_ in zip(vst.tiles, vst.res):
        S.ins("pool", "memset", writes=[r_], ap=t_[:, :, 128:129], constant=1.0)
    pT = ps[:, 0, :].bitcast(BF16).rearrange("p (a c) -> p a c", c=128)
    r_pT = Res()
    xb16 = A.ring("xb16", 2, [128, D], BF16)
    pSec = Ring([ps[:, 2:4, :], ps[:, 4:6, :]])
    pTq = Ring([ps[:, 6, :].bitcast(BF16).rearrange("p (a c) -> p a c", c=128),
                ps[:, 7, :].bitcast(BF16).rearrange("p (a c) -> p a c", c=128)])
    pending = []
    q3 = []

    def flush():
        while pending:
            pending.pop(0)()

    loaded = {}
    NTILE = T // 128

    def prefetch(upto):
        for ti in range(len(loaded), min(upto + 1, NTILE)):
            tok_ = ti * 128
            pos_ = seq_pos(k, tok_)
            xt_, r_xt_ = xs.next()
            ct_, r_ct_ = cs.next()
            S.dma("sp", out=xt_[:], in_=k.x2_d[tok_:tok_ + 128, :], writes=[r_xt_])
            S.dma("sp", out=ct_[:], in_=k.rope_seq[pos_:pos_ + 128, :], writes=[r_ct_])
            loaded[ti] = (xt_, r_xt_, ct_, r_ct_)

    for g in range(T // 512):
        sq_t, r_sq = stq.next()
        sk_t, r_sk = stk.next()
        for t in range(4):
            tok = g * 512 + t * 128
            pos = seq_pos(k, tok)
            prefetch(g * 4 + t + 2)
            xt, r_xt, ct, r_ct = loaded[g * 4 + t]
            xh, r_xh = xb16.next()
            S.ins("act", "activation", reads=[r_xt], writes=[r_xh], out=xh[:], in_=xt[:], func=AF.Copy)
            S.op("pe", [("transpose", dict(out=pT[:, kk, :], in_=xh[:, kk * 128:(kk + 1) * 128], identity=k.ident_b[:]))
                        for kk in range(8)], reads=[r_xh, k.r_id], writes=[r_pT])
            xTt, r_xT = xT.next()
            S.ins("act", "activation", reads=[r_pT], writes=[r_xT], out=xTt[:], in_=pT, func=AF.Copy)
            backs = []
            for sec in range(3):
                p0, r_p0 = pSec.next()
                p0f = p0.rearrange("p a c -> p (a c)")
                items = []
                for half in (0, 1):
                    for kk in range(8):
                        c0 = sec * 1024 + half * 512
                        items.append(("matmul", dict(out=p0[:, half, :], lhsT=xTt[:, kk, :], rhs=w[:, kk, c0:c0 + 512],
                                                     start=(kk == 0), stop=(kk == 7))))
                S.op("pe", items, reads=[r_xT, r_w], writes=[r_p0])
                if sec == 2:
                    vt, r_vt = vst.next()
                    S.ins("dve", "tensor_copy", reads=[r_p0], writes=[r_vt], out=vt[:, :, 0:128],
                          in_=p0f.rearrange("p (h d) -> p h d", d=128))
                    S.dma("sp", out=k.v1_d[tok:tok + 128, :], in_=vt[:].rearrange("p h e -> p (h e)"), reads=[r_vt])
                    continue
                xb, r_xb = xsb.next()
                S.ins("act", "activation", reads=[r_p0], writes=[r_xb], out=xb[:], in_=p0f, func=AF.Copy)
                def stage2(sec=sec, xb=xb, r_xb=r_xb, ct=ct, r_ct=r_ct, sq_t=sq_t, r_sq=r_sq, sk_t=sk_t, r_sk=r_sk, t=t, g=g):
                    x4 = xb[:].rearrange("p (h b f) -> p h b f", b=2, f=32)
                    x1, x2 = x4[:, :, 0, :], x4[:, :, 1, :]
                    c3 = ct[:].rearrange("p (s f) -> p s f", s=2)
                    cosb = c3[:, 0:1, :].to_broadcast([128, 16, 32])
                    sinb = c3[:, 1:2, :].to_broadcast([128, 16, 32])
                    qk_t, r_qk = qkr.next()
                    r_qa, r_qb = Res(), Res()
                    o4 = qk_t[:].rearrange("p (h b f) -> p h b f", b=2, f=32)
                    ta, r_ta = tmpA.next()
                    tb, r_tb = tmpB.next()
                    tva = [ta[:, i, :].rearrange("p (h f) -> p h f", f=32) for i in range(2)]
                    tvb = [tb[:, i, :].rearrange("p (h f) -> p h f", f=32) for i in range(2)]
                    S.ins("dve", "tensor_tensor", reads=[r_xb, r_ct], writes=[r_ta], out=tva[0], in0=x1, in1=cosb, op=ALU.mult)
                    S.ins("dve", "tensor_tensor", reads=[r_xb, r_ct], writes=[r_ta], out=tva[1], in0=x2, in1=sinb, op=ALU.mult)
                    S.ins("dve", "tensor_tensor", reads=[r_ta, r_qk], writes=[r_qa], out=o4[:, :, 0, :], in0=tva[0],
                          in1=tva[1], op=ALU.subtract)
                    S.ins("pool", "tensor_tensor", reads=[r_xb, r_ct], writes=[r_tb], out=tvb[0], in0=x1, in1=sinb, op=ALU.mult)
                    S.ins("pool", "tensor_tensor", reads=[r_xb, r_ct], writes=[r_tb], out=tvb[1], in0=x2, in1=cosb, op=ALU.mult)
                    S.ins("pool", "tensor_tensor", reads=[r_tb, r_qk], writes=[r_qb], out=o4[:, :, 1, :], in0=tvb[0],
                          in1=tvb[1], op=ALU.add)

                    def back():
                        pq, r_pq = pTq.next()
                        S.op("pe", [("transpose", dict(out=pq[:, j, :], in_=qk_t[:, j * 128:(j + 1) * 128], identity=k.ident_b[:]))
                                    for j in range(8)], reads=[r_qa, r_qb, k.r_id], writes=[r_pq])
                        dst_t, r_dst = (sq_t, r_sq) if sec == 0 else (sk_t, r_sk)
                        S.ins("act", "activation", reads=[r_pq, r_qa, r_qb], writes=[r_dst, r_qk],
                              out=dst_t[:, :, t * 128:(t + 1) * 128], in_=pq, func=AF.Copy)
                        if t == 3:
                            dd = k.qT_d if sec == 0 else k.kT_d
                            S.dma("sp", out=dd.rearrange("(c p) t -> p c t", p=128)[:, :, g * 512:(g + 1) * 512], in_=dst_t[:],
                                  reads=[r_dst])
                    q3.append(back)
                backs.append(stage2)
            while q3:
                q3.pop(0)()
            while pending:
                pending.pop(0)()
            pending.extend(backs)
    while pending or q3:
        q3_now = list(q3)
        del q3[:]
        for f in q3_now:
            f()
        while pending:
            pending.pop(0)()
    S.barrier()
    A.pop()


def phase_B1(k):
    S, A, ps = k.S, k.A, k.ps
    Lmax = max(k.seqs)
    NTmax = Lmax // 128
    qv = k.qT_d.rearrange("(c p) t -> p c t", p=128)
    kv = k.kT_d.rearrange("(c p) t -> p c t", p=128)
    A.push()
    sets = []
    for i in range(2):
        sets.append((A.alloc("KTd", [128, 2, Lmax], BF16), A.alloc("QTd", [128, 2, Lmax], BF16),
                     A.alloc("Vd", [128, NTmax, 258], BF16), Res()))
    E = A.ring("Ed", 3, [128, 2, 384], BF16)
    Ost = A.ring("Od", 2, [128, 3, 256], BF16)
    rc = A.ring("rcd", 2, [128, 2, 3], F32)
    t1 = A.ring("t1d", 2, [128, 3, 128], F32)
    t2 = A.ring("t2d", 2, [128, 3, 128], F32)
    ssd = A.ring("ssd", 2, [128, 3], F32)
    pS = Ring([ps[:, 0:2, :], ps[:, 2:4, :]])
    pO = Ring([ps[:, 4:6, :], ps[:, 6:8, :]])
    passes = []
    tok0 = 0
    for L in k.seqs:
        for hp in range(4):
            passes.append((tok0, L, hp))
        tok0 += L

    def load(p):
        if p >= len(passes):
            return
        tok0, L, hp = passes[p]
        KT, QT, V, r_in = sets[p % 2]
        NT = L // 128
        S.dma("sp", out=QT[:, :, 0:L], in_=qv[:, 2 * hp:2 * hp + 2, tok0:tok0 + L], writes=[r_in])
        S.dma("sp", out=KT[:, :, 0:L], in_=kv[:, 2 * hp:2 * hp + 2, tok0:tok0 + L], writes=[r_in])
        S.dma("sp", out=V[:, 0:NT, :], in_=k.v1_d[tok0:tok0 + L, hp * 258:(hp + 1) * 258].rearrange("(n p) f -> p n f", p=128),
              writes=[r_in])

    load(0)
    steps = []
    for p, (tok0, L, hp) in enumerate(passes):
        KT, QT, V, r_in = sets[p % 2]
        NT = L // 128
        first = True
        for (q0, nq) in qgroups(NT, 3):
            gst = {}
            NQ = nq * 128
            for hl in range(2):
                hst = {}
                for kc in range(NT):
                    st = {}

                    def qk(st=st, q0=q0, NQ=NQ, hl=hl, kc=kc, KT=KT, QT=QT, r_in=r_in):
                        st["ps"] = pS.next()
                        p_s, r_ps = st["ps"]
                        items = []
                        for a in (0, 1):
                            items.append(("matmul", dict(out=p_s[:, a, 0:NQ], lhsT=KT[a * 64:(a + 1) * 64, hl, kc * 128:(kc + 1) * 128],
                                                         rhs=QT[a * 64:(a + 1) * 64, hl, q0 * 128:q0 * 128 + NQ],
                                                         start=True, stop=True, skip_group_check=True)))
                        S.op("pe", items, reads=[r_in], writes=[r_ps])

                    def mid(st=st, NQ=NQ, pre=(p + 1 if first else None)):
                        if pre is not None:
                            load(pre)
                        p_s, r_ps = st["ps"]
                        st["e"] = E.next()
                        e_t, r_e = st["e"]
                        S.ins("act", "activation", reads=[r_ps], writes=[r_e], out=e_t[:, :, 0:NQ], in_=p_s[:, :, 0:NQ],
                              func=AF.Exp, scale=0.125)

                    first = False

                    def pv(st=st, hst=hst, kc=kc, nq=nq, hl=hl, V=V, r_in=r_in, NT=NT):
                        if kc == 0:
                            hst["po"] = pO.next()
                        po, r_po = hst["po"]
                        e_t, r_e = st["e"]
                        items = []
                        for a in (0, 1):
                            for qt in range(nq):
                                items.append(("matmul", dict(out=po[:, a, qt * 129:(qt + 1) * 129],
                                                             lhsT=e_t[:, a, qt * 128:(qt + 1) * 128],
                                                             rhs=V[:, kc, hl * 129:(hl + 1) * 129],
                                                             start=(kc == 0 and qt == 0), stop=(kc == NT - 1),
                                                             skip_group_check=True)))
                        S.op("pe", items, reads=[r_e, r_in], writes=[r_po])

                    fin = None
                    if kc == NT - 1:
                        def fin(hst=hst, gst=gst, q0=q0, nq=nq, NQ=NQ, hl=hl, tok0=tok0, hp=hp):
                            if hl == 0:
                                gst["ot"] = Ost.next()
                            ot, r_ot = gst["ot"]
                            po, r_po = hst["po"]
                            pov = po[:, :, 0:nq * 129].rearrange("p a (q e) -> p a q e", e=129)
                            rct, r_rc = rc.next()
                            S.ins("dve", "reciprocal", reads=[r_po], writes=[r_rc], out=rct[:, :, 0:nq].unsqueeze(3), in_=pov[:, :, :, 128:129])
                            S.ins("dve", "tensor_scalar", reads=[r_rc, k.r_c], writes=[r_rc], out=rct[:, 1, 0:nq], in0=rct[:, 1, 0:nq],
                                  scalar1=k.lam[:, 1:2], scalar2=None, op0=ALU.mult)
                            a1, r_a1 = t1.next()
                            a2, r_a2 = t2.next()
                            S.ins("dve", "tensor_tensor", reads=[r_po, r_rc], writes=[r_a1], out=a1[:, 0:nq, :], in0=pov[:, 0, :, 0:128],
                                  in1=rct[:, 0, 0:nq].unsqueeze(2).to_broadcast([128, nq, 128]), op=ALU.mult)
                            S.ins("dve", "tensor_tensor", reads=[r_po, r_rc], writes=[r_a2], out=a2[:, 0:nq, :], in0=pov[:, 1, :, 0:128],
                                  in1=rct[:, 1, 0:nq].unsqueeze(2).to_broadcast([128, nq, 128]), op=ALU.mult)
                            S.ins("pool", "tensor_tensor", reads=[r_a1, r_a2], writes=[r_a1], out=a1[:, 0:nq, :], in0=a1[:, 0:nq, :],
                                  in1=a2[:, 0:nq, :], op=ALU.add)
                            S.ins("pool", "tensor_tensor", reads=[r_a1], writes=[r_a2], out=a2[:, 0:nq, :], in0=a1[:, 0:nq, :],
                                  in1=a1[:, 0:nq, :], op=ALU.mult)
                            sst, r_ss = ssd.next()
                            S.ins("dve", "tensor_reduce", reads=[r_a2], writes=[r_ss], out=sst[:, 0:nq], in_=a2[:, 0:nq, :], axis=AX.X,
                                  op=ALU.add)
                            rsqrt_pool(k, sst[:, 0:nq], sst[:, 0:nq], 1.0 / 128, SUBLN_EPS, [r_ss], [r_ss])
                            S.ins("dve", "tensor_tensor", reads=[r_a1, r_ss], writes=[r_a1], out=a1[:, 0:nq, :], in0=a1[:, 0:nq, :],
                                  in1=sst[:, 0:nq].unsqueeze(2).to_broadcast([128, nq, 128]), op=ALU.mult)
                            S.ins("pool", "tensor_tensor", reads=[r_a1, k.r_c], writes=[r_ot], out=ot[:, 0:nq, hl * 128:(hl + 1) * 128],
                                  in0=a1[:, 0:nq, :], in1=k.G1[:].unsqueeze(1).to_broadcast([128, nq, 128]), op=ALU.mult)
                            if hl == 1:
                                t0 = tok0 + q0 * 128
                                S.dma("sp", out=k.ao1_d[t0:t0 + NQ, hp * 256:(hp + 1) * 256].rearrange("(q p) f -> p q f", p=128),
                                      in_=ot[:, 0:nq, :], reads=[r_ot])
                    steps.append((qk, mid, pv, fin))
    run_pipeline(steps, skew=2)
    S.barrier()
    A.pop()


def rope_tables():
    theta = np.float32(10000.0)
    t = np.arange(4096)

    def cs(pos, dim):
        inv = (theta ** (-(np.arange(0, dim, 2, dtype=np.float32)) / np.float32(dim))).astype(np.float32)
        ang = pos.astype(np.float32)[:, None] * inv[None, :]
        return np.cos(ang).astype(np.float32), np.sin(ang).astype(np.float32)

    rc, rs = cs(t // GRID_W, 32)
    cc, cs_ = cs(t % GRID_W, 32)
    ax = np.stack([np.stack([rc, cc], 1), np.stack([rs, cs_], 1)], 1)
    sc, ss = cs(t, 64)
    sq = np.stack([sc, ss], 1)
    return np.ascontiguousarray(ax.reshape(4096, 64)), np.ascontiguousarray(sq.reshape(4096, 64))


def host_weights(inp):
    f = lambda a: np.ascontiguousarray(np.asarray(a, dtype=np.float32))
    w0 = f(inp["w_in_mix0"][0])
    na_q, na_k, na_v = w0[:, 0:512], w0[:, 512:1024], w0[:, 1024:1536]
    gq, gk, gv = w0[:, 1536:2048], w0[:, 2048:2176], w0[:, 2176:2304]
    order = [0, 4, 1, 5, 2, 6, 3, 7]
    gqp = np.concatenate([gq[:, h * 64:(h + 1) * 64] for h in order], 1)
    w_in0 = f(np.concatenate([na_q, na_k, gqp, gk, na_v, gv], 1))
    wg = f(inp["w_ffn_gate"]).reshape(2, 8, 128, NCH, 128).transpose(0, 3, 2, 1, 4).reshape(2, NCH, 128, 1024)
    wu = f(inp["w_ffn_up"]).reshape(2, 8, 128, NCH, 128).transpose(0, 3, 2, 1, 4).reshape(2, NCH, 128, 1024)
    rpb = f(inp["rpb_na"][0])
    kc = np.arange(64)[:, None]
    qc = np.arange(64)[None, :]
    c0 = np.clip(qc - 8, 0, GRID_W - 16)
    valid = (kc >= c0) & (kc < c0 + 16)
    idx = np.clip(kc - qc + 15, 0, 30)
    tpm = np.where(valid[None, None], rpb[:, :, idx], np.float32(NEG)).astype(np.float32)
    ax, sq = rope_tables()
    return dict(
        w_in0=w_in0, w_out0=f(inp["w_out_mix0"][0]), w_in1=f(inp["w_in_mix1"][0]), w_out1=f(inp["w_out_mix1"][0]),
        wg=f(wg), wu=f(wu), wd=f(inp["w_ffn_down"]), tpm=f(tpm),
        g_q=f(inp["g_q_gqa"][0]), g_k=f(inp["g_k_gqa"][0]), lq1=f(inp["lam_q1"][0]), lk1=f(inp["lam_k1"][0]),
        lq2=f(inp["lam_q2"][0]), lk2=f(inp["lam_k2"][0]), g_sub=f(inp["g_subln"][0]),
        ln_mix_g=f(inp["ln_mix_g"]), ln_mix_b=f(inp["ln_mix_b"]), ln_ffn_g=f(inp["ln_ffn_g"]), ln_ffn_b=f(inp["ln_ffn_b"]),
        ident=np.eye(128, dtype=np.float32), rope_ax=ax, rope_seq=sq,
    )


_CACHE = {}


def kernel(**inputs):
    xp = np.asarray(inputs["x_prompt"], dtype=np.float32)
    xs = np.asarray(inputs["x_sample"], dtype=np.float32)
    hw = host_weights(inputs)
    if "nc" not in _CACHE:
        _CACHE["nc"] = build(SEQS)[0]
    nc = _CACHE["nc"]
    in_maps = []
    for c in range(N_CORES):
        xc = np.concatenate([xp[2 * c], xp[2 * c + 1], xs[c]], 0)
        m = dict(hw)
        m["x"] = np.ascontiguousarray(xc)
        in_maps.append(m)
    res = run_bass_kernel_spmd(nc, in_maps, core_ids=list(range(N_CORES)))
    yp = np.empty_like(xp)
    ys = np.empty_like(xs)
    for c in range(N_CORES):
        y = np.asarray(res.results[c]["y"], dtype=np.float32)
        yp[2 * c] = y[0:2048]
        yp[2 * c + 1] = y[2048:4096]
        ys[c] = y[4096:8192]
    return (yp, ys)
```

```python
import math
from contextlib import ExitStack

import numpy as np
import concourse.bass as bass
import concourse.mybir as mybir
from concourse.bass_utils import run_bass_kernel_spmd

F32 = mybir.dt.float32
BF16 = mybir.dt.bfloat16
U8 = mybir.dt.uint8
AF = mybir.ActivationFunctionType
ALU = mybir.AluOpType
AX = mybir.AxisListType

D = 1024
DFF = 2816
NCH = DFF // 128
GRID_W = 64
IN0_W = 2304
IN1_W = 3072
ALPHA = (2.0 * 2) ** 0.25
LAMBDA_INIT = 0.8 - 0.6 * math.exp(-0.3 * 1)
LN_EPS = 1e-5
RMS_EPS = 1e-6
SUBLN_EPS = 1e-5
NEG = -30000.0
N_CORES = 8
SEQS = (2048, 2048, 4096)

COMPUTE = ("pe", "act", "dve", "pool")


class Res:
    __slots__ = ("name", "w", "r")

    def __init__(self, name=""):
        self.name = name
        self.w = None
        self.r = {}


class Sched:
    def __init__(self, nc, ring=12):
        self.nc = nc
        self.engs = ("pe", "act", "dve", "pool", "sp")
        self.streams = {e: [] for e in self.engs}
        self.cnt = {}
        self.seen = {e: {} for e in self.engs}
        self.ring = ring
        self.dma_k = {e: 0 for e in self.engs}
        self.semkeys = list(COMPUTE)
        for e in ("sp", "pool"):
            for i in range(ring):
                self.semkeys.append(("d", e, i))
        for k in self.semkeys:
            self.cnt[k] = 0
        self.n_ins = 0
        self.n_wait = 0

    def _need(self, eng, tickets):
        seen = self.seen[eng]
        best = {}
        for (k, v) in tickets:
            if v <= seen.get(k, 0):
                continue
            if v > best.get(k, 0):
                best[k] = v
        out = []
        for k, v in best.items():
            seen[k] = v
            out.append((k, v))
        return out

    def op(self, eng, items, reads=(), writes=(), dma=False):
        tickets = []
        for r in reads:
            if r.w is not None:
                if dma or r.w[0] != eng or eng != "pe":
                    tickets.append(r.w)
        for w in writes:
            if w.w is not None and (dma or w.w[0] != eng):
                tickets.append(w.w)
            for k, v in w.r.items():
                if dma or k != eng:
                    tickets.append((k, v))
        if dma:
            i = self.dma_k[eng]
            self.dma_k[eng] = i + 1
            key = ("d", eng, i % self.ring)
            if self.cnt[key] > 0:
                tickets.append((key, self.cnt[key]))
            self.cnt[key] += 16
            ticket = (key, self.cnt[key])
            inc = (key, 16)
        else:
            self.cnt[eng] += 1
            ticket = (eng, self.cnt[eng])
            inc = (eng, 1)
        waits = self._need(eng, tickets)
        self.n_wait += len(waits)
        self.n_ins += len(items)
        self.streams[eng].append((waits, items, inc))
        k, v = ticket
        for r in reads:
            if v > r.r.get(k, 0):
                r.r[k] = v
        for w in writes:
            w.w = ticket
            w.r = {}
        return ticket

    def ins(self, eng, name, reads=(), writes=(), **kw):
        return self.op(eng, [(name, kw)], reads, writes)

    def dma(self, eng, out, in_, reads=(), writes=()):
        return self.op(eng, [("dma_start", dict(out=out, in_=in_))], reads, writes, dma=True)

    def barrier(self, final=False):
        tickets = [(k, v) for k, v in self.cnt.items() if v > 0 and (final or not (isinstance(k, tuple) and k[1] == "pool"))]
        for e in self.engs:
            waits = self._need(e, [t for t in tickets if t[0] != e])
            self.streams[e].append((waits, None, None))

    def emit(self, sems, block):
        streams = self.streams

        def body_for(ename):
            def body(e):
                for waits, items, inc in streams[ename]:
                    for (k, v) in waits:
                        e.wait_ge(sems[k], v)
                    if items is None:
                        continue
                    ins = None
                    for name, kw in items:
                        ins = getattr(e, name)(**kw)
                    ins.then_inc(sems[inc[0]], inc[1])
            return body

        block.tensor(body_for("pe"))
        block.scalar(body_for("act"))
        block.vector(body_for("dve"))
        block.gpsimd(body_for("pool"))
        block.sync(body_for("sp"))


class Ring:
    def __init__(self, tiles):
        self.tiles = tiles
        self.res = [Res() for _ in tiles]
        self.i = 0

    def next(self):
        n = len(self.tiles)
        t, r = self.tiles[self.i % n], self.res[self.i % n]
        self.i += 1
        return t, r


class Arena:
    def __init__(self, nc, base, size):
        self.nc, self.base, self.size = nc, base, size
        self.top = 0
        self.marks = []
        self.uid = 0
        self.peak = 0

    def alloc(self, name, shape, dt):
        esz = {F32: 4, BF16: 2, U8: 1}[dt]
        nbytes = esz
        for s in shape[1:]:
            nbytes *= s
        off = (self.top + 31) // 32 * 32
        self.uid += 1
        t = self.nc.alloc_sbuf_tensor_at(f"{name}_{self.uid}", list(shape), dt, offset=self.base + off)
        self.top = off + nbytes
        self.peak = max(self.peak, self.top)
        assert self.top <= self.size, (name, self.top, self.size)
        return t

    def ring(self, name, n, shape, dt):
        return Ring([self.alloc(f"{name}{i}", shape, dt) for i in range(n)])

    def push(self):
        self.marks.append(self.top)

    def pop(self):
        self.top = self.marks.pop()


class K:
    pass


def build(seqs=SEQS, dbg=False):
    T = sum(seqs)
    assert T % 512 == 0
    nc = bass.Bass("TRN2", target_bir_lowering=False)
    k = K()
    k.nc = nc
    k.T = T
    k.seqs = seqs

    def din(name, shape, dt=F32):
        return nc.dram_tensor(name, list(shape), dt, kind="ExternalInput").ap()

    def dscr(name, shape, dt):
        kind = "ExternalOutput" if dbg else "Internal"
        return nc.dram_tensor(name, list(shape), dt, kind=kind).ap()

    k.x = din("x", [T, D])
    k.w_in0 = din("w_in0", [D, IN0_W])
    k.w_out0 = din("w_out0", [D, D])
    k.w_in1 = din("w_in1", [D, IN1_W])
    k.w_out1 = din("w_out1", [D, D])
    k.wg = din("wg", [2, NCH, 128, 8 * 128])
    k.wu = din("wu", [2, NCH, 128, 8 * 128])
    k.wd = din("wd", [2, DFF, D])
    k.tpm = din("tpm", [8, 15, 64, 64])
    k.g_q = din("g_q", [64])
    k.g_k = din("g_k", [64])
    k.lq1 = din("lq1", [64])
    k.lk1 = din("lk1", [64])
    k.lq2 = din("lq2", [64])
    k.lk2 = din("lk2", [64])
    k.g_sub = din("g_sub", [128])
    k.ln_mix_g = din("ln_mix_g", [2, D])
    k.ln_mix_b = din("ln_mix_b", [2, D])
    k.ln_ffn_g = din("ln_ffn_g", [2, D])
    k.ln_ffn_b = din("ln_ffn_b", [2, D])
    k.ident = din("ident", [128, 128])
    k.rope_ax = din("rope_ax", [4096, 64])
    k.rope_seq = din("rope_seq", [4096, 64])
    k.y = nc.dram_tensor("y", [T, D], F32, kind="ExternalOutput").ap()

    k.w_in0_b = dscr("w_in0_b", [D, IN0_W], BF16)
    k.w_out0_b = dscr("w_out0_b", [D, D], BF16)
    k.w_in1_b = dscr("w_in1_b", [D, IN1_W], BF16)
    k.w_out1_b = dscr("w_out1_b", [D, D], BF16)
    k.wg_b = dscr("wg_b", [2, NCH, 128, 1024], BF16)
    k.wu_b = dscr("wu_b", [2, NCH, 128, 1024], BF16)
    k.wd_b = dscr("wd_b", [2, DFF, D], BF16)
    k.naT_d = dscr("naT_d", [1024, T], BF16)
    k.gT_d = dscr("gT_d", [640, T], BF16)
    k.vna_d = dscr("vna_d", [T, 520], BF16)
    k.vg_d = dscr("vg_d", [T, 130], BF16)
    k.ao_d = dscr("ao_d", [T, D], BF16)
    k.x2_d = dscr("x2_d", [T, D], F32)
    k.qT_d = dscr("qT_d", [1024, T], BF16)
    k.kT_d = dscr("kT_d", [1024, T], BF16)
    k.v1_d = dscr("v1_d", [T, 1032], BF16)
    k.ao1_d = dscr("ao1_d", [T, D], BF16)
    k.mb_d = dscr("mb_d", [128, 8 * 14 * 64], BF16)
    k.x1_d = dscr("x1_d", [T, D], F32)
    k.dbg = dbg
    if dbg:
        k.dbg_x1 = dscr("dbg_x1", [2, T, D], F32)
        k.dbg_z = dscr("dbg_z", [2, T, D], F32)
        k.dbg_h = dscr("dbg_h", [2, DFF, T], BF16)
        k.dbg_s = dscr("dbg_s", [4, T, 1], F32)
        k.dbg_zn = dscr("dbg_zn", [T, D], F32)
    k.dbg_tok = None

    S = Sched(nc)
    k.S = S
    with ExitStack() as es:
        ARENA = 207000
        arena_t = es.enter_context(nc.sbuf_tensor("arena", [128, ARENA], U8))
        base = nc.sbuf_base - ARENA
        A = Arena(nc, base, ARENA)
        k.A = A
        k.ps = es.enter_context(nc.psum_tensor("ps", [128, 8, 512], F32))
        sems = {key: es.enter_context(nc.semaphore(f"s{i}")) for i, key in enumerate(S.semkeys)}
        block = es.enter_context(nc.Block())

        setup(k)
        phase_A0(k)
        A.pop()
        cast_late(k)
        tok0 = 0
        for L in seqs:
            phase_B0_na(k, tok0, L)
            phase_B0_gqa(k, tok0, L)
            tok0 += L
        phase_C(k, 0, k.x, k.ao_d, k.x2_d)
        phase_A1(k)
        phase_B1(k)
        phase_C(k, 1, k.x2_d, k.ao1_d, k.y)
        S.barrier(final=True)
        S.emit(sems, block)
    k.stats = dict(n_ins=S.n_ins, n_wait=S.n_wait, peak=A.peak)
    return nc, k


def cast_weights(k, casts):
    S = k.S
    for name, src, dst in casts:
        r = Res(name)
        k.r_w[name] = r
        s2 = src if len(src.shape) == 2 else src.rearrange("c p n -> (c p) n")
        d2 = dst if len(dst.shape) == 2 else dst.rearrange("c p n -> (c p) n")
        if s2.shape[1] > 2048:
            half = s2.shape[1] // 2
            s2 = s2.rearrange("r (a n) -> (r a) n", n=half)
            d2 = d2.rearrange("r (a n) -> (r a) n", n=half)
        rows = s2.shape[0]
        step = 1024
        for r0 in range(0, rows, step):
            r1 = min(rows, r0 + step)
            chain = k.cast_chain[k.cast_i % 4]
            k.cast_i += 1
            S.dma("pool", out=d2[r0:r1], in_=s2[r0:r1], writes=[r, chain])


def cast_late(k):
    casts = [("w_out0", k.w_out0, k.w_out0_b)]
    for l in range(2):
        if l == 1:
            casts += [("w_in1", k.w_in1, k.w_in1_b), ("w_out1", k.w_out1, k.w_out1_b)]
        casts += [(f"wg{l}", k.wg[l], k.wg_b[l]), (f"wu{l}", k.wu[l], k.wu_b[l]), (f"wd{l}", k.wd[l], k.wd_b[l])]
    cast_weights(k, casts)


def bcast_rows(ap, n=128):
    return ap.partition_broadcast(n)


def setup(k):
    S, A = k.S, k.A
    k.r_w = {}
    k.cast_chain = [Res() for _ in range(4)]
    k.cast_i = 0
    cast_weights(k, [("w_in0", k.w_in0, k.w_in0_b)])

    k.ident_f = A.alloc("ident_f", [128, 128], F32)
    k.ident_b = A.alloc("ident_b", [128, 128], BF16)
    k.r_id = Res("ident")
    S.dma("sp", out=k.ident_f[:], in_=k.ident, writes=[k.r_id])
    S.ins("dve", "tensor_copy", reads=[k.r_id], writes=[k.r_id], out=k.ident_b[:], in_=k.ident_f[:])
    k.G0 = A.alloc("G0", [128, 10, 64], F32)
    k.G1 = A.alloc("G1", [128, 128], F32)
    k.lam = A.alloc("lam", [128, 2], F32)
    k.cst = A.alloc("cst", [128, 8], F32)
    k.r_c = Res("consts")
    S.ins("dve", "memset", writes=[k.r_c], ap=k.cst[:, 0:1], constant=-0.5)
    S.ins("dve", "memset", writes=[k.r_c], ap=k.cst[:, 1:2], constant=LN_EPS)
    S.ins("dve", "memset", writes=[k.r_c], ap=k.cst[:, 2:3], constant=RMS_EPS)
    A.push()
    MBt = A.alloc("MB", [128, 8, 14, 64], BF16)
    v = A.alloc("vecs", [128, 6, 64], F32)
    g1 = A.alloc("g1t", [128, 128], F32)
    stage = A.alloc("mbst", [128, 8, 14, 64], F32)
    r_v = Res()
    for i, src in enumerate((k.g_q, k.g_k, k.lq1, k.lk1, k.lq2, k.lk2)):
        S.dma("sp", out=v[:, i, :], in_=bcast_rows(src), writes=[r_v])
    S.dma("sp", out=g1[:], in_=bcast_rows(k.g_sub), writes=[r_v])
    S.ins("dve", "tensor_scalar", reads=[r_v], writes=[k.r_c], out=k.G0[:, 0:8, :],
          in0=v[:, 0:1, :].to_broadcast([128, 8, 64]), scalar1=0.125, scalar2=None, op0=ALU.mult)
    S.ins("dve", "tensor_copy", reads=[r_v], writes=[k.r_c], out=k.G0[:, 8:10, :],
          in_=v[:, 1:2, :].to_broadcast([128, 2, 64]))
    S.ins("dve", "tensor_scalar", reads=[r_v], writes=[k.r_c], out=k.G1[:], in0=g1[:],
          scalar1=1.0 - LAMBDA_INIT, scalar2=None, op0=ALU.mult)
    pr = A.alloc("pr", [128, 2, 64], F32)
    sm = A.alloc("sm", [128, 2], F32)
    r_p = Res()
    S.ins("dve", "tensor_tensor", reads=[r_v], writes=[r_p], out=pr[:, 0, :], in0=v[:, 2, :], in1=v[:, 3, :], op=ALU.mult)
    S.ins("dve", "tensor_tensor", reads=[r_v], writes=[r_p], out=pr[:, 1, :], in0=v[:, 4, :], in1=v[:, 5, :], op=ALU.mult)
    S.ins("dve", "tensor_reduce", reads=[r_p], writes=[r_p], out=sm[:], in_=pr[:], axis=AX.X, op=ALU.add)
    S.ins("act", "activation", reads=[r_p], writes=[r_p], out=sm[:], in_=sm[:], func=AF.Exp)
    S.ins("dve", "tensor_tensor", reads=[r_p], writes=[k.r_c], out=k.lam[:, 0:1], in0=sm[:, 0:1], in1=sm[:, 1:2], op=ALU.subtract)
    S.ins("dve", "tensor_scalar", reads=[k.r_c], writes=[k.r_c], out=k.lam[:, 0:1], in0=k.lam[:, 0:1],
          scalar1=LAMBDA_INIT, scalar2=None, op0=ALU.add)
    S.ins("dve", "tensor_scalar", reads=[k.r_c], writes=[k.r_c], out=k.lam[:, 1:2], in0=k.lam[:, 0:1],
          scalar1=-1.0, scalar2=None, op0=ALU.mult)
    r_st = Res()
    for b in range(2):
        for h in range(8):
            S.dma("sp", out=stage[b * 64:(b + 1) * 64, h, :, :],
                  in_=k.tpm[h, b:b + 14].rearrange("r k q -> k r q"), writes=[r_st])
    r_mb = Res()
    S.ins("act", "activation", reads=[r_st], writes=[r_mb], out=MBt[:].rearrange("p h m q -> p (h m q)"),
          in_=stage[:].rearrange("p h m q -> p (h m q)"), func=AF.Exp)
    S.dma("sp", out=k.mb_d, in_=MBt[:].rearrange("p h m q -> p (h m q)"), reads=[r_mb])


def rsqrt_pool(k, out, in_, scale, eps, reads, writes):
    S = k.S
    S.ins("pool", "tensor_scalar", reads=reads, writes=writes, out=out, in0=in_, scalar1=scale, scalar2=eps,
          op0=ALU.mult, op1=ALU.add)
    S.ins("pool", "tensor_tensor", reads=list(writes) + [k.r_c], writes=writes, out=out, in0=out,
          in1=k.cst[:, 0:1].to_broadcast(list(out.shape)), op=ALU.pow)


def rsqrt_act(k, out, in_, scale, eps, reads, writes):
    S = k.S
    col = {LN_EPS: 1, RMS_EPS: 2}[eps]
    S.ins("act", "activation", reads=list(reads) + [k.r_c], writes=writes, out=out, in_=in_, func=AF.Sqrt,
          bias=k.cst[:, col:col + 1], scale=scale)
    S.ins("dve", "reciprocal", reads=writes, writes=writes, out=out, in_=out)


def seq_pos(k, tok):
    t0 = 0
    for L in k.seqs:
        if tok < t0 + L:
            return tok - t0
        t0 += L
    raise AssertionError


def layer_norm_tile(k, z, r_z, gt, bt, r_ln, out, r_out, st, r_st):
    S = k.S
    stats, mv, rstd, nmr = st
    S.ins("dve", "bn_stats", reads=[r_z], writes=[r_st], out=stats[:, 0, :], in_=z[:, 0:512])
    S.ins("dve", "bn_stats", reads=[r_z], writes=[r_st], out=stats[:, 1, :], in_=z[:, 512:1024])
    S.ins("dve", "bn_aggr", reads=[r_st], writes=[r_st], out=mv[:], in_=stats[:].rearrange("p a b -> p (a b)"))
    rsqrt_pool(k, rstd[:], mv[:, 1:2], 1.0, LN_EPS, [r_st], [r_st])
    S.ins("dve", "tensor_scalar", reads=[r_z, r_st], writes=[r_z], out=z[:], in0=z[:], scalar1=mv[:, 0:1],
          scalar2=rstd[:, 0:1], op0=ALU.subtract, op1=ALU.mult)
    if k.dbg and k.dbg_tok is not None:
        tok = k.dbg_tok
        S.dma("sp", out=k.dbg_s[0, tok:tok + 128, :], in_=mv[:, 0:1], reads=[r_st])
        S.dma("sp", out=k.dbg_s[1, tok:tok + 128, :], in_=mv[:, 1:2], reads=[r_st])
        S.dma("sp", out=k.dbg_s[2, tok:tok + 128, :], in_=rstd[:], reads=[r_st])
        S.dma("sp", out=k.dbg_s[3, tok:tok + 128, :], in_=nmr[:], reads=[r_st])
        S.dma("sp", out=k.dbg_zn[tok:tok + 128, :], in_=z[:], reads=[r_z])
    S.ins("pool", "tensor_tensor", reads=[r_z, r_ln], writes=[r_z], out=z[:], in0=z[:], in1=gt[:], op=ALU.mult)
    S.ins("pool", "tensor_tensor", reads=[r_z, r_ln], writes=[r_out], out=out, in0=z[:], in1=bt[:], op=ALU.add)


def phase_A0(k):
    S, A, ps, T = k.S, k.A, k.ps, k.T
    A.push()
    w = A.alloc("w_in0", [128, 8, IN0_W], BF16)
    r_w = Res()
    S.dma("sp", out=w[:], in_=k.w_in0_b.rearrange("(k p) n -> p k n", p=128), reads=[k.r_w["w_in0"]], writes=[r_w])
    xs = A.ring("xs", 4, [128, D], F32)
    cs = A.ring("cs", 6, [128, 64], F32)
    xT = A.ring("xT", 2, [128, 8, 512], BF16)
    sq = A.ring("sq", 2, [128, 640], F32)
    xsb = A.ring("xsb", 3, [128, 640], F32)
    sst = A.ring("sst", 3, [128, 10], F32)
    tmpA = A.ring("tmpA", 2, [128, 2, 320], F32)
    tmpB = A.ring("tmpB", 2, [128, 2, 320], F32)
    qkr = A.ring("qkr", 4, [128, 640], BF16)
    vna = A.ring("vna", 2, [128, 8, 65], BF16)
    vg = A.ring("vg", 2, [128, 2, 65], BF16)
    gst = A.ring("gst", 2, [128, 5, 512], BF16)
    nst = A.ring("nst", 2, [128, 8, 512], BF16)
    for t_, r_ in zip(vna.tiles + vg.tiles, vna.res + vg.res):
        S.ins("pool", "memset", writes=[r_], ap=t_[:, :, 64:65], constant=1.0)
    pT = ps[:, 0, :].bitcast(BF16).rearrange("p (a c) -> p a c", c=128)
    r_pT = Res()
    pSec = Ring([ps[:, 1:3, :], ps[:, 3:5, :]])
    pFMr = Ring([ps[:, 5, :], ps[:, 6, :]])
    xb16 = A.ring("xb16", 2, [128, D], BF16)
    pTq = ps[:, 7, :].bitcast(BF16).rearrange("p (a c) -> p a c", c=128)
    r_pTq = Res()
    pending = []
    q3 = []

    def flush():
        while pending:
            pending.pop(0)()

    loaded = {}
    NTILE = T // 128

    def prefetch(upto):
        for ti in range(len(loaded), min(upto + 1, NTILE)):
            tok_ = ti * 128
            pos_ = seq_pos(k, tok_)
            xt_, r_xt_ = xs.next()
            ct_, r_ct_ = cs.next()
            S.dma("sp", out=xt_[:], in_=k.x[tok_:tok_ + 128, :], writes=[r_xt_])
            S.dma("sp", out=ct_[:], in_=k.rope_ax[pos_:pos_ + 128, :], writes=[r_ct_])
            loaded[ti] = (xt_, r_xt_, ct_, r_ct_)

    fronts = {}
    xTg = {}

    def front(ti):
        if ti >= NTILE:
            return
        prefetch(ti)
        g_, t_ = divmod(ti, 4)
        if t_ == 0:
            xTg[g_] = xT.next()
        xTt_, r_xT_ = xTg[g_]
        xt_, r_xt_, _, _ = loaded[ti]
        xh, r_xh = xb16.next()
        S.ins("act", "activation", reads=[r_xt_], writes=[r_xh], out=xh[:], in_=xt_[:], func=AF.Copy)
        S.op("pe", [("transpose", dict(out=pT[:, kk, :], in_=xh[:, kk * 128:(kk + 1) * 128], identity=k.ident_b[:]))
                    for kk in range(8)], reads=[r_xh, k.r_id], writes=[r_pT])
        S.ins("act", "activation", reads=[r_pT], writes=[r_xT_], out=xTt_[:, :, t_ * 128:(t_ + 1) * 128], in_=pT,
              func=AF.Copy)

    front(0)
    for g in range(T // 512):
        xTt, r_xT = xTg[g]
        gs, r_gs = gst.next()
        ns, r_ns = nst.next()
        for t in range(4):
            tok = g * 512 + t * 128
            pos = seq_pos(k, tok)
            prefetch(g * 4 + t + 2)
            xt, r_xt, ct, r_ct = loaded[g * 4 + t]
            p0, r_p0 = pSec.next()
            p0f = p0.rearrange("p a c -> p (a c)")
            items = []
            for (c0, c1) in ((0, 512), (512, 640)):
                for kk in range(8):
                    items.append(("matmul", dict(out=p0f[:, c0:c1], lhsT=xTt[:, kk, t * 128:(t + 1) * 128],
                                                 rhs=w[:, kk, 1024 + c0:1024 + c1], start=(kk == 0), stop=(kk == 7))))
            S.op("pe", items, reads=[r_xT, r_w], writes=[r_p0])
            front(g * 4 + t + 1)
            sqt, r_sq = sq.next()
            xb, r_xsb = xsb.next()
            ss, r_ss = sst.next()
            S.ins("act", "activation", reads=[r_p0], writes=[r_xsb], out=xb[:], in_=p0f[:, 0:640], func=AF.Copy)
            S.ins("dve", "tensor_tensor", reads=[r_xsb], writes=[r_sq], out=sqt[:], in0=xb[:], in1=xb[:], op=ALU.mult)
            S.ins("dve", "tensor_reduce", reads=[r_sq], writes=[r_ss], out=ss[:],
                  in_=sqt[:].rearrange("p (h d) -> p h d", d=64), axis=AX.X, op=ALU.add)
            S.ins("act", "activation", reads=[r_ss, k.r_c], writes=[r_ss], out=ss[:], in_=ss[:], func=AF.Sqrt,
                  bias=k.cst[:, 2:3], scale=1.0 / 64)

            def stage2(xb=xb, r_xsb=r_xsb, ss=ss, r_ss=r_ss, ct=ct, r_ct=r_ct, gs=gs, r_gs=r_gs, t=t, g=g):
                S.ins("dve", "reciprocal", reads=[r_ss], writes=[r_ss], out=ss[:], in_=ss[:])
                xv = xb[:].rearrange("p (h d) -> p h d", d=64)
                S.ins("dve", "tensor_tensor", reads=[r_xsb, r_ss], writes=[r_xsb], out=xv, in0=xv,
                      in1=ss[:].unsqueeze(2).to_broadcast([128, 10, 64]), op=ALU.mult)
                S.ins("dve", "tensor_tensor", reads=[r_xsb, k.r_c], writes=[r_xsb], out=xv, in0=xv, in1=k.G0[:], op=ALU.mult)
                x5 = xb[:].rearrange("p (h a b f) -> p h a b f", a=2, b=2, f=16)
                x1, x2 = x5[:, :, :, 0, :], x5[:, :, :, 1, :]
                c4 = ct[:].rearrange("p (s a f) -> p s a f", s=2, a=2)
                cosb = c4[:, 0:1, :, :].to_broadcast([128, 10, 2, 16])
                sinb = c4[:, 1:2, :, :].to_broadcast([128, 10, 2, 16])
                qk_t, r_qk = qkr.next()
                r_qa, r_qb = Res(), Res()
                o5 = qk_t[:].rearrange("p (h a b f) -> p h a b f", a=2, b=2, f=16)
                ta, r_ta = tmpA.next()
                tb, r_tb = tmpB.next()
                tva = [ta[:, i, :].rearrange("p (h a f) -> p h a f", a=2, f=16) for i in range(2)]
                tvb = [tb[:, i, :].rearrange("p (h a f) -> p h a f", a=2, f=16) for i in range(2)]
                S.ins("dve", "tensor_tensor", reads=[r_xsb, r_ct], writes=[r_ta], out=tva[0], in0=x1, in1=cosb, op=ALU.mult)
                S.ins("dve", "tensor_tensor", reads=[r_xsb, r_ct], writes=[r_ta], out=tva[1], in0=x2, in1=sinb, op=ALU.mult)
                S.ins("dve", "tensor_tensor", reads=[r_ta, r_qk], writes=[r_qa], out=o5[:, :, :, 0, :], in0=tva[0],
                      in1=tva[1], op=ALU.subtract)
                S.ins("pool", "tensor_tensor", reads=[r_xsb, r_ct], writes=[r_tb], out=tvb[0], in0=x1, in1=sinb, op=ALU.mult)
                S.ins("pool", "tensor_tensor", reads=[r_xsb, r_ct], writes=[r_tb], out=tvb[1], in0=x2, in1=cosb, op=ALU.mult)
                S.ins("pool", "tensor_tensor", reads=[r_tb, r_qk], writes=[r_qb], out=o5[:, :, :, 1, :], in0=tvb[0],
                      in1=tvb[1], op=ALU.add)

                def back():
                    S.op("pe", [("transpose", dict(out=pTq[:, j, :], in_=qk_t[:, j * 128:(j + 1) * 128], identity=k.ident_b[:]))
                                for j in range(5)], reads=[r_qa, r_qb, k.r_id], writes=[r_pTq])
                    S.ins("act", "activation", reads=[r_pTq, r_qa, r_qb], writes=[r_gs, r_qk], out=gs[:, :, t * 128:(t + 1) * 128],
                          in_=pTq[:, 0:5, :], func=AF.Copy)
                    if t == 3:
                        S.dma("sp", out=k.gT_d.rearrange("(c p) t -> p c t", p=128)[:, :, g * 512:(g + 1) * 512], in_=gs[:],
                              reads=[r_gs])
                q3.append(back)
            p1, r_p1 = pSec.next()
            p1f = p1.rearrange("p a c -> p (a c)")
            items = []
            for (c0, c1) in ((0, 512), (512, 640)):
                for kk in range(8):
                    items.append(("matmul", dict(out=p1f[:, c0:c1], lhsT=xTt[:, kk, t * 128:(t + 1) * 128],
                                                 rhs=w[:, kk, 1664 + c0:1664 + c1], start=(kk == 0), stop=(kk == 7))))
            S.op("pe", items, reads=[r_xT, r_w], writes=[r_p1])
            while q3:
                q3.pop(0)()
            while pending:
                pending.pop(0)()
            pending.append(stage2)
            vn, r_vn = vna.next()
            vgt, r_vg = vg.next()
            S.ins("dve", "tensor_copy", reads=[r_p1], writes=[r_vn], out=vn[:, :, 0:64],
                  in_=p1f[:, 0:512].rearrange("p (h d) -> p h d", d=64))
            S.ins("dve", "tensor_copy", reads=[r_p1], writes=[r_vg], out=vgt[:, :, 0:64],
                  in_=p1f[:, 512:640].rearrange("p (h d) -> p h d", d=64))
            S.dma("sp", out=k.vna_d[tok:tok + 128, :], in_=vn[:].rearrange("p h e -> p (h e)"), reads=[r_vn])
            S.dma("sp", out=k.vg_d[tok:tok + 128, :], in_=vgt[:].rearrange("p h e -> p (h e)"), reads=[r_vg])
        for oc in range(8):
            pFM, r_pFM = pFMr.next()
            S.op("pe", [("matmul", dict(out=pFM, lhsT=w[:, kk, oc * 128:(oc + 1) * 128], rhs=xTt[:, kk, :],
                                        start=(kk == 0), stop=(kk == 7))) for kk in range(8)],
                 reads=[r_xT, r_w], writes=[r_pFM])
            S.ins("act", "activation", reads=[r_pFM], writes=[r_ns], out=ns[:, oc, :], in_=pFM, func=AF.Copy)
        S.dma("sp", out=k.naT_d.rearrange("(c p) t -> p c t", p=128)[:, :, g * 512:(g + 1) * 512], in_=ns[:],
              reads=[r_ns])
    while pending or q3:
        q3_now = list(q3)
        del q3[:]
        for f in q3_now:
            f()
        if pending:
            pending.pop(0)()
    S.barrier()
    A.pop()


def phase_B0_na(k, tok0, L):
    S, A, ps = k.S, k.A, k.ps
    R = L // GRID_W
    NT = L // 128
    A.push()
    KT = A.alloc("KTna", [128, 4, L], BF16)
    QT = A.alloc("QTna", [128, 4, L], BF16)
    Ve = A.alloc("Ve", [128, NT, 520], BF16)
    Vo = A.alloc("Vo", [128, NT - 1, 520], BF16)
    MB = A.alloc("MBna", [128, 8, 14, 64], BF16)
    r_in, r_mb, r_v = Res(), Res(), Res()
    nav = k.naT_d.rearrange("(c p) t -> p c t", p=128)
    S.dma("sp", out=QT[:], in_=nav[:, 0:4, tok0:tok0 + L], writes=[r_in])
    S.dma("sp", out=KT[:], in_=nav[:, 4:8, tok0:tok0 + L], writes=[r_in])
    S.dma("sp", out=MB[:].rearrange("p h m q -> p (h m q)"), in_=k.mb_d, writes=[r_mb])
    S.dma("sp", out=Ve[:], in_=k.vna_d[tok0:tok0 + L, :].rearrange("(n p) f -> p n f", p=128), writes=[r_v])
    S.dma("sp", out=Vo[:], in_=k.vna_d[tok0 + 64:tok0 + 64 + (NT - 1) * 128, :].rearrange("(n p) f -> p n f", p=128),
          writes=[r_v])
    E = A.ring("Ena", 4, [128, 2, 256], BF16)
    Ost = A.ring("Ona", 2, [128, 512], BF16)
    rc = A.ring("rcna", 2, [128, 8], F32)
    pS = Ring([ps[:, 0:2, 0:256], ps[:, 2:4, 0:256]])
    pO = Ring([ps[:, 4:6, :], ps[:, 6:8, :]])
    steps = []
    for i in range(NT):
        tst = {"started": {}}
        for rr in (0, 1):
            r = 2 * i + rr
            r0 = min(max(r - 4, 0), R - 8)
            for hp in range(4):
                st = {}

                def qk(st=st, r=r, r0=r0, hp=hp):
                    st["ps"] = pS.next()
                    p_s, r_ps = st["ps"]
                    items = []
                    for a in (0, 1):
                        for c in range(4):
                            k0 = (r0 + 2 * c) * 64
                            items.append(("matmul", dict(out=p_s[:, a, c * 64:(c + 1) * 64],
                                                         lhsT=KT[a * 64:(a + 1) * 64, hp, k0:k0 + 128],
                                                         rhs=QT[a * 64:(a + 1) * 64, hp, r * 64:(r + 1) * 64],
                                                         start=True, stop=True, skip_group_check=True)))
                    S.op("pe", items, reads=[r_in], writes=[r_ps])

                def mid(st=st, r=r, r0=r0, hp=hp):
                    p_s, r_ps = st["ps"]
                    st["e"] = E.next()
                    e_t, r_e = st["e"]
                    S.ins("act", "activation", reads=[r_ps], writes=[r_e], out=e_t[:], in_=p_s, func=AF.Exp, scale=0.125)
                    m0 = r0 - r + 7
                    ev = e_t[:].rearrange("p a (c q) -> p a c q", q=64)
                    S.ins("dve", "tensor_tensor", reads=[r_e, r_mb], writes=[r_e], out=ev, in0=ev,
                          in1=MB[:, 2 * hp:2 * hp + 2, m0:m0 + 7:2, :], op=ALU.mult)

                def pv(st=st, tst=tst, rr=rr, r0=r0, hp=hp):
                    if rr == 0 and hp == 0:
                        tst["po"] = pO.next()
                    po, r_po = tst["po"]
                    e_t, r_e = st["e"]
                    Vt = Ve if r0 % 2 == 0 else Vo
                    kc0 = r0 // 2
                    items = []
                    for a in (0, 1):
                        h = 2 * hp + a
                        bank = h // 4
                        for c in range(4):
                            stt_ = not tst["started"].get((bank, rr), False)
                            tst["started"][(bank, rr)] = True
                            items.append(("matmul", dict(out=po[rr * 64:(rr + 1) * 64, bank, (h % 4) * 65:(h % 4 + 1) * 65],
                                                         lhsT=e_t[:, a, c * 64:(c + 1) * 64],
                                                         rhs=Vt[:, kc0 + c, h * 65:(h + 1) * 65],
                                                         start=stt_, stop=(c == 3), skip_group_check=True)))
                    S.op("pe", items, reads=[r_e, r_v], writes=[r_po])

                fin = None
                if rr == 1 and hp == 3:
                    def fin(tst=tst, i=i):
                        po, r_po = tst["po"]
                        pov = po[:, :, 0:260].rearrange("p b (h e) -> p b h e", e=65)
                        rct, r_rc = rc.next()
                        ot, r_ot = Ost.next()
                        rc4 = rct[:].rearrange("p (b h e) -> p b h e", b=2, e=1)
                        S.ins("dve", "reciprocal", reads=[r_po], writes=[r_rc], out=rc4, in_=pov[:, :, :, 64:65])
                        S.ins("dve", "tensor_tensor", reads=[r_po, r_rc], writes=[r_ot],
                              out=ot[:].rearrange("p (b h d) -> p b h d", b=2, d=64), in0=pov[:, :, :, 0:64],
                              in1=rc4.to_broadcast([128, 2, 4, 64]), op=ALU.mult)
                        t0 = tok0 + i * 128
                        S.dma("sp", out=k.ao_d[t0:t0 + 128, 0:512], in_=ot[:], reads=[r_ot])
                steps.append((qk, mid, pv, fin))
    run_pipeline(steps, skew=2)
    S.barrier()
    A.pop()


def qgroups(NT, gmax):
    out = []
    q = 0
    rem = NT
    while rem > 0:
        if rem > gmax + 1 or rem == gmax:
            n = gmax
        elif rem == gmax + 1 and gmax > 2:
            n = gmax - 1
        else:
            n = min(rem, gmax)
        out.append((q, n))
        q += n
        rem -= n
    return out


def run_pipeline(steps, skew=2):
    n = len(steps)
    for j in range(min(skew, n)):
        steps[j][0]()
    for i in range(n):
        steps[i][1]()
        if i + skew < n:
            steps[i + skew][0]()
        steps[i][2]()
        if steps[i][3] is not None:
            steps[i][3]()


def phase_B0_gqa(k, tok0, L):
    S, A, ps = k.S, k.A, k.ps
    NT = L // 128
    A.push()
    KT = A.alloc("KTg", [128, L], BF16)
    QT = A.alloc("QTg", [128, 4, L], BF16)
    V = A.alloc("Vg", [128, NT, 130], BF16)
    r_in, r_v = Res(), Res()
    gv = k.gT_d.rearrange("(c p) t -> p c t", p=128)
    S.dma("sp", out=QT[:], in_=gv[:, 0:4, tok0:tok0 + L], writes=[r_in])
    S.dma("sp", out=KT[:], in_=k.gT_d[512:640, tok0:tok0 + L], writes=[r_in])
    S.dma("sp", out=V[:], in_=k.vg_d[tok0:tok0 + L, :].rearrange("(n p) f -> p n f", p=128), writes=[r_v])
    E = A.ring("Eg", 3, [128, 2, 512], BF16)
    Ost = A.ring("Og", 2, [128, 4, 512], BF16)
    rc = A.ring("rcg", 2, [128, 2, 4], F32)
    pS = Ring([ps[:, 0:2, :], ps[:, 2:4, :]])
    pO = Ring([ps[:, 4:6, :], ps[:, 6:8, :]])
    r_out = Res()
    steps = []
    for g in range(L // 512):
        gst = {}
        for j in range(4):
            hst = {}
            for kc in range(NT):
                st = {}

                def qk(st=st, g=g, j=j, kc=kc):
                    st["ps"] = pS.next()
                    p_s, r_ps = st["ps"]
                    items = []
                    for a in (0, 1):
                        items.append(("matmul", dict(out=p_s[:, a, :], lhsT=KT[a * 64:(a + 1) * 64, kc * 128:(kc + 1) * 128],
                                                     rhs=QT[a * 64:(a + 1) * 64, j, g * 512:(g + 1) * 512],
                                                     start=True, stop=True, skip_group_check=True)))
                    S.op("pe", items, reads=[r_in], writes=[r_ps])

                def mid(st=st):
                    p_s, r_ps = st["ps"]
                    st["e"] = E.next()
                    e_t, r_e = st["e"]
                    S.ins("act", "activation", reads=[r_ps], writes=[r_e], out=e_t[:], in_=p_s, func=AF.Exp)

                def pv(st=st, hst=hst, kc=kc):
                    if kc == 0:
                        hst["po"] = pO.next()
                    po, r_po = hst["po"]
                    e_t, r_e = st["e"]
                    items = []
                    for a in (0, 1):
                        for qt in range(4):
                            items.append(("matmul", dict(out=po[:, a, qt * 65:(qt + 1) * 65],
                                                         lhsT=e_t[:, a, qt * 128:(qt + 1) * 128],
                                                         rhs=V[:, kc, a * 65:(a + 1) * 65],
                                                         start=(kc == 0 and qt == 0), stop=(kc == NT - 1),
                                                         skip_group_check=True)))
                    S.op("pe", items, reads=[r_e, r_v], writes=[r_po])

                fin = None
                if kc == NT - 1:
                    def fin(hst=hst, gst=gst, g=g, j=j):
                        if j == 0:
                            gst["ot"] = Ost.next()
                        ot, r_ot = gst["ot"]
                        po, r_po = hst["po"]
                        pov = po[:, :, 0:260].rearrange("p a (q e) -> p a q e", e=65)
                        rct, r_rc = rc.next()
                        S.ins("dve", "reciprocal", reads=[r_po], writes=[r_rc], out=rct[:].unsqueeze(3), in_=pov[:, :, :, 64:65])
                        for a in (0, 1):
                            h = j + 4 * a
                            S.ins("dve", "tensor_tensor", reads=[r_po, r_rc], writes=[r_ot], out=ot[:, :, h * 64:(h + 1) * 64],
                                  in0=pov[:, a, :, 0:64], in1=rct[:, a, :].unsqueeze(2).to_broadcast([128, 4, 64]), op=ALU.mult)
                        if j == 3:
                            t0 = tok0 + g * 512
                            S.dma("sp", out=k.ao_d[t0:t0 + 512, 512:1024].rearrange("(q p) f -> p q f", p=128), in_=ot[:],
                                  reads=[r_ot])
                steps.append((qk, mid, pv, fin))
    run_pipeline(steps, skew=2)
    S.barrier()
    A.pop()


def phase_C(k, layer, src, ao, dst):
    S, A, ps, T = k.S, k.A, k.ps, k.T
    NG = T // 512
    A.push()
    Wo = A.alloc("Wo", [128, 8, D], BF16)
    Wd = A.alloc("Wd", [128, NCH, D], BF16)
    lnp = [A.alloc(f"ln{i}", [128, D], F32) for i in range(4)]
    r_wo, r_wd, r_ln = Res(), Res(), Res()
    wo_b = k.w_out0_b if layer == 0 else k.w_out1_b
    S.dma("sp", out=Wo[:], in_=wo_b.rearrange("(k p) n -> p k n", p=128), reads=[k.r_w[f"w_out{layer}"]], writes=[r_wo])
    o_in = A.ring("o_in", 2, [128, D], BF16)
    oT = A.ring("oT", 2, [128, 8, 128], BF16)
    x_in = A.ring("x_in", 2, [128, D], F32)
    z = A.ring("z", 2, [128, D], F32)
    x1t = A.ring("x1t", 2, [128, D], F32)
    x1r = A.ring("x1r", 2, [128, D], F32)
    x1b = A.ring("x1b", 2, [128, D], BF16)
    x1T = A.ring("x1T", 2, [128, 8, 512], BF16)
    hT = A.alloc("hT", [128, NCH, 512], BF16)
    r_hT = Res()
    wgu = A.ring("wgu", 6, [128, 2, 1024], BF16)
    sg = A.ring("sg", 2, [128, 512], BF16)
    yo = A.ring("yo", 2, [128, D], F32)
    stt = [A.alloc("stats", [128, 2, 6], F32), A.alloc("mv", [128, 2], F32), A.alloc("rstd", [128, 1], F32),
           A.alloc("nmr", [128, 1], F32)]
    stt2 = [A.alloc("stats2", [128, 2, 6], F32), A.alloc("mv2", [128, 2], F32), A.alloc("rstd2", [128, 1], F32),
            A.alloc("nmr2", [128, 1], F32)]
    r_st, r_st2 = Res(), Res()
    pTb = Ring([ps[:, 0, :].bitcast(BF16).rearrange("p (a c) -> p a c", c=128),
                ps[:, 1, :].bitcast(BF16).rearrange("p (a c) -> p a c", c=128)])
    pOut, r_pOut = ps[:, 2:4, :], Res()
    pGU = Ring([ps[:, 4:6, :], ps[:, 6:8, :]])
    r_wsrc = [k.r_w[f"wg{layer}"], k.r_w[f"wu{layer}"]]
    r_x1d = [Res() for _ in range(T // 128)]

    wq = {}

    def load_w(idx):
        g_, c_ = divmod(idx, NCH)
        if g_ >= NG or idx in wq:
            return
        wt, r_wt = wgu.next()
        S.dma("sp", out=wt[:, 0, :], in_=k.wg_b[layer, c_], reads=[r_wsrc[0]], writes=[r_wt])
        S.dma("sp", out=wt[:, 1, :], in_=k.wu_b[layer, c_], reads=[r_wsrc[1]], writes=[r_wt])
        wq[idx] = (wt, r_wt)

    ld = {}

    def load_c1(ti):
        if ti >= T // 128 or ti in ld:
            return
        tok = ti * 128
        oi, r_oi = o_in.next()
        xi, r_xi = x_in.next()
        S.dma("sp", out=oi[:], in_=ao[tok:tok + 128, :], writes=[r_oi])
        S.dma("sp", out=xi[:], in_=src[tok:tok + 128, :], writes=[r_xi])
        ld[ti] = (oi, r_oi, xi, r_xi)

    xTs = {}
    c1st = {}
    deferred = []

    def flush_deferred():
        while deferred:
            deferred.pop(0)()

    def c1a(ti):
        g_, t = divmod(ti, 4)
        tok = ti * 128
        load_c1(ti)
        oi, r_oi, xi, r_xi = ld[ti]
        if t == 0:
            xTs[g_] = x1T.next()
        pt, r_pt = pTb.next()
        S.op("pe", [("transpose", dict(out=pt[:, kk, :], in_=oi[:, kk * 128:(kk + 1) * 128], identity=k.ident_b[:]))
                    for kk in range(8)], reads=[r_oi, k.r_id], writes=[r_pt])
        ott, r_oT = oT.next()
        S.ins("act", "activation", reads=[r_pt], writes=[r_oT], out=ott[:], in_=pt, func=AF.Copy)
        items = []
        for half in (0, 1):
            for kk in range(8):
                items.append(("matmul", dict(out=pOut[:, half, :], lhsT=ott[:, kk, :],
                                             rhs=Wo[:, kk, half * 512:(half + 1) * 512], start=(kk == 0), stop=(kk == 7))))
        S.op("pe", items, reads=[r_oT, r_wo], writes=[r_pOut])
        zt, r_z = z.next()
        S.ins("dve", "scalar_tensor_tensor", reads=[r_xi, r_pOut], writes=[r_z], out=zt[:], in0=xi[:], scalar=ALPHA,
              in1=pOut.rearrange("p a c -> p (a c)"), op0=ALU.mult, op1=ALU.add)
        xt1, r_xt1 = x1t.next()
        layer_norm_tile(k, zt, r_z, lnp[0], lnp[1], r_ln, xt1[:], r_xt1, stt, r_st)
        deferred.append(lambda: S.dma("sp", out=k.x1_d[tok:tok + 128, :], in_=xt1[:], reads=[r_xt1], writes=[r_x1d[ti]]))
        xb, r_xb = x1b.next()
        S.ins("pool", "tensor_copy", reads=[r_xt1], writes=[r_xb], out=xb[:], in_=xt1[:])
        c1st[ti] = (xb, r_xb)

    def c1b(ti):
        g_, t = divmod(ti, 4)
        flush_deferred()
        xb, r_xb = c1st.pop(ti)
        xT_t, r_xT = xTs[g_]
        pt, r_pt = pTb.next()
        S.op("pe", [("transpose", dict(out=pt[:, kk, :], in_=xb[:, kk * 128:(kk + 1) * 128], identity=k.ident_b[:]))
                    for kk in range(8)], reads=[r_xb, k.r_id], writes=[r_pt])
        S.ins("act", "activation", reads=[r_pt], writes=[r_xT], out=xT_t[:, :, t * 128:(t + 1) * 128], in_=pt, func=AF.Copy)

    load_c1(0)
    for i, src_v in enumerate((k.ln_mix_g, k.ln_mix_b, k.ln_ffn_g, k.ln_ffn_b)):
        S.dma("sp", out=lnp[i][:], in_=bcast_rows(src_v[layer]), writes=[r_ln])
    load_c1(1)
    for i in range(5):
        load_w(i)
    S.dma("sp", out=Wd[:], in_=k.wd_b[layer].rearrange("(c p) n -> p c n", p=128), reads=[k.r_w[f"wd{layer}"]],
          writes=[r_wd])
    for t in range(4):
        c1a(t)
        load_c1(t + 2)
        c1b(t)
    ydef = []
    A_AT = {1: 0, 6: 1, 11: 2, 16: 3}
    B_AT = {5: 0, 10: 1, 15: 2, 20: 3}
    for g in range(NG):
        xT_t, r_xT = xTs[g]
        for c in range(NCH):
            load_w(g * NCH + c + 5)
            wt, r_wt = wq.pop(g * NCH + c)
            pgu, r_pgu = pGU.next()
            items = []
            for m in (0, 1):
                for kk in range(8):
                    items.append(("matmul", dict(out=pgu[:, m, :], lhsT=wt[:, m, kk * 128:(kk + 1) * 128], rhs=xT_t[:, kk, :],
                                                 start=(kk == 0), stop=(kk == 7))))
            S.op("pe", items, reads=[r_wt, r_xT], writes=[r_pgu])
            sgt, r_sg = sg.next()
            S.ins("act", "activation", reads=[r_pgu], writes=[r_sg], out=sgt[:], in_=pgu[:, 0, :], func=AF.Silu)
            S.ins("dve", "tensor_tensor", reads=[r_sg, r_pgu], writes=[r_hT], out=hT[:, c, :], in0=sgt[:], in1=pgu[:, 1, :],
                  op=ALU.mult)
            if c == 2:
                while ydef:
                    ydef.pop(0)()
            if g + 1 < NG:
                if c in A_AT:
                    ti = (g + 1) * 4 + A_AT[c]
                    c1a(ti)
                    load_c1(ti + 2 if A_AT[c] < 2 else -1 + 10 ** 9)
                if c in B_AT:
                    c1b((g + 1) * 4 + B_AT[c])
                if c == 0:
                    load_c1((g + 1) * 4)
                    load_c1((g + 1) * 4 + 1)
        flush_deferred()
        xr = {}

        def load_x1r(t, g=g, xr=xr):
            ti = g * 4 + t
            xt_, r_xt_ = x1r.next()
            S.dma("sp", out=xt_[:], in_=k.x1_d[ti * 128:(ti + 1) * 128, :], reads=[r_x1d[ti]], writes=[r_xt_])
            xr[t] = (xt_, r_xt_)

        load_x1r(0)
        load_x1r(1)
        for t in range(4):
            tok = g * 512 + t * 128
            po, r_po = (pOut, r_pOut) if t % 2 == 0 else pGU.next()
            items = []
            for half in (0, 1):
                for c in range(NCH):
                    items.append(("matmul", dict(out=po[:, half, :], lhsT=hT[:, c, t * 128:(t + 1) * 128],
                                                 rhs=Wd[:, c, half * 512:(half + 1) * 512], start=(c == 0), stop=(c == NCH - 1))))
            S.op("pe", items, reads=[r_hT, r_wd], writes=[r_po])
            xt_, r_xt_ = xr[t]
            zt, r_z = z.next()
            S.ins("dve", "scalar_tensor_tensor", reads=[r_xt_, r_po], writes=[r_z], out=zt[:], in0=xt_[:],
                  scalar=ALPHA, in1=po.rearrange("p a c -> p (a c)"), op0=ALU.mult, op1=ALU.add)
            if t + 2 < 4:
                load_x1r(t + 2)
            yt, r_yt = yo.next()
            layer_norm_tile(k, zt, r_z, lnp[2], lnp[3], r_ln, yt[:], r_yt, stt2, r_st2)
            ydef.append(lambda tok=tok, yt=yt, r_yt=r_yt: S.dma("sp", out=dst[tok:tok + 128, :], in_=yt[:], reads=[r_yt]))
            if len(ydef) > 1:
                ydef.pop(0)()
    while ydef:
        ydef.pop(0)()
    S.barrier()
    A.pop()


def phase_A1(k):
    S, A, ps, T = k.S, k.A, k.ps, k.T
    A.push()
    w = A.alloc("w_in1", [128, 8, IN1_W], BF16)
    r_w = Res()
    S.dma("sp", out=w[:], in_=k.w_in1_b.rearrange("(k p) n -> p k n", p=128), reads=[k.r_w["w_in1"]], writes=[r_w])
    xs = A.ring("xs", 4, [128, D], F32)
    cs = A.ring("cs", 6, [128, 64], F32)
    xT = A.ring("xT", 2, [128, 8, 128], BF16)
    xsb = A.ring("xsb", 6, [128, D], F32)
    tmpA = A.ring("tmpA", 2, [128, 2, 512], F32)
    tmpB = A.ring("tmpB", 2, [128, 2, 512], F32)
    qkr = A.ring("qkr", 7, [128, D], BF16)
    vst = A.ring("vst", 2, [128, 8, 129], BF16)
    stq = A.ring("stq", 2, [128, 8, 512], BF16)
    stk = A.ring("stk", 2, [128, 8, 512], BF16)
    for t_, r_ in zip(vst.tiles, vst.res):
        S.ins("pool", "memset", writes=[r_], ap=t_[:, :, 128:129], constant=1.0)
    pT = ps[:, 0, :].bitcast(BF16).rearrange("p (a c) -> p a c", c=128)
    r_pT = Res()
    xb16 = A.ring("xb16", 2, [128, D], BF16)
    pSec = Ring([ps[:, 2:4, :], ps[:, 4:6, :]])
    pTq = Ring([ps[:, 6, :].bitcast(BF16).rearrange("p (a c) -> p a c", c=128),
                ps[:, 7, :].bitcast(BF16).rearrange("p (a c) -> p a c", c=128)])
    pending = []
    q3 = []

    def flush():
        while pending:
            pending.pop(0)()

    loaded = {}
    NTILE = T // 128

    def prefetch(upto):
        for ti in range(len(loaded), min(upto + 1, NTILE)):
            tok_ = ti * 128
            pos_ = seq_pos(k, tok_)
            xt_, r_xt_ = xs.next()
            ct_, r_ct_ = cs.next()
            S.dma("sp", out=xt_[:], in_=k.x2_d[tok_:tok_ + 128, :], writes=[r_xt_])
            S.dma("sp", out=ct_[:], in_=k.rope_seq[pos_:pos_ + 128, :], writes=[r_ct_])
            loaded[ti] = (xt_, r_xt_, ct_, r_ct_)

    fronts = {}

    def front(ti):
        if ti >= NTILE:
            return
        prefetch(ti)
        xt_, r_xt_, _, _ = loaded[ti]
        xh, r_xh = xb16.next()
        S.ins("act", "activation", reads=[r_xt_], writes=[r_xh], out=xh[:], in_=xt_[:], func=AF.Copy)
        S.op("pe", [("transpose", dict(out=pT[:, kk, :], in_=xh[:, kk * 128:(kk + 1) * 128], identity=k.ident_b[:]))
                    for kk in range(8)], reads=[r_xh, k.r_id], writes=[r_pT])
        xTt_, r_xT_ = xT.next()
        S.ins("act", "activation", reads=[r_pT], writes=[r_xT_], out=xTt_[:], in_=pT, func=AF.Copy)
        fronts[ti] = (xTt_, r_xT_)

    for g in range(T // 512):
        sq_t, r_sq = stq.next()
        sk_t, r_sk = stk.next()
        for t in range(4):
            tok = g * 512 + t * 128
            pos = seq_pos(k, tok)
            prefetch(g * 4 + t + 2)
            if g * 4 + t == 0:
                front(0)
            xt, r_xt, ct, r_ct = loaded[g * 4 + t]
            xTt, r_xT = fronts.pop(g * 4 + t)
            backs = []
            for sec in range(3):
                p0, r_p0 = pSec.next()
                p0f = p0.rearrange("p a c -> p (a c)")
                items = []
                for half in (0, 1):
                    for kk in range(8):
                        c0 = sec * 1024 + half * 512
                        items.append(("matmul", dict(out=p0[:, half, :], lhsT=xTt[:, kk, :], rhs=w[:, kk, c0:c0 + 512],
                                                     start=(kk == 0), stop=(kk == 7))))
                S.op("pe", items, reads=[r_xT, r_w], writes=[r_p0])
                if sec == 0:
                    front(g * 4 + t + 1)
                if sec == 2:
                    vt, r_vt = vst.next()
                    S.ins("dve", "tensor_copy", reads=[r_p0], writes=[r_vt], out=vt[:, :, 0:128],
                          in_=p0f.rearrange("p (h d) -> p h d", d=128))
                    S.dma("sp", out=k.v1_d[tok:tok + 128, :], in_=vt[:].rearrange("p h e -> p (h e)"), reads=[r_vt])
                    continue
                xb, r_xb = xsb.next()
                S.ins("act", "activation", reads=[r_p0], writes=[r_xb], out=xb[:], in_=p0f, func=AF.Copy)
                def stage2(sec=sec, xb=xb, r_xb=r_xb, ct=ct, r_ct=r_ct, sq_t=sq_t, r_sq=r_sq, sk_t=sk_t, r_sk=r_sk, t=t, g=g):
                    x4 = xb[:].rearrange("p (h b f) -> p h b f", b=2, f=32)
                    x1, x2 = x4[:, :, 0, :], x4[:, :, 1, :]
                    c3 = ct[:].rearrange("p (s f) -> p s f", s=2)
                    cosb = c3[:, 0:1, :].to_broadcast([128, 16, 32])
                    sinb = c3[:, 1:2, :].to_broadcast([128, 16, 32])
                    qk_t, r_qk = qkr.next()
                    r_qa, r_qb = Res(), Res()
                    o4 = qk_t[:].rearrange("p (h b f) -> p h b f", b=2, f=32)
                    ta, r_ta = tmpA.next()
                    tb, r_tb = tmpB.next()
                    tva = [ta[:, i, :].rearrange("p (h f) -> p h f", f=32) for i in range(2)]
                    tvb = [tb[:, i, :].rearrange("p (h f) -> p h f", f=32) for i in range(2)]
                    S.ins("dve", "tensor_tensor", reads=[r_xb, r_ct], writes=[r_ta], out=tva[0], in0=x1, in1=cosb, op=ALU.mult)
                    S.ins("dve", "tensor_tensor", reads=[r_xb, r_ct], writes=[r_ta], out=tva[1], in0=x2, in1=sinb, op=ALU.mult)
                    S.ins("dve", "tensor_tensor", reads=[r_ta, r_qk], writes=[r_qa], out=o4[:, :, 0, :], in0=tva[0],
                          in1=tva[1], op=ALU.subtract)
                    S.ins("pool", "tensor_tensor", reads=[r_xb, r_ct], writes=[r_tb], out=tvb[0], in0=x1, in1=sinb, op=ALU.mult)
                    S.ins("pool", "tensor_tensor", reads=[r_xb, r_ct], writes=[r_tb], out=tvb[1], in0=x2, in1=cosb, op=ALU.mult)
                    S.ins("pool", "tensor_tensor", reads=[r_tb, r_qk], writes=[r_qb], out=o4[:, :, 1, :], in0=tvb[0],
                          in1=tvb[1], op=ALU.add)

                    def back():
                        pq, r_pq = pTq.next()
                        S.op("pe", [("transpose", dict(out=pq[:, j, :], in_=qk_t[:, j * 128:(j + 1) * 128], identity=k.ident_b[:]))
                                    for j in range(8)], reads=[r_qa, r_qb, k.r_id], writes=[r_pq])
                        dst_t, r_dst = (sq_t, r_sq) if sec == 0 else (sk_t, r_sk)
                        S.ins("act", "activation", reads=[r_pq, r_qa, r_qb], writes=[r_dst, r_qk],
                              out=dst_t[:, :, t * 128:(t + 1) * 128], in_=pq, func=AF.Copy)
                        if t == 3:
                            dd = k.qT_d if sec == 0 else k.kT_d
                            S.dma("sp", out=dd.rearrange("(c p) t -> p c t", p=128)[:, :, g * 512:(g + 1) * 512], in_=dst_t[:],
                                  reads=[r_dst])
                    q3.append(back)
                backs.append(stage2)
            while q3:
                q3.pop(0)()
            while pending:
                pending.pop(0)()
            pending.extend(backs)
    while pending or q3:
        q3_now = list(q3)
        del q3[:]
        for f in q3_now:
            f()
        while pending:
            pending.pop(0)()
    S.barrier()
    A.pop()


def phase_B1(k):
    S, A, ps = k.S, k.A, k.ps
    Lmax = max(k.seqs)
    NTmax = Lmax // 128
    qv = k.qT_d.rearrange("(c p) t -> p c t", p=128)
    kv = k.kT_d.rearrange("(c p) t -> p c t", p=128)
    A.push()
    sets = []
    for i in range(2):
        sets.append((A.alloc("KTd", [128, 2, Lmax], BF16), A.alloc("QTd", [128, 2, Lmax], BF16),
                     A.alloc("Vd", [128, NTmax, 258], BF16), Res()))
    E = A.ring("Ed", 3, [128, 2, 384], BF16)
    Ost = A.ring("Od", 2, [128, 3, 256], BF16)
    rc = A.ring("rcd", 2, [128, 2, 3], F32)
    t1 = A.ring("t1d", 2, [128, 3, 128], F32)
    t2 = A.ring("t2d", 2, [128, 3, 128], F32)
    ssd = A.ring("ssd", 2, [128, 3], F32)
    pS = Ring([ps[:, 0:2, :], ps[:, 2:4, :]])
    pO = Ring([ps[:, 4:6, :], ps[:, 6:8, :]])
    passes = []
    tok0 = 0
    for L in k.seqs:
        for hp in range(4):
            passes.append((tok0, L, hp))
        tok0 += L

    def load(p):
        if p >= len(passes):
            return
        tok0, L, hp = passes[p]
        KT, QT, V, r_in = sets[p % 2]
        NT = L // 128
        S.dma("sp", out=QT[:, :, 0:L], in_=qv[:, 2 * hp:2 * hp + 2, tok0:tok0 + L], writes=[r_in])
        S.dma("sp", out=KT[:, :, 0:L], in_=kv[:, 2 * hp:2 * hp + 2, tok0:tok0 + L], writes=[r_in])
        S.dma("sp", out=V[:, 0:NT, :], in_=k.v1_d[tok0:tok0 + L, hp * 258:(hp + 1) * 258].rearrange("(n p) f -> p n f", p=128),
              writes=[r_in])

    load(0)
    steps = []
    for p, (tok0, L, hp) in enumerate(passes):
        KT, QT, V, r_in = sets[p % 2]
        NT = L // 128
        first = True
        for (q0, nq) in qgroups(NT, 3):
            gst = {}
            NQ = nq * 128
            for hl in range(2):
                hst = {}
                for kc in range(NT):
                    st = {}

                    def qk(st=st, q0=q0, NQ=NQ, hl=hl, kc=kc, KT=KT, QT=QT, r_in=r_in):
                        st["ps"] = pS.next()
                        p_s, r_ps = st["ps"]
                        items = []
                        for a in (0, 1):
                            items.append(("matmul", dict(out=p_s[:, a, 0:NQ], lhsT=KT[a * 64:(a + 1) * 64, hl, kc * 128:(kc + 1) * 128],
                                                         rhs=QT[a * 64:(a + 1) * 64, hl, q0 * 128:q0 * 128 + NQ],
                                                         start=True, stop=True, skip_group_check=True)))
                        S.op("pe", items, reads=[r_in], writes=[r_ps])

                    def mid(st=st, NQ=NQ, pre=(p + 1 if first else None)):
                        if pre is not None:
                            load(pre)
                        p_s, r_ps = st["ps"]
                        st["e"] = E.next()
                        e_t, r_e = st["e"]
                        S.ins("act", "activation", reads=[r_ps], writes=[r_e], out=e_t[:, :, 0:NQ], in_=p_s[:, :, 0:NQ],
                              func=AF.Exp, scale=0.125)

                    first = False

                    def pv(st=st, hst=hst, kc=kc, nq=nq, hl=hl, V=V, r_in=r_in, NT=NT):
                        if kc == 0:
                            hst["po"] = pO.next()
                        po, r_po = hst["po"]
                        e_t, r_e = st["e"]
                        items = []
                        for a in (0, 1):
                            for qt in range(nq):
                                items.append(("matmul", dict(out=po[:, a, qt * 129:(qt + 1) * 129],
                                                             lhsT=e_t[:, a, qt * 128:(qt + 1) * 128],
                                                             rhs=V[:, kc, hl * 129:(hl + 1) * 129],
                                                             start=(kc == 0 and qt == 0), stop=(kc == NT - 1),
                                                             skip_group_check=True)))
                        S.op("pe", items, reads=[r_e, r_in], writes=[r_po])

                    fin = None
                    if kc == NT - 1:
                        def fin(hst=hst, gst=gst, q0=q0, nq=nq, NQ=NQ, hl=hl, tok0=tok0, hp=hp):
                            if hl == 0:
                                gst["ot"] = Ost.next()
                            ot, r_ot = gst["ot"]
                            po, r_po = hst["po"]
                            pov = po[:, :, 0:nq * 129].rearrange("p a (q e) -> p a q e", e=129)
                            rct, r_rc = rc.next()
                            S.ins("dve", "reciprocal", reads=[r_po], writes=[r_rc], out=rct[:, :, 0:nq].unsqueeze(3), in_=pov[:, :, :, 128:129])
                            S.ins("dve", "tensor_scalar", reads=[r_rc, k.r_c], writes=[r_rc], out=rct[:, 1, 0:nq], in0=rct[:, 1, 0:nq],
                                  scalar1=k.lam[:, 1:2], scalar2=None, op0=ALU.mult)
                            a1, r_a1 = t1.next()
                            a2, r_a2 = t2.next()
                            S.ins("dve", "tensor_tensor", reads=[r_po, r_rc], writes=[r_a1], out=a1[:, 0:nq, :], in0=pov[:, 0, :, 0:128],
                                  in1=rct[:, 0, 0:nq].unsqueeze(2).to_broadcast([128, nq, 128]), op=ALU.mult)
                            S.ins("dve", "tensor_tensor", reads=[r_po, r_rc], writes=[r_a2], out=a2[:, 0:nq, :], in0=pov[:, 1, :, 0:128],
                                  in1=rct[:, 1, 0:nq].unsqueeze(2).to_broadcast([128, nq, 128]), op=ALU.mult)
                            S.ins("pool", "tensor_tensor", reads=[r_a1, r_a2], writes=[r_a1], out=a1[:, 0:nq, :], in0=a1[:, 0:nq, :],
                                  in1=a2[:, 0:nq, :], op=ALU.add)
                            S.ins("pool", "tensor_tensor", reads=[r_a1], writes=[r_a2], out=a2[:, 0:nq, :], in0=a1[:, 0:nq, :],
                                  in1=a1[:, 0:nq, :], op=ALU.mult)
                            sst, r_ss = ssd.next()
                            S.ins("dve", "tensor_reduce", reads=[r_a2], writes=[r_ss], out=sst[:, 0:nq], in_=a2[:, 0:nq, :], axis=AX.X,
                                  op=ALU.add)
                            rsqrt_pool(k, sst[:, 0:nq], sst[:, 0:nq], 1.0 / 128, SUBLN_EPS, [r_ss], [r_ss])
                            S.ins("dve", "tensor_tensor", reads=[r_a1, r_ss], writes=[r_a1], out=a1[:, 0:nq, :], in0=a1[:, 0:nq, :],
                                  in1=sst[:, 0:nq].unsqueeze(2).to_broadcast([128, nq, 128]), op=ALU.mult)
                            S.ins("pool", "tensor_tensor", reads=[r_a1, k.r_c], writes=[r_ot], out=ot[:, 0:nq, hl * 128:(hl + 1) * 128],
                                  in0=a1[:, 0:nq, :], in1=k.G1[:].unsqueeze(1).to_broadcast([128, nq, 128]), op=ALU.mult)
                            if hl == 1:
                                t0 = tok0 + q0 * 128
                                S.dma("sp", out=k.ao1_d[t0:t0 + NQ, hp * 256:(hp + 1) * 256].rearrange("(q p) f -> p q f", p=128),
                                      in_=ot[:, 0:nq, :], reads=[r_ot])
                    steps.append((qk, mid, pv, fin))
    run_pipeline(steps, skew=2)
    S.barrier()
    A.pop()


def rope_tables():
    theta = np.float32(10000.0)
    t = np.arange(4096)

    def cs(pos, dim):
        inv = (theta ** (-(np.arange(0, dim, 2, dtype=np.float32)) / np.float32(dim))).astype(np.float32)
        ang = pos.astype(np.float32)[:, None] * inv[None, :]
        return np.cos(ang).astype(np.float32), np.sin(ang).astype(np.float32)

    rc, rs = cs(t // GRID_W, 32)
    cc, cs_ = cs(t % GRID_W, 32)
    ax = np.stack([np.stack([rc, cc], 1), np.stack([rs, cs_], 1)], 1)
    sc, ss = cs(t, 64)
    sq = np.stack([sc, ss], 1)
    return np.ascontiguousarray(ax.reshape(4096, 64)), np.ascontiguousarray(sq.reshape(4096, 64))


def host_weights(inp):
    f = lambda a: np.ascontiguousarray(np.asarray(a, dtype=np.float32))
    w0 = f(inp["w_in_mix0"][0])
    na_q, na_k, na_v = w0[:, 0:512], w0[:, 512:1024], w0[:, 1024:1536]
    gq, gk, gv = w0[:, 1536:2048], w0[:, 2048:2176], w0[:, 2176:2304]
    order = [0, 4, 1, 5, 2, 6, 3, 7]
    gqp = np.concatenate([gq[:, h * 64:(h + 1) * 64] for h in order], 1)
    w_in0 = f(np.concatenate([na_q, na_k, gqp, gk, na_v, gv], 1))
    wg = f(inp["w_ffn_gate"]).reshape(2, 8, 128, NCH, 128).transpose(0, 3, 2, 1, 4).reshape(2, NCH, 128, 1024)
    wu = f(inp["w_ffn_up"]).reshape(2, 8, 128, NCH, 128).transpose(0, 3, 2, 1, 4).reshape(2, NCH, 128, 1024)
    rpb = f(inp["rpb_na"][0])
    kc = np.arange(64)[:, None]
    qc = np.arange(64)[None, :]
    c0 = np.clip(qc - 8, 0, GRID_W - 16)
    valid = (kc >= c0) & (kc < c0 + 16)
    idx = np.clip(kc - qc + 15, 0, 30)
    tpm = np.where(valid[None, None], rpb[:, :, idx], np.float32(NEG)).astype(np.float32)
    ax, sq = rope_tables()
    return dict(
        w_in0=w_in0, w_out0=f(inp["w_out_mix0"][0]), w_in1=f(inp["w_in_mix1"][0]), w_out1=f(inp["w_out_mix1"][0]),
        wg=f(wg), wu=f(wu), wd=f(inp["w_ffn_down"]), tpm=f(tpm),
        g_q=f(inp["g_q_gqa"][0]), g_k=f(inp["g_k_gqa"][0]), lq1=f(inp["lam_q1"][0]), lk1=f(inp["lam_k1"][0]),
        lq2=f(inp["lam_q2"][0]), lk2=f(inp["lam_k2"][0]), g_sub=f(inp["g_subln"][0]),
        ln_mix_g=f(inp["ln_mix_g"]), ln_mix_b=f(inp["ln_mix_b"]), ln_ffn_g=f(inp["ln_ffn_g"]), ln_ffn_b=f(inp["ln_ffn_b"]),
        ident=np.eye(128, dtype=np.float32), rope_ax=ax, rope_seq=sq,
    )


_CACHE = {}


def kernel(**inputs):
    xp = np.asarray(inputs["x_prompt"], dtype=np.float32)
    xs = np.asarray(inputs["x_sample"], dtype=np.float32)
    hw = host_weights(inputs)
    if "nc" not in _CACHE:
        _CACHE["nc"] = build(SEQS)[0]
    nc = _CACHE["nc"]
    in_maps = []
    for c in range(N_CORES):
        xc = np.concatenate([xp[2 * c], xp[2 * c + 1], xs[c]], 0)
        m = dict(hw)
        m["x"] = np.ascontiguousarray(xc)
        in_maps.append(m)
    res = run_bass_kernel_spmd(nc, in_maps, core_ids=list(range(N_CORES)))
    yp = np.empty_like(xp)
    ys = np.empty_like(xs)
    for c in range(N_CORES):
        y = np.asarray(res.results[c]["y"], dtype=np.float32)
        yp[2 * c] = y[0:2048]
        yp[2 * c + 1] = y[2048:4096]
        ys[c] = y[4096:8192]
    return (yp, ys)
```

```python
import math
from contextlib import ExitStack

import numpy as np
import concourse.bass as bass
import concourse.mybir as mybir
from concourse.bass_utils import run_bass_kernel_spmd

F32 = mybir.dt.float32
BF16 = mybir.dt.bfloat16
U8 = mybir.dt.uint8
AF = mybir.ActivationFunctionType
ALU = mybir.AluOpType
AX = mybir.AxisListType

D = 1024
DFF = 2816
NCH = DFF // 128
GRID_W = 64
IN0_W = 2304
IN1_W = 3072
ALPHA = (2.0 * 2) ** 0.25
LAMBDA_INIT = 0.8 - 0.6 * math.exp(-0.3 * 1)
LN_EPS = 1e-5
RMS_EPS = 1e-6
SUBLN_EPS = 1e-5
NEG = -30000.0
N_CORES = 8
SEQS = (2048, 2048, 4096)

COMPUTE = ("pe", "act", "dve", "pool")


class Res:
    __slots__ = ("name", "w", "r")

    def __init__(self, name=""):
        self.name = name
        self.w = None
        self.r = {}


class Sched:
    def __init__(self, nc, ring=12):
        self.nc = nc
        self.engs = ("pe", "act", "dve", "pool", "sp")
        self.streams = {e: [] for e in self.engs}
        self.cnt = {}
        self.seen = {e: {} for e in self.engs}
        self.ring = ring
        self.dma_k = {e: 0 for e in self.engs}
        self.semkeys = list(COMPUTE)
        for e in ("sp", "pool"):
            for i in range(ring):
                self.semkeys.append(("d", e, i))
        for k in self.semkeys:
            self.cnt[k] = 0
        self.n_ins = 0
        self.n_wait = 0

    def _need(self, eng, tickets):
        seen = self.seen[eng]
        best = {}
        for (k, v) in tickets:
            if v <= seen.get(k, 0):
                continue
            if v > best.get(k, 0):
                best[k] = v
        out = []
        for k, v in best.items():
            seen[k] = v
            out.append((k, v))
        return out

    def op(self, eng, items, reads=(), writes=(), dma=False):
        tickets = []
        for r in reads:
            if r.w is not None:
                if dma or r.w[0] != eng or eng != "pe":
                    tickets.append(r.w)
        for w in writes:
            if w.w is not None and (dma or w.w[0] != eng):
                tickets.append(w.w)
            for k, v in w.r.items():
                if dma or k != eng:
                    tickets.append((k, v))
        if dma:
            i = self.dma_k[eng]
            self.dma_k[eng] = i + 1
            key = ("d", eng, i % self.ring)
            if self.cnt[key] > 0:
                tickets.append((key, self.cnt[key]))
            self.cnt[key] += 16
            ticket = (key, self.cnt[key])
            inc = (key, 16)
        else:
            self.cnt[eng] += 1
            ticket = (eng, self.cnt[eng])
            inc = (eng, 1)
        waits = self._need(eng, tickets)
        self.n_wait += len(waits)
        self.n_ins += len(items)
        self.streams[eng].append((waits, items, inc))
        k, v = ticket
        for r in reads:
            if v > r.r.get(k, 0):
                r.r[k] = v
        for w in writes:
            w.w = ticket
            w.r = {}
        return ticket

    def ins(self, eng, name, reads=(), writes=(), **kw):
        return self.op(eng, [(name, kw)], reads, writes)

    def dma(self, eng, out, in_, reads=(), writes=()):
        return self.op(eng, [("dma_start", dict(out=out, in_=in_))], reads, writes, dma=True)

    def barrier(self, final=False):
        tickets = [(k, v) for k, v in self.cnt.items() if v > 0 and (final or not (isinstance(k, tuple) and k[1] == "pool"))]
        for e in self.engs:
            waits = self._need(e, [t for t in tickets if t[0] != e])
            self.streams[e].append((waits, None, None))

    def emit(self, sems, block):
        streams = self.streams

        def body_for(ename):
            def body(e):
                for waits, items, inc in streams[ename]:
                    for (k, v) in waits:
                        e.wait_ge(sems[k], v)
                    if items is None:
                        continue
                    ins = None
                    for name, kw in items:
                        ins = getattr(e, name)(**kw)
                    ins.then_inc(sems[inc[0]], inc[1])
            return body

        block.tensor(body_for("pe"))
        block.scalar(body_for("act"))
        block.vector(body_for("dve"))
        block.gpsimd(body_for("pool"))
        block.sync(body_for("sp"))


class Ring:
    def __init__(self, tiles):
        self.tiles = tiles
        self.res = [Res() for _ in tiles]
        self.i = 0

    def next(self):
        n = len(self.tiles)
        t, r = self.tiles[self.i % n], self.res[self.i % n]
        self.i += 1
        return t, r


class Arena:
    def __init__(self, nc, base, size):
        self.nc, self.base, self.size = nc, base, size
        self.top = 0
        self.marks = []
        self.uid = 0
        self.peak = 0

    def alloc(self, name, shape, dt):
        esz = {F32: 4, BF16: 2, U8: 1}[dt]
        nbytes = esz
        for s in shape[1:]:
            nbytes *= s
        off = (self.top + 31) // 32 * 32
        self.uid += 1
        t = self.nc.alloc_sbuf_tensor_at(f"{name}_{self.uid}", list(shape), dt, offset=self.base + off)
        self.top = off + nbytes
        self.peak = max(self.peak, self.top)
        assert self.top <= self.size, (name, self.top, self.size)
        return t

    def ring(self, name, n, shape, dt):
        return Ring([self.alloc(f"{name}{i}", shape, dt) for i in range(n)])

    def push(self):
        self.marks.append(self.top)

    def pop(self):
        self.top = self.marks.pop()


class K:
    pass


def build(seqs=SEQS, dbg=False):
    T = sum(seqs)
    assert T % 512 == 0
    nc = bass.Bass("TRN2", target_bir_lowering=False)
    k = K()
    k.nc = nc
    k.T = T
    k.seqs = seqs

    def din(name, shape, dt=F32):
        return nc.dram_tensor(name, list(shape), dt, kind="ExternalInput").ap()

    def dscr(name, shape, dt):
        kind = "ExternalOutput" if dbg else "Internal"
        return nc.dram_tensor(name, list(shape), dt, kind=kind).ap()

    k.x = din("x", [T, D])
    k.w_in0 = din("w_in0", [D, IN0_W])
    k.w_out0 = din("w_out0", [D, D])
    k.w_in1 = din("w_in1", [D, IN1_W])
    k.w_out1 = din("w_out1", [D, D])
    k.wg = din("wg", [2, NCH, 128, 8 * 128])
    k.wu = din("wu", [2, NCH, 128, 8 * 128])
    k.wd = din("wd", [2, DFF, D])
    k.tpm = din("tpm", [8, 15, 64, 64])
    k.g_q = din("g_q", [64])
    k.g_k = din("g_k", [64])
    k.lq1 = din("lq1", [64])
    k.lk1 = din("lk1", [64])
    k.lq2 = din("lq2", [64])
    k.lk2 = din("lk2", [64])
    k.g_sub = din("g_sub", [128])
    k.ln_mix_g = din("ln_mix_g", [2, D])
    k.ln_mix_b = din("ln_mix_b", [2, D])
    k.ln_ffn_g = din("ln_ffn_g", [2, D])
    k.ln_ffn_b = din("ln_ffn_b", [2, D])
    k.ident = din("ident", [128, 128])
    k.rope_ax = din("rope_ax", [4096, 64])
    k.rope_seq = din("rope_seq", [4096, 64])
    k.y = nc.dram_tensor("y", [T, D], F32, kind="ExternalOutput").ap()

    k.w_in0_b = dscr("w_in0_b", [D, IN0_W], BF16)
    k.w_out0_b = dscr("w_out0_b", [D, D], BF16)
    k.w_in1_b = dscr("w_in1_b", [D, IN1_W], BF16)
    k.w_out1_b = dscr("w_out1_b", [D, D], BF16)
    k.wg_b = dscr("wg_b", [2, NCH, 128, 1024], BF16)
    k.wu_b = dscr("wu_b", [2, NCH, 128, 1024], BF16)
    k.wd_b = dscr("wd_b", [2, DFF, D], BF16)
    k.naT_d = dscr("naT_d", [1024, T], BF16)
    k.gT_d = dscr("gT_d", [640, T], BF16)
    k.vna_d = dscr("vna_d", [T, 520], BF16)
    k.vg_d = dscr("vg_d", [T, 130], BF16)
    k.ao_d = dscr("ao_d", [T, D], BF16)
    k.x2_d = dscr("x2_d", [T, D], F32)
    k.qT_d = dscr("qT_d", [1024, T], BF16)
    k.kT_d = dscr("kT_d", [1024, T], BF16)
    k.v1_d = dscr("v1_d", [T, 1032], BF16)
    k.ao1_d = dscr("ao1_d", [T, D], BF16)
    k.mb_d = dscr("mb_d", [128, 8 * 14 * 64], BF16)
    k.x1_d = dscr("x1_d", [T, D], F32)
    k.dbg = dbg
    if dbg:
        k.dbg_x1 = dscr("dbg_x1", [2, T, D], F32)
        k.dbg_z = dscr("dbg_z", [2, T, D], F32)
        k.dbg_h = dscr("dbg_h", [2, DFF, T], BF16)
        k.dbg_s = dscr("dbg_s", [4, T, 1], F32)
        k.dbg_zn = dscr("dbg_zn", [T, D], F32)
    k.dbg_tok = None

    S = Sched(nc)
    k.S = S
    with ExitStack() as es:
        ARENA = 207000
        arena_t = es.enter_context(nc.sbuf_tensor("arena", [128, ARENA], U8))
        base = nc.sbuf_base - ARENA
        A = Arena(nc, base, ARENA)
        k.A = A
        k.ps = es.enter_context(nc.psum_tensor("ps", [128, 8, 512], F32))
        sems = {key: es.enter_context(nc.semaphore(f"s{i}")) for i, key in enumerate(S.semkeys)}
        block = es.enter_context(nc.Block())

        setup(k)
        phase_A0(k)
        A.pop()
        cast_late(k)
        tok0 = 0
        for L in seqs:
            phase_B0_na(k, tok0, L)
            phase_B0_gqa(k, tok0, L)
            tok0 += L
        phase_C(k, 0, k.x, k.ao_d, k.x2_d)
        phase_A1(k)
        phase_B1(k)
        phase_C(k, 1, k.x2_d, k.ao1_d, k.y)
        S.barrier(final=True)
        S.emit(sems, block)
    k.stats = dict(n_ins=S.n_ins, n_wait=S.n_wait, peak=A.peak)
    return nc, k


def cast_weights(k, casts):
    S = k.S
    for name, src, dst in casts:
        r = Res(name)
        k.r_w[name] = r
        s2 = src if len(src.shape) == 2 else src.rearrange("c p n -> (c p) n")
        d2 = dst if len(dst.shape) == 2 else dst.rearrange("c p n -> (c p) n")
        if s2.shape[1] > 2048:
            half = s2.shape[1] // 2
            s2 = s2.rearrange("r (a n) -> (r a) n", n=half)
            d2 = d2.rearrange("r (a n) -> (r a) n", n=half)
        rows = s2.shape[0]
        step = 1024
        for r0 in range(0, rows, step):
            r1 = min(rows, r0 + step)
            chain = k.cast_chain[k.cast_i % 4]
            k.cast_i += 1
            S.dma("pool", out=d2[r0:r1], in_=s2[r0:r1], writes=[r, chain])


def cast_late(k):
    casts = [("w_out0", k.w_out0, k.w_out0_b)]
    for l in range(2):
        if l == 1:
            casts += [("w_in1", k.w_in1, k.w_in1_b), ("w_out1", k.w_out1, k.w_out1_b)]
        casts += [(f"wg{l}", k.wg[l], k.wg_b[l]), (f"wu{l}", k.wu[l], k.wu_b[l]), (f"wd{l}", k.wd[l], k.wd_b[l])]
    cast_weights(k, casts)


def bcast_rows(ap, n=128):
    return ap.partition_broadcast(n)


def setup(k):
    S, A = k.S, k.A
    k.r_w = {}
    k.cast_chain = [Res() for _ in range(4)]
    k.cast_i = 0
    cast_weights(k, [("w_in0", k.w_in0, k.w_in0_b)])

    k.ident_f = A.alloc("ident_f", [128, 128], F32)
    k.ident_b = A.alloc("ident_b", [128, 128], BF16)
    k.r_id = Res("ident")
    S.dma("sp", out=k.ident_f[:], in_=k.ident, writes=[k.r_id])
    S.ins("dve", "tensor_copy", reads=[k.r_id], writes=[k.r_id], out=k.ident_b[:], in_=k.ident_f[:])
    k.G0 = A.alloc("G0", [128, 10, 64], F32)
    k.G1 = A.alloc("G1", [128, 128], F32)
    k.lam = A.alloc("lam", [128, 2], F32)
    k.cst = A.alloc("cst", [128, 8], F32)
    k.r_c = Res("consts")
    S.ins("dve", "memset", writes=[k.r_c], ap=k.cst[:, 0:1], constant=-0.5)
    S.ins("dve", "memset", writes=[k.r_c], ap=k.cst[:, 1:2], constant=LN_EPS)
    S.ins("dve", "memset", writes=[k.r_c], ap=k.cst[:, 2:3], constant=RMS_EPS)
    A.push()
    MBt = A.alloc("MB", [128, 8, 14, 64], BF16)
    v = A.alloc("vecs", [128, 6, 64], F32)
    g1 = A.alloc("g1t", [128, 128], F32)
    stage = A.alloc("mbst", [128, 8, 14, 64], F32)
    r_v = Res()
    for i, src in enumerate((k.g_q, k.g_k, k.lq1, k.lk1, k.lq2, k.lk2)):
        S.dma("sp", out=v[:, i, :], in_=bcast_rows(src), writes=[r_v])
    S.dma("sp", out=g1[:], in_=bcast_rows(k.g_sub), writes=[r_v])
    S.ins("dve", "tensor_scalar", reads=[r_v], writes=[k.r_c], out=k.G0[:, 0:8, :],
          in0=v[:, 0:1, :].to_broadcast([128, 8, 64]), scalar1=0.125, scalar2=None, op0=ALU.mult)
    S.ins("dve", "tensor_copy", reads=[r_v], writes=[k.r_c], out=k.G0[:, 8:10, :],
          in_=v[:, 1:2, :].to_broadcast([128, 2, 64]))
    S.ins("dve", "tensor_scalar", reads=[r_v], writes=[k.r_c], out=k.G1[:], in0=g1[:],
          scalar1=1.0 - LAMBDA_INIT, scalar2=None, op0=ALU.mult)
    pr = A.alloc("pr", [128, 2, 64], F32)
    sm = A.alloc("sm", [128, 2], F32)
    r_p = Res()
    S.ins("dve", "tensor_tensor", reads=[r_v], writes=[r_p], out=pr[:, 0, :], in0=v[:, 2, :], in1=v[:, 3, :], op=ALU.mult)
    S.ins("dve", "tensor_tensor", reads=[r_v], writes=[r_p], out=pr[:, 1, :], in0=v[:, 4, :], in1=v[:, 5, :], op=ALU.mult)
    S.ins("dve", "tensor_reduce", reads=[r_p], writes=[r_p], out=sm[:], in_=pr[:], axis=AX.X, op=ALU.add)
    S.ins("act", "activation", reads=[r_p], writes=[r_p], out=sm[:], in_=sm[:], func=AF.Exp)
    S.ins("dve", "tensor_tensor", reads=[r_p], writes=[k.r_c], out=k.lam[:, 0:1], in0=sm[:, 0:1], in1=sm[:, 1:2], op=ALU.subtract)
    S.ins("dve", "tensor_scalar", reads=[k.r_c], writes=[k.r_c], out=k.lam[:, 0:1], in0=k.lam[:, 0:1],
          scalar1=LAMBDA_INIT, scalar2=None, op0=ALU.add)
    S.ins("dve", "tensor_scalar", reads=[k.r_c], writes=[k.r_c], out=k.lam[:, 1:2], in0=k.lam[:, 0:1],
          scalar1=-1.0, scalar2=None, op0=ALU.mult)
    r_st = Res()
    for b in range(2):
        for h in range(8):
            S.dma("sp", out=stage[b * 64:(b + 1) * 64, h, :, :],
                  in_=k.tpm[h, b:b + 14].rearrange("r k q -> k r q"), writes=[r_st])
    r_mb = Res()
    S.ins("act", "activation", reads=[r_st], writes=[r_mb], out=MBt[:].rearrange("p h m q -> p (h m q)"),
          in_=stage[:].rearrange("p h m q -> p (h m q)"), func=AF.Exp)
    S.dma("sp", out=k.mb_d, in_=MBt[:].rearrange("p h m q -> p (h m q)"), reads=[r_mb])


def rsqrt_pool(k, out, in_, scale, eps, reads, writes):
    S = k.S
    S.ins("pool", "tensor_scalar", reads=reads, writes=writes, out=out, in0=in_, scalar1=scale, scalar2=eps,
          op0=ALU.mult, op1=ALU.add)
    S.ins("pool", "tensor_tensor", reads=list(writes) + [k.r_c], writes=writes, out=out, in0=out,
          in1=k.cst[:, 0:1].to_broadcast(list(out.shape)), op=ALU.pow)


def rsqrt_act(k, out, in_, scale, eps, reads, writes):
    S = k.S
    col = {LN_EPS: 1, RMS_EPS: 2}[eps]
    S.ins("act", "activation", reads=list(reads) + [k.r_c], writes=writes, out=out, in_=in_, func=AF.Sqrt,
          bias=k.cst[:, col:col + 1], scale=scale)
    S.ins("dve", "reciprocal", reads=writes, writes=writes, out=out, in_=out)


def seq_pos(k, tok):
    t0 = 0
    for L in k.seqs:
        if tok < t0 + L:
            return tok - t0
        t0 += L
    raise AssertionError


def layer_norm_tile(k, z, r_z, gt, bt, r_ln, out, r_out, st, r_st):
    S = k.S
    stats, mv, rstd, nmr = st
    S.ins("dve", "bn_stats", reads=[r_z], writes=[r_st], out=stats[:, 0, :], in_=z[:, 0:512])
    S.ins("dve", "bn_stats", reads=[r_z], writes=[r_st], out=stats[:, 1, :], in_=z[:, 512:1024])
    S.ins("dve", "bn_aggr", reads=[r_st], writes=[r_st], out=mv[:], in_=stats[:].rearrange("p a b -> p (a b)"))
    rsqrt_pool(k, rstd[:], mv[:, 1:2], 1.0, LN_EPS, [r_st], [r_st])
    S.ins("dve", "tensor_scalar", reads=[r_z, r_st], writes=[r_z], out=z[:], in0=z[:], scalar1=mv[:, 0:1],
          scalar2=rstd[:, 0:1], op0=ALU.subtract, op1=ALU.mult)
    if k.dbg and k.dbg_tok is not None:
        tok = k.dbg_tok
        S.dma("sp", out=k.dbg_s[0, tok:tok + 128, :], in_=mv[:, 0:1], reads=[r_st])
        S.dma("sp", out=k.dbg_s[1, tok:tok + 128, :], in_=mv[:, 1:2], reads=[r_st])
        S.dma("sp", out=k.dbg_s[2, tok:tok + 128, :], in_=rstd[:], reads=[r_st])
        S.dma("sp", out=k.dbg_s[3, tok:tok + 128, :], in_=nmr[:], reads=[r_st])
        S.dma("sp", out=k.dbg_zn[tok:tok + 128, :], in_=z[:], reads=[r_z])
    S.ins("pool", "tensor_tensor", reads=[r_z, r_ln], writes=[r_z], out=z[:], in0=z[:], in1=gt[:], op=ALU.mult)
    S.ins("pool", "tensor_tensor", reads=[r_z, r_ln], writes=[r_out], out=out, in0=z[:], in1=bt[:], op=ALU.add)


def phase_A0(k):
    S, A, ps, T = k.S, k.A, k.ps, k.T
    A.push()
    w = A.alloc("w_in0", [128, 8, IN0_W], BF16)
    r_w = Res()
    S.dma("sp", out=w[:], in_=k.w_in0_b.rearrange("(k p) n -> p k n", p=128), reads=[k.r_w["w_in0"]], writes=[r_w])
    xs = A.ring("xs", 4, [128, D], F32)
    cs = A.ring("cs", 6, [128, 64], F32)
    xT = A.ring("xT", 2, [128, 8, 512], BF16)
    sq = A.ring("sq", 2, [128, 640], F32)
    xsb = A.ring("xsb", 3, [128, 640], F32)
    sst = A.ring("sst", 3, [128, 10], F32)
    tmpA = A.ring("tmpA", 2, [128, 2, 320], F32)
    tmpB = A.ring("tmpB", 2, [128, 2, 320], F32)
    qkr = A.ring("qkr", 4, [128, 640], BF16)
    vna = A.ring("vna", 2, [128, 8, 65], BF16)
    vg = A.ring("vg", 2, [128, 2, 65], BF16)
    gst = A.ring("gst", 2, [128, 5, 512], BF16)
    nst = A.ring("nst", 2, [128, 8, 512], BF16)
    for t_, r_ in zip(vna.tiles + vg.tiles, vna.res + vg.res):
        S.ins("pool", "memset", writes=[r_], ap=t_[:, :, 64:65], constant=1.0)
    pT = ps[:, 0, :].bitcast(BF16).rearrange("p (a c) -> p a c", c=128)
    r_pT = Res()
    pSec = Ring([ps[:, 1:3, :], ps[:, 3:5, :]])
    pFMr = Ring([ps[:, 5, :], ps[:, 6, :]])
    xb16 = A.ring("xb16", 2, [128, D], BF16)
    pTq = ps[:, 7, :].bitcast(BF16).rearrange("p (a c) -> p a c", c=128)
    r_pTq = Res()
    pending = []
    q3 = []

    def flush():
        while pending:
            pending.pop(0)()

    loaded = {}
    NTILE = T // 128

    def prefetch(upto):
        for ti in range(len(loaded), min(upto + 1, NTILE)):
            tok_ = ti * 128
            pos_ = seq_pos(k, tok_)
            xt_, r_xt_ = xs.next()
            ct_, r_ct_ = cs.next()
            S.dma("sp", out=xt_[:], in_=k.x[tok_:tok_ + 128, :], writes=[r_xt_])
            S.dma("sp", out=ct_[:], in_=k.rope_ax[pos_:pos_ + 128, :], writes=[r_ct_])
            loaded[ti] = (xt_, r_xt_, ct_, r_ct_)

    fronts = {}
    xTg = {}

    def front(ti):
        if ti >= NTILE:
            return
        prefetch(ti)
        g_, t_ = divmod(ti, 4)
        if t_ == 0:
            xTg[g_] = xT.next()
        xTt_, r_xT_ = xTg[g_]
        xt_, r_xt_, _, _ = loaded[ti]
        xh, r_xh = xb16.next()
        S.ins("act", "activation", reads=[r_xt_], writes=[r_xh], out=xh[:], in_=xt_[:], func=AF.Copy)
        S.op("pe", [("transpose", dict(out=pT[:, kk, :], in_=xh[:, kk * 128:(kk + 1) * 128], identity=k.ident_b[:]))
                    for kk in range(8)], reads=[r_xh, k.r_id], writes=[r_pT])
        S.ins("act", "activation", reads=[r_pT], writes=[r_xT_], out=xTt_[:, :, t_ * 128:(t_ + 1) * 128], in_=pT,
              func=AF.Copy)

    front(0)
    for g in range(T // 512):
        xTt, r_xT = xTg[g]
        gs, r_gs = gst.next()
        ns, r_ns = nst.next()
        for t in range(4):
            tok = g * 512 + t * 128
            pos = seq_pos(k, tok)
            prefetch(g * 4 + t + 2)
            xt, r_xt, ct, r_ct = loaded[g * 4 + t]
            p0, r_p0 = pSec.next()
            p0f = p0.rearrange("p a c -> p (a c)")
            items = []
            for (c0, c1) in ((0, 512), (512, 640)):
                for kk in range(8):
                    items.append(("matmul", dict(out=p0f[:, c0:c1], lhsT=xTt[:, kk, t * 128:(t + 1) * 128],
                                                 rhs=w[:, kk, 1024 + c0:1024 + c1], start=(kk == 0), stop=(kk == 7))))
            S.op("pe", items, reads=[r_xT, r_w], writes=[r_p0])
            front(g * 4 + t + 1)
            sqt, r_sq = sq.next()
            xb, r_xsb = xsb.next()
            ss, r_ss = sst.next()
            S.ins("act", "activation", reads=[r_p0], writes=[r_xsb], out=xb[:], in_=p0f[:, 0:640], func=AF.Copy)
            S.ins("dve", "tensor_tensor", reads=[r_xsb], writes=[r_sq], out=sqt[:], in0=xb[:], in1=xb[:], op=ALU.mult)
            S.ins("dve", "tensor_reduce", reads=[r_sq], writes=[r_ss], out=ss[:],
                  in_=sqt[:].rearrange("p (h d) -> p h d", d=64), axis=AX.X, op=ALU.add)
            S.ins("act", "activation", reads=[r_ss, k.r_c], writes=[r_ss], out=ss[:], in_=ss[:], func=AF.Sqrt,
                  bias=k.cst[:, 2:3], scale=1.0 / 64)

            def stage2(xb=xb, r_xsb=r_xsb, ss=ss, r_ss=r_ss, ct=ct, r_ct=r_ct, gs=gs, r_gs=r_gs, t=t, g=g):
                S.ins("dve", "reciprocal", reads=[r_ss], writes=[r_ss], out=ss[:], in_=ss[:])
                xv = xb[:].rearrange("p (h d) -> p h d", d=64)
                S.ins("dve", "tensor_tensor", reads=[r_xsb, r_ss], writes=[r_xsb], out=xv, in0=xv,
                      in1=ss[:].unsqueeze(2).to_broadcast([128, 10, 64]), op=ALU.mult)
                S.ins("dve", "tensor_tensor", reads=[r_xsb, k.r_c], writes=[r_xsb], out=xv, in0=xv, in1=k.G0[:], op=ALU.mult)
                x5 = xb[:].rearrange("p (h a b f) -> p h a b f", a=2, b=2, f=16)
                x1, x2 = x5[:, :, :, 0, :], x5[:, :, :, 1, :]
                c4 = ct[:].rearrange("p (s a f) -> p s a f", s=2, a=2)
                cosb = c4[:, 0:1, :, :].to_broadcast([128, 10, 2, 16])
                sinb = c4[:, 1:2, :, :].to_broadcast([128, 10, 2, 16])
                qk_t, r_qk = qkr.next()
                r_qa, r_qb = Res(), Res()
                o5 = qk_t[:].rearrange("p (h a b f) -> p h a b f", a=2, b=2, f=16)
                ta, r_ta = tmpA.next()
                tb, r_tb = tmpB.next()
                tva = [ta[:, i, :].rearrange("p (h a f) -> p h a f", a=2, f=16) for i in range(2)]
                tvb = [tb[:, i, :].rearrange("p (h a f) -> p h a f", a=2, f=16) for i in range(2)]
                S.ins("dve", "tensor_tensor", reads=[r_xsb, r_ct], writes=[r_ta], out=tva[0], in0=x1, in1=cosb, op=ALU.mult)
                S.ins("dve", "tensor_tensor", reads=[r_xsb, r_ct], writes=[r_ta], out=tva[1], in0=x2, in1=sinb, op=ALU.mult)
                S.ins("dve", "tensor_tensor", reads=[r_ta, r_qk], writes=[r_qa], out=o5[:, :, :, 0, :], in0=tva[0],
                      in1=tva[1], op=ALU.subtract)
                S.ins("pool", "tensor_tensor", reads=[r_xsb, r_ct], writes=[r_tb], out=tvb[0], in0=x1, in1=sinb, op=ALU.mult)
                S.ins("pool", "tensor_tensor", reads=[r_xsb, r_ct], writes=[r_tb], out=tvb[1], in0=x2, in1=cosb, op=ALU.mult)
                S.ins("pool", "tensor_tensor", reads=[r_tb, r_qk], writes=[r_qb], out=o5[:, :, :, 1, :], in0=tvb[0],
                      in1=tvb[1], op=ALU.add)

                def back():
                    S.op("pe", [("transpose", dict(out=pTq[:, j, :], in_=qk_t[:, j * 128:(j + 1) * 128], identity=k.ident_b[:]))
                                for j in range(5)], reads=[r_qa, r_qb, k.r_id], writes=[r_pTq])
                    S.ins("act", "activation", reads=[r_pTq, r_qa, r_qb], writes=[r_gs, r_qk], out=gs[:, :, t * 128:(t + 1) * 128],
                          in_=pTq[:, 0:5, :], func=AF.Copy)
                    if t == 3:
                        S.dma("sp", out=k.gT_d.rearrange("(c p) t -> p c t", p=128)[:, :, g * 512:(g + 1) * 512], in_=gs[:],
                              reads=[r_gs])
                q3.append(back)
            p1, r_p1 = pSec.next()
            p1f = p1.rearrange("p a c -> p (a c)")
            items = []
            for (c0, c1) in ((0, 512), (512, 640)):
                for kk in range(8):
                    items.append(("matmul", dict(out=p1f[:, c0:c1], lhsT=xTt[:, kk, t * 128:(t + 1) * 128],
                                                 rhs=w[:, kk, 1664 + c0:1664 + c1], start=(kk == 0), stop=(kk == 7))))
            S.op("pe", items, reads=[r_xT, r_w], writes=[r_p1])
            while q3:
                q3.pop(0)()
            while pending:
                pending.pop(0)()
            pending.append(stage2)
            vn, r_vn = vna.next()
            vgt, r_vg = vg.next()
            S.ins("dve", "tensor_copy", reads=[r_p1], writes=[r_vn], out=vn[:, :, 0:64],
                  in_=p1f[:, 0:512].rearrange("p (h d) -> p h d", d=64))
            S.ins("dve", "tensor_copy", reads=[r_p1], writes=[r_vg], out=vgt[:, :, 0:64],
                  in_=p1f[:, 512:640].rearrange("p (h d) -> p h d", d=64))
            S.dma("sp", out=k.vna_d[tok:tok + 128, :], in_=vn[:].rearrange("p h e -> p (h e)"), reads=[r_vn])
            S.dma("sp", out=k.vg_d[tok:tok + 128, :], in_=vgt[:].rearrange("p h e -> p (h e)"), reads=[r_vg])
        for oc in range(8):
            pFM, r_pFM = pFMr.next()
            S.op("pe", [("matmul", dict(out=pFM, lhsT=w[:, kk, oc * 128:(oc + 1) * 128], rhs=xTt[:, kk, :],
                                        start=(kk == 0), stop=(kk == 7))) for kk in range(8)],
                 reads=[r_xT, r_w], writes=[r_pFM])
            S.ins("act", "activation", reads=[r_pFM], writes=[r_ns], out=ns[:, oc, :], in_=pFM, func=AF.Copy)
        S.dma("sp", out=k.naT_d.rearrange("(c p) t -> p c t", p=128)[:, :, g * 512:(g + 1) * 512], in_=ns[:],
              reads=[r_ns])
    while pending or q3:
        q3_now = list(q3)
        del q3[:]
        for f in q3_now:
            f()
        if pending:
            pending.pop(0)()
    S.barrier()
    A.pop()


def phase_B0_na(k, tok0, L):
    S, A, ps = k.S, k.A, k.ps
    R = L // GRID_W
    NT = L // 128
    A.push()
    KT = A.alloc("KTna", [128, 4, L], BF16)
    QT = A.alloc("QTna", [128, 4, L], BF16)
    Ve = A.alloc("Ve", [128, NT, 520], BF16)
    Vo = A.alloc("Vo", [128, NT - 1, 520], BF16)
    MB = A.alloc("MBna", [128, 8, 14, 64], BF16)
    r_in, r_mb, r_v = Res(), Res(), Res()
    nav = k.naT_d.rearrange("(c p) t -> p c t", p=128)
    S.dma("sp", out=QT[:], in_=nav[:, 0:4, tok0:tok0 + L], writes=[r_in])
    S.dma("sp", out=KT[:], in_=nav[:, 4:8, tok0:tok0 + L], writes=[r_in])
    S.dma("sp", out=MB[:].rearrange("p h m q -> p (h m q)"), in_=k.mb_d, writes=[r_mb])
    S.dma("sp", out=Ve[:], in_=k.vna_d[tok0:tok0 + L, :].rearrange("(n p) f -> p n f", p=128), writes=[r_v])
    S.dma("sp", out=Vo[:], in_=k.vna_d[tok0 + 64:tok0 + 64 + (NT - 1) * 128, :].rearrange("(n p) f -> p n f", p=128),
          writes=[r_v])
    E = A.ring("Ena", 4, [128, 2, 256], BF16)
    Ost = A.ring("Ona", 2, [128, 512], BF16)
    rc = A.ring("rcna", 2, [128, 8], F32)
    pS = Ring([ps[:, 0:2, 0:256], ps[:, 2:4, 0:256]])
    pO = Ring([ps[:, 4:6, :], ps[:, 6:8, :]])
    steps = []
    for i in range(NT):
        tst = {"started": {}}
        for rr in (0, 1):
            r = 2 * i + rr
            r0 = min(max(r - 4, 0), R - 8)
            for hp in range(4):
                st = {}

                def qk(st=st, r=r, r0=r0, hp=hp):
                    st["ps"] = pS.next()
                    p_s, r_ps = st["ps"]
                    items = []
                    for a in (0, 1):
                        for c in range(4):
                            k0 = (r0 + 2 * c) * 64
                            items.append(("matmul", dict(out=p_s[:, a, c * 64:(c + 1) * 64],
                                                         lhsT=KT[a * 64:(a + 1) * 64, hp, k0:k0 + 128],
                                                         rhs=QT[a * 64:(a + 1) * 64, hp, r * 64:(r + 1) * 64],
                                                         start=True, stop=True, skip_group_check=True)))
                    S.op("pe", items, reads=[r_in], writes=[r_ps])

                def mid(st=st, r=r, r0=r0, hp=hp):
                    p_s, r_ps = st["ps"]
                    st["e"] = E.next()
                    e_t, r_e = st["e"]
                    S.ins("act", "activation", reads=[r_ps], writes=[r_e], out=e_t[:], in_=p_s, func=AF.Exp, scale=0.125)
                    m0 = r0 - r + 7
                    ev = e_t[:].rearrange("p a (c q) -> p a c q", q=64)
                    S.ins("dve", "tensor_tensor", reads=[r_e, r_mb], writes=[r_e], out=ev, in0=ev,
                          in1=MB[:, 2 * hp:2 * hp + 2, m0:m0 + 7:2, :], op=ALU.mult)

                def pv(st=st, tst=tst, rr=rr, r0=r0, hp=hp):
                    if rr == 0 and hp == 0:
                        tst["po"] = pO.next()
                    po, r_po = tst["po"]
                    e_t, r_e = st["e"]
                    Vt = Ve if r0 % 2 == 0 else Vo
                    kc0 = r0 // 2
                    items = []
                    for a in (0, 1):
                        h = 2 * hp + a
                        bank = h // 4
                        for c in range(4):
                            stt_ = not tst["started"].get((bank, rr), False)
                            tst["started"][(bank, rr)] = True
                            items.append(("matmul", dict(out=po[rr * 64:(rr + 1) * 64, bank, (h % 4) * 65:(h % 4 + 1) * 65],
                                                         lhsT=e_t[:, a, c * 64:(c + 1) * 64],
                                                         rhs=Vt[:, kc0 + c, h * 65:(h + 1) * 65],
                                                         start=stt_, stop=(c == 3), skip_group_check=True)))
                    S.op("pe", items, reads=[r_e, r_v], writes=[r_po])

                fin = None
                if rr == 1 and hp == 3:
                    def fin(tst=tst, i=i):
                        po, r_po = tst["po"]
                        pov = po[:, :, 0:260].rearrange("p b (h e) -> p b h e", e=65)
                        rct, r_rc = rc.next()
                        ot, r_ot = Ost.next()
                        rc4 = rct[:].rearrange("p (b h e) -> p b h e", b=2, e=1)
                        S.ins("dve", "reciprocal", reads=[r_po], writes=[r_rc], out=rc4, in_=pov[:, :, :, 64:65])
                        S.ins("dve", "tensor_tensor", reads=[r_po, r_rc], writes=[r_ot],
                              out=ot[:].rearrange("p (b h d) -> p b h d", b=2, d=64), in0=pov[:, :, :, 0:64],
                              in1=rc4.to_broadcast([128, 2, 4, 64]), op=ALU.mult)
                        t0 = tok0 + i * 128
                        S.dma("sp", out=k.ao_d[t0:t0 + 128, 0:512], in_=ot[:], reads=[r_ot])
                steps.append((qk, mid, pv, fin))
    run_pipeline(steps, skew=2)
    S.barrier()
    A.pop()


def qgroups(NT, gmax):
    out = []
    q = 0
    rem = NT
    while rem > 0:
        if rem > gmax + 1 or rem == gmax:
            n = gmax
        elif rem == gmax + 1 and gmax > 2:
            n = gmax - 1
        else:
            n = min(rem, gmax)
        out.append((q, n))
        q += n
        rem -= n
    return out


def run_pipeline(steps, skew=2):
    n = len(steps)
    for j in range(min(skew, n)):
        steps[j][0]()
    for i in range(n):
        steps[i][1]()
        if i + skew < n:
            steps[i + skew][0]()
        steps[i][2]()
        if steps[i][3] is not None:
            steps[i][3]()


def phase_B0_gqa(k, tok0, L):
    S, A, ps = k.S, k.A, k.ps
    NT = L // 128
    A.push()
    KT = A.alloc("KTg", [128, L], BF16)
    QT = A.alloc("QTg", [128, 4, L], BF16)
    V = A.alloc("Vg", [128, NT, 130], BF16)
    r_in, r_v = Res(), Res()
    gv = k.gT_d.rearrange("(c p) t -> p c t", p=128)
    S.dma("sp", out=KT[:], in_=k.gT_d[512:640, tok0:tok0 + L], writes=[r_in])
    r_q = [Res() for _ in range(4)]
    for jj in range(4):
        S.dma("sp", out=QT[:, jj, :], in_=gv[:, jj, tok0:tok0 + L], writes=[r_q[jj]])
    S.dma("sp", out=V[:], in_=k.vg_d[tok0:tok0 + L, :].rearrange("(n p) f -> p n f", p=128), writes=[r_v])
    E = A.ring("Eg", 3, [128, 2, 512], BF16)
    Ost = A.ring("Og", 2, [128, 4, 512], BF16)
    rc = A.ring("rcg", 2, [128, 2, 4], F32)
    pS = Ring([ps[:, 0:2, :], ps[:, 2:4, :]])
    pO = Ring([ps[:, 4:6, :], ps[:, 6:8, :]])
    r_out = Res()
    steps = []
    for g in range(L // 512):
        gst = {}
        for j in range(4):
            hst = {}
            for kc in range(NT):
                st = {}

                def qk(st=st, g=g, j=j, kc=kc):
                    st["ps"] = pS.next()
                    p_s, r_ps = st["ps"]
                    items = []
                    for a in (0, 1):
                        items.append(("matmul", dict(out=p_s[:, a, :], lhsT=KT[a * 64:(a + 1) * 64, kc * 128:(kc + 1) * 128],
                                                     rhs=QT[a * 64:(a + 1) * 64, j, g * 512:(g + 1) * 512],
                                                     start=True, stop=True, skip_group_check=True)))
                    S.op("pe", items, reads=[r_in, r_q[j]], writes=[r_ps])

                def mid(st=st):
                    p_s, r_ps = st["ps"]
                    st["e"] = E.next()
                    e_t, r_e = st["e"]
                    S.ins("act", "activation", reads=[r_ps], writes=[r_e], out=e_t[:], in_=p_s, func=AF.Exp)

                def pv(st=st, hst=hst, kc=kc):
                    if kc == 0:
                        hst["po"] = pO.next()
                    po, r_po = hst["po"]
                    e_t, r_e = st["e"]
                    items = []
                    for a in (0, 1):
                        for qt in range(4):
                            items.append(("matmul", dict(out=po[:, a, qt * 65:(qt + 1) * 65],
                                                         lhsT=e_t[:, a, qt * 128:(qt + 1) * 128],
                                                         rhs=V[:, kc, a * 65:(a + 1) * 65],
                                                         start=(kc == 0 and qt == 0), stop=(kc == NT - 1),
                                                         skip_group_check=True)))
                    S.op("pe", items, reads=[r_e, r_v], writes=[r_po])

                fin = None
                if kc == NT - 1:
                    def fin(hst=hst, gst=gst, g=g, j=j):
                        if j == 0:
                            gst["ot"] = Ost.next()
                        ot, r_ot = gst["ot"]
                        po, r_po = hst["po"]
                        pov = po[:, :, 0:260].rearrange("p a (q e) -> p a q e", e=65)
                        rct, r_rc = rc.next()
                        S.ins("dve", "reciprocal", reads=[r_po], writes=[r_rc], out=rct[:].unsqueeze(3), in_=pov[:, :, :, 64:65])
                        for a in (0, 1):
                            h = j + 4 * a
                            S.ins("dve", "tensor_tensor", reads=[r_po, r_rc], writes=[r_ot], out=ot[:, :, h * 64:(h + 1) * 64],
                                  in0=pov[:, a, :, 0:64], in1=rct[:, a, :].unsqueeze(2).to_broadcast([128, 4, 64]), op=ALU.mult)
                        if j == 3:
                            t0 = tok0 + g * 512
                            S.dma("sp", out=k.ao_d[t0:t0 + 512, 512:1024].rearrange("(q p) f -> p q f", p=128), in_=ot[:],
                                  reads=[r_ot])
                steps.append((qk, mid, pv, fin))
    run_pipeline(steps, skew=2)
    S.barrier()
    A.pop()


def phase_C(k, layer, src, ao, dst):
    S, A, ps, T = k.S, k.A, k.ps, k.T
    NG = T // 512
    A.push()
    Wo = A.alloc("Wo", [128, 8, D], BF16)
    Wd = A.alloc("Wd", [128, NCH, D], BF16)
    lnp = [A.alloc(f"ln{i}", [128, D], F32) for i in range(4)]
    r_wo, r_wd, r_ln = Res(), Res(), Res()
    wo_b = k.w_out0_b if layer == 0 else k.w_out1_b
    S.dma("sp", out=Wo[:], in_=wo_b.rearrange("(k p) n -> p k n", p=128), reads=[k.r_w[f"w_out{layer}"]], writes=[r_wo])
    o_in = A.ring("o_in", 2, [128, D], BF16)
    oT = A.ring("oT", 2, [128, 8, 128], BF16)
    x_in = A.ring("x_in", 2, [128, D], F32)
    z = A.ring("z", 2, [128, D], F32)
    x1t = A.ring("x1t", 2, [128, D], F32)
    x1r = A.ring("x1r", 2, [128, D], F32)
    x1b = A.ring("x1b", 4, [128, D], BF16)
    x1T = A.ring("x1T", 2, [128, 8, 512], BF16)
    hT = A.alloc("hT", [128, NCH, 512], BF16)
    r_hT = Res()
    wgu = A.ring("wgu", 6, [128, 2, 1024], BF16)
    sg = A.ring("sg", 2, [128, 512], BF16)
    yo = A.ring("yo", 2, [128, D], F32)
    stt = [A.alloc("stats", [128, 2, 6], F32), A.alloc("mv", [128, 2], F32), A.alloc("rstd", [128, 1], F32),
           A.alloc("nmr", [128, 1], F32)]
    stt2 = [A.alloc("stats2", [128, 2, 6], F32), A.alloc("mv2", [128, 2], F32), A.alloc("rstd2", [128, 1], F32),
            A.alloc("nmr2", [128, 1], F32)]
    r_st, r_st2 = Res(), Res()
    pTb = Ring([ps[:, 0, :].bitcast(BF16).rearrange("p (a c) -> p a c", c=128),
                ps[:, 1, :].bitcast(BF16).rearrange("p (a c) -> p a c", c=128)])
    pOut, r_pOut = ps[:, 2:4, :], Res()
    pGU = Ring([ps[:, 4:6, :], ps[:, 6:8, :]])
    r_wsrc = [k.r_w[f"wg{layer}"], k.r_w[f"wu{layer}"]]
    r_x1d = [Res() for _ in range(T // 128)]

    wq = {}

    def load_w(idx):
        g_, c_ = divmod(idx, NCH)
        if g_ >= NG or idx in wq:
            return
        wt, r_wt = wgu.next()
        S.dma("sp", out=wt[:, 0, :], in_=k.wg_b[layer, c_], reads=[r_wsrc[0]], writes=[r_wt])
        S.dma("sp", out=wt[:, 1, :], in_=k.wu_b[layer, c_], reads=[r_wsrc[1]], writes=[r_wt])
        wq[idx] = (wt, r_wt)

    ld = {}

    def load_c1(ti):
        if ti >= T // 128 or ti in ld:
            return
        tok = ti * 128
        oi, r_oi = o_in.next()
        xi, r_xi = x_in.next()
        S.dma("sp", out=oi[:], in_=ao[tok:tok + 128, :], writes=[r_oi])
        S.dma("sp", out=xi[:], in_=src[tok:tok + 128, :], writes=[r_xi])
        ld[ti] = (oi, r_oi, xi, r_xi)

    xTs = {}
    c1st = {}
    deferred = []

    def flush_deferred():
        while deferred:
            deferred.pop(0)()

    def c1a(ti):
        g_, t = divmod(ti, 4)
        tok = ti * 128
        while len(deferred) >= 2:
            deferred.pop(0)()
        load_c1(ti)
        oi, r_oi, xi, r_xi = ld[ti]
        if t == 0:
            xTs[g_] = x1T.next()
        pt, r_pt = pTb.next()
        S.op("pe", [("transpose", dict(out=pt[:, kk, :], in_=oi[:, kk * 128:(kk + 1) * 128], identity=k.ident_b[:]))
                    for kk in range(8)], reads=[r_oi, k.r_id], writes=[r_pt])
        ott, r_oT = oT.next()
        S.ins("act", "activation", reads=[r_pt], writes=[r_oT], out=ott[:], in_=pt, func=AF.Copy)
        items = []
        for half in (0, 1):
            for kk in range(8):
                items.append(("matmul", dict(out=pOut[:, half, :], lhsT=ott[:, kk, :],
                                             rhs=Wo[:, kk, half * 512:(half + 1) * 512], start=(kk == 0), stop=(kk == 7))))
        S.op("pe", items, reads=[r_oT, r_wo], writes=[r_pOut])
        zt, r_z = z.next()
        S.ins("dve", "scalar_tensor_tensor", reads=[r_xi, r_pOut], writes=[r_z], out=zt[:], in0=xi[:], scalar=ALPHA,
              in1=pOut.rearrange("p a c -> p (a c)"), op0=ALU.mult, op1=ALU.add)
        xt1, r_xt1 = x1t.next()
        layer_norm_tile(k, zt, r_z, lnp[0], lnp[1], r_ln, xt1[:], r_xt1, stt, r_st)
        deferred.append(lambda: S.dma("sp", out=k.x1_d[tok:tok + 128, :], in_=xt1[:], reads=[r_xt1], writes=[r_x1d[ti]]))
        xb, r_xb = x1b.next()
        S.ins("pool", "tensor_copy", reads=[r_xt1], writes=[r_xb], out=xb[:], in_=xt1[:])
        c1st[ti] = (xb, r_xb)

    def c1b(ti):
        g_, t = divmod(ti, 4)
        flush_deferred()
        xb, r_xb = c1st.pop(ti)
        xT_t, r_xT = xTs[g_]
        pt, r_pt = pTb.next()
        S.op("pe", [("transpose", dict(out=pt[:, kk, :], in_=xb[:, kk * 128:(kk + 1) * 128], identity=k.ident_b[:]))
                    for kk in range(8)], reads=[r_xb, k.r_id], writes=[r_pt])
        S.ins("act", "activation", reads=[r_pt], writes=[r_xT], out=xT_t[:, :, t * 128:(t + 1) * 128], in_=pt, func=AF.Copy)

    load_c1(0)
    for i, src_v in enumerate((k.ln_mix_g, k.ln_mix_b, k.ln_ffn_g, k.ln_ffn_b)):
        S.dma("sp", out=lnp[i][:], in_=bcast_rows(src_v[layer]), writes=[r_ln])
    load_c1(1)
    for i in range(5):
        load_w(i)
    S.dma("sp", out=Wd[:], in_=k.wd_b[layer].rearrange("(c p) n -> p c n", p=128), reads=[k.r_w[f"wd{layer}"]],
          writes=[r_wd])
    for t in range(4):
        c1a(t)
        load_c1(t + 2)
    for t in range(4):
        c1b(t)
    ydef = []
    A_AT = {1: 0, 6: 1, 11: 2, 16: 3}
    B_AT = {5: 0, 10: 1, 15: 2, 20: 3}
    for g in range(NG):
        xT_t, r_xT = xTs[g]
        for c in range(NCH):
            load_w(g * NCH + c + 5)
            wt, r_wt = wq.pop(g * NCH + c)
            pgu, r_pgu = pGU.next()
            items = []
            for m in (0, 1):
                for kk in range(8):
                    items.append(("matmul", dict(out=pgu[:, m, :], lhsT=wt[:, m, kk * 128:(kk + 1) * 128], rhs=xT_t[:, kk, :],
                                                 start=(kk == 0), stop=(kk == 7))))
            S.op("pe", items, reads=[r_wt, r_xT], writes=[r_pgu])
            sgt, r_sg = sg.next()
            S.ins("act", "activation", reads=[r_pgu], writes=[r_sg], out=sgt[:], in_=pgu[:, 0, :], func=AF.Silu)
            S.ins("dve", "tensor_tensor", reads=[r_sg, r_pgu], writes=[r_hT], out=hT[:, c, :], in0=sgt[:], in1=pgu[:, 1, :],
                  op=ALU.mult)
            if c == 2:
                while ydef:
                    ydef.pop(0)()
            if g + 1 < NG:
                if c in A_AT:
                    ti = (g + 1) * 4 + A_AT[c]
                    c1a(ti)
                    load_c1(ti + 2 if A_AT[c] < 2 else -1 + 10 ** 9)
                if c in B_AT:
                    c1b((g + 1) * 4 + B_AT[c])
                if c == 0:
                    load_c1((g + 1) * 4)
                    load_c1((g + 1) * 4 + 1)
        flush_deferred()
        xr = {}

        def load_x1r(t, g=g, xr=xr):
            ti = g * 4 + t
            xt_, r_xt_ = x1r.next()
            S.dma("sp", out=xt_[:], in_=k.x1_d[ti * 128:(ti + 1) * 128, :], reads=[r_x1d[ti]], writes=[r_xt_])
            xr[t] = (xt_, r_xt_)

        load_x1r(0)
        load_x1r(1)
        for t in range(4):
            tok = g * 512 + t * 128
            po, r_po = (pOut, r_pOut) if t % 2 == 0 else pGU.next()
            items = []
            for half in (0, 1):
                for c in range(NCH):
                    items.append(("matmul", dict(out=po[:, half, :], lhsT=hT[:, c, t * 128:(t + 1) * 128],
                                                 rhs=Wd[:, c, half * 512:(half + 1) * 512], start=(c == 0), stop=(c == NCH - 1))))
            S.op("pe", items, reads=[r_hT, r_wd], writes=[r_po])
            xt_, r_xt_ = xr[t]
            zt, r_z = z.next()
            S.ins("dve", "scalar_tensor_tensor", reads=[r_xt_, r_po], writes=[r_z], out=zt[:], in0=xt_[:],
                  scalar=ALPHA, in1=po.rearrange("p a c -> p (a c)"), op0=ALU.mult, op1=ALU.add)
            if t + 2 < 4:
                load_x1r(t + 2)
            yt, r_yt = yo.next()
            layer_norm_tile(k, zt, r_z, lnp[2], lnp[3], r_ln, yt[:], r_yt, stt2, r_st2)
            ydef.append(lambda tok=tok, yt=yt, r_yt=r_yt: S.dma("sp", out=dst[tok:tok + 128, :], in_=yt[:], reads=[r_yt]))
            if len(ydef) > 1:
                ydef.pop(0)()
    while ydef:
        ydef.pop(0)()
    S.barrier()
    A.pop()


def phase_A1(k):
    S, A, ps, T = k.S, k.A, k.ps, k.T
    A.push()
    w = A.alloc("w_in1", [128, 8, IN1_W], BF16)
    r_w = Res()
    S.dma("sp", out=w[:], in_=k.w_in1_b.rearrange("(k p) n -> p k n", p=128), reads=[k.r_w["w_in1"]], writes=[r_w])
    xs = A.ring("xs", 4, [128, D], F32)
    cs = A.ring("cs", 6, [128, 64], F32)
    xT = A.ring("xT", 2, [128, 8, 128], BF16)
    xsb = A.ring("xsb", 6, [128, D], F32)
    tmpA = A.ring("tmpA", 2, [128, 2, 512], F32)
    tmpB = A.ring("tmpB", 2, [128, 2, 512], F32)
    qkr = A.ring("qkr", 7, [128, D], BF16)
    vst = A.ring("vst", 2, [128, 8, 129], BF16)
    stq = A.ring("stq", 2, [128, 8, 512], BF16)
    stk = A.ring("stk", 2, [128, 8, 512], BF16)
    for t_, r_ in zip(vst.tiles, vst.res):
        S.ins("pool", "memset", writes=[r_], ap=t_[:, :, 128:129], constant=1.0)
    pT = ps[:, 0, :].bitcast(BF16).rearrange("p (a c) -> p a c", c=128)
    r_pT = Res()
    xb16 = A.ring("xb16", 2, [128, D], BF16)
    pSec = Ring([ps[:, 2:4, :], ps[:, 4:6, :]])
    pTq = Ring([ps[:, 6, :].bitcast(BF16).rearrange("p (a c) -> p a c", c=128),
                ps[:, 7, :].bitcast(BF16).rearrange("p (a c) -> p a c", c=128)])
    pending = []
    q3 = []

    def flush():
        while pending:
            pending.pop(0)()

    loaded = {}
    NTILE = T // 128

    def prefetch(upto):
        for ti in range(len(loaded), min(upto + 1, NTILE)):
            tok_ = ti * 128
            pos_ = seq_pos(k, tok_)
            xt_, r_xt_ = xs.next()
            ct_, r_ct_ = cs.next()
            S.dma("sp", out=xt_[:], in_=k.x2_d[tok_:tok_ + 128, :], writes=[r_xt_])
            S.dma("sp", out=ct_[:], in_=k.rope_seq[pos_:pos_ + 128, :], writes=[r_ct_])
            loaded[ti] = (xt_, r_xt_, ct_, r_ct_)

    fronts = {}

    def front(ti):
        if ti >= NTILE:
            return
        prefetch(ti)
        xt_, r_xt_, _, _ = loaded[ti]
        xh, r_xh = xb16.next()
        S.ins("act", "activation", reads=[r_xt_], writes=[r_xh], out=xh[:], in_=xt_[:], func=AF.Copy)
        S.op("pe", [("transpose", dict(out=pT[:, kk, :], in_=xh[:, kk * 128:(kk + 1) * 128], identity=k.ident_b[:]))
                    for kk in range(8)], reads=[r_xh, k.r_id], writes=[r_pT])
        xTt_, r_xT_ = xT.next()
        S.ins("act", "activation", reads=[r_pT], writes=[r_xT_], out=xTt_[:], in_=pT, func=AF.Copy)
        fronts[ti] = (xTt_, r_xT_)

    for g in range(T // 512):
        sq_t, r_sq = stq.next()
        sk_t, r_sk = stk.next()
        for t in range(4):
            tok = g * 512 + t * 128
            pos = seq_pos(k, tok)
            prefetch(g * 4 + t + 2)
            if g * 4 + t == 0:
                front(0)
            xt, r_xt, ct, r_ct = loaded[g * 4 + t]
            xTt, r_xT = fronts.pop(g * 4 + t)
            backs = []
            for sec in range(3):
                p0, r_p0 = pSec.next()
                p0f = p0.rearrange("p a c -> p (a c)")
                items = []
                for half in (0, 1):
                    for kk in range(8):
                        c0 = sec * 1024 + half * 512
                        items.append(("matmul", dict(out=p0[:, half, :], lhsT=xTt[:, kk, :], rhs=w[:, kk, c0:c0 + 512],
                                                     start=(kk == 0), stop=(kk == 7))))
                S.op("pe", items, reads=[r_xT, r_w], writes=[r_p0])
                if sec == 0:
                    front(g * 4 + t + 1)
                if sec == 2:
                    vt, r_vt = vst.next()
                    S.ins("dve", "tensor_copy", reads=[r_p0], writes=[r_vt], out=vt[:, :, 0:128],
                          in_=p0f.rearrange("p (h d) -> p h d", d=128))
                    S.dma("sp", out=k.v1_d[tok:tok + 128, :], in_=vt[:].rearrange("p h e -> p (h e)"), reads=[r_vt])
                    continue
                xb, r_xb = xsb.next()
                S.ins("act", "activation", reads=[r_p0], writes=[r_xb], out=xb[:], in_=p0f, func=AF.Copy)
                def stage2(sec=sec, xb=xb, r_xb=r_xb, ct=ct, r_ct=r_ct, sq_t=sq_t, r_sq=r_sq, sk_t=sk_t, r_sk=r_sk, t=t, g=g):
                    x4 = xb[:].rearrange("p (h b f) -> p h b f", b=2, f=32)
                    x1, x2 = x4[:, :, 0, :], x4[:, :, 1, :]
                    c3 = ct[:].rearrange("p (s f) -> p s f", s=2)
                    cosb = c3[:, 0:1, :].to_broadcast([128, 16, 32])
                    sinb = c3[:, 1:2, :].to_broadcast([128, 16, 32])
                    qk_t, r_qk = qkr.next()
                    r_qa, r_qb = Res(), Res()
                    o4 = qk_t[:].rearrange("p (h b f) -> p h b f", b=2, f=32)
                    ta, r_ta = tmpA.next()
                    tb, r_tb = tmpB.next()
                    tva = [ta[:, i, :].rearrange("p (h f) -> p h f", f=32) for i in range(2)]
                    tvb = [tb[:, i, :].rearrange("p (h f) -> p h f", f=32) for i in range(2)]
                    S.ins("dve", "tensor_tensor", reads=[r_xb, r_ct], writes=[r_ta], out=tva[0], in0=x1, in1=cosb, op=ALU.mult)
                    S.ins("dve", "tensor_tensor", reads=[r_xb, r_ct], writes=[r_ta], out=tva[1], in0=x2, in1=sinb, op=ALU.mult)
                    S.ins("dve", "tensor_tensor", reads=[r_ta, r_qk], writes=[r_qa], out=o4[:, :, 0, :], in0=tva[0],
                          in1=tva[1], op=ALU.subtract)
                    S.ins("pool", "tensor_tensor", reads=[r_xb, r_ct], writes=[r_tb], out=tvb[0], in0=x1, in1=sinb, op=ALU.mult)
                    S.ins("pool", "tensor_tensor", reads=[r_xb, r_ct], writes=[r_tb], out=tvb[1], in0=x2, in1=cosb, op=ALU.mult)
                    S.ins("pool", "tensor_tensor", reads=[r_tb, r_qk], writes=[r_qb], out=o4[:, :, 1, :], in0=tvb[0],
                          in1=tvb[1], op=ALU.add)

                    def back():
                        pq, r_pq = pTq.next()
                        S.op("pe", [("transpose", dict(out=pq[:, j, :], in_=qk_t[:, j * 128:(j + 1) * 128], identity=k.ident_b[:]))
                                    for j in range(8)], reads=[r_qa, r_qb, k.r_id], writes=[r_pq])
                        dst_t, r_dst = (sq_t, r_sq) if sec == 0 else (sk_t, r_sk)
                        S.ins("act", "activation", reads=[r_pq, r_qa, r_qb], writes=[r_dst, r_qk],
                              out=dst_t[:, :, t * 128:(t + 1) * 128], in_=pq, func=AF.Copy)
                        if t == 3:
                            dd = k.qT_d if sec == 0 else k.kT_d
                            S.dma("sp", out=dd.rearrange("(c p) t -> p c t", p=128)[:, :, g * 512:(g + 1) * 512], in_=dst_t[:],
                                  reads=[r_dst])
                    q3.append(back)
                backs.append(stage2)
            while q3:
                q3.pop(0)()
            while pending:
                pending.pop(0)()
            pending.extend(backs)
    while pending or q3:
        q3_now = list(q3)
        del q3[:]
        for f in q3_now:
            f()
        while pending:
            pending.pop(0)()
    S.barrier()
    A.pop()


def phase_B1(k):
    S, A, ps = k.S, k.A, k.ps
    Lmax = max(k.seqs)
    NTmax = Lmax // 128
    qv = k.qT_d.rearrange("(c p) t -> p c t", p=128)
    kv = k.kT_d.rearrange("(c p) t -> p c t", p=128)
    A.push()
    sets = []
    for i in range(2):
        sets.append((A.alloc("KTd", [128, 2, Lmax], BF16), A.alloc("QTd", [128, 2, Lmax], BF16),
                     A.alloc("Vd", [128, NTmax, 258], BF16), Res()))
    E = A.ring("Ed", 3, [128, 2, 384], BF16)
    Ost = A.ring("Od", 2, [128, 3, 256], BF16)
    rc = A.ring("rcd", 2, [128, 2, 3], F32)
    t1 = A.ring("t1d", 2, [128, 3, 128], F32)
    t2 = A.ring("t2d", 2, [128, 3, 128], F32)
    ssd = A.ring("ssd", 2, [128, 3], F32)
    pS = Ring([ps[:, 0:2, :], ps[:, 2:4, :]])
    pO = Ring([ps[:, 4:6, :], ps[:, 6:8, :]])
    passes = []
    tok0 = 0
    for L in k.seqs:
        for hp in range(4):
            passes.append((tok0, L, hp))
        tok0 += L

    def load(p):
        if p >= len(passes):
            return
        tok0, L, hp = passes[p]
        KT, QT, V, r_in = sets[p % 2]
        NT = L // 128
        S.dma("sp", out=QT[:, :, 0:L], in_=qv[:, 2 * hp:2 * hp + 2, tok0:tok0 + L], writes=[r_in])
        S.dma("sp", out=KT[:, :, 0:L], in_=kv[:, 2 * hp:2 * hp + 2, tok0:tok0 + L], writes=[r_in])
        S.dma("sp", out=V[:, 0:NT, :], in_=k.v1_d[tok0:tok0 + L, hp * 258:(hp + 1) * 258].rearrange("(n p) f -> p n f", p=128),
              writes=[r_in])

    load(0)
    steps = []
    for p, (tok0, L, hp) in enumerate(passes):
        KT, QT, V, r_in = sets[p % 2]
        NT = L // 128
        first = True
        for (q0, nq) in qgroups(NT, 3):
            gst = {}
            NQ = nq * 128
            for hl in range(2):
                hst = {}
                for kc in range(NT):
                    st = {}

                    def qk(st=st, q0=q0, NQ=NQ, hl=hl, kc=kc, KT=KT, QT=QT, r_in=r_in):
                        st["ps"] = pS.next()
                        p_s, r_ps = st["ps"]
                        items = []
                        for a in (0, 1):
                            items.append(("matmul", dict(out=p_s[:, a, 0:NQ], lhsT=KT[a * 64:(a + 1) * 64, hl, kc * 128:(kc + 1) * 128],
                                                         rhs=QT[a * 64:(a + 1) * 64, hl, q0 * 128:q0 * 128 + NQ],
                                                         start=True, stop=True, skip_group_check=True)))
                        S.op("pe", items, reads=[r_in], writes=[r_ps])

                    def mid(st=st, NQ=NQ, pre=(p + 1 if first else None)):
                        if pre is not None:
                            load(pre)
                        p_s, r_ps = st["ps"]
                        st["e"] = E.next()
                        e_t, r_e = st["e"]
                        S.ins("act", "activation", reads=[r_ps], writes=[r_e], out=e_t[:, :, 0:NQ], in_=p_s[:, :, 0:NQ],
                              func=AF.Exp, scale=0.125)

                    first = False

                    def pv(st=st, hst=hst, kc=kc, nq=nq, hl=hl, V=V, r_in=r_in, NT=NT):
                        if kc == 0:
                            hst["po"] = pO.next()
                        po, r_po = hst["po"]
                        e_t, r_e = st["e"]
                        items = []
                        for a in (0, 1):
                            for qt in range(nq):
                                items.append(("matmul", dict(out=po[:, a, qt * 129:(qt + 1) * 129],
                                                             lhsT=e_t[:, a, qt * 128:(qt + 1) * 128],
                                                             rhs=V[:, kc, hl * 129:(hl + 1) * 129],
                                                             start=(kc == 0 and qt == 0), stop=(kc == NT - 1),
                                                             skip_group_check=True)))
                        S.op("pe", items, reads=[r_e, r_in], writes=[r_po])

                    fin = None
                    if kc == NT - 1:
                        def fin(hst=hst, gst=gst, q0=q0, nq=nq, NQ=NQ, hl=hl, tok0=tok0, hp=hp):
                            if hl == 0:
                                gst["ot"] = Ost.next()
                            ot, r_ot = gst["ot"]
                            po, r_po = hst["po"]
                            pov = po[:, :, 0:nq * 129].rearrange("p a (q e) -> p a q e", e=129)
                            rct, r_rc = rc.next()
                            S.ins("dve", "reciprocal", reads=[r_po], writes=[r_rc], out=rct[:, :, 0:nq].unsqueeze(3), in_=pov[:, :, :, 128:129])
                            S.ins("dve", "tensor_scalar", reads=[r_rc, k.r_c], writes=[r_rc], out=rct[:, 1, 0:nq], in0=rct[:, 1, 0:nq],
                                  scalar1=k.lam[:, 1:2], scalar2=None, op0=ALU.mult)
                            a1, r_a1 = t1.next()
                            a2, r_a2 = t2.next()
                            S.ins("dve", "tensor_tensor", reads=[r_po, r_rc], writes=[r_a1], out=a1[:, 0:nq, :], in0=pov[:, 0, :, 0:128],
                                  in1=rct[:, 0, 0:nq].unsqueeze(2).to_broadcast([128, nq, 128]), op=ALU.mult)
                            S.ins("dve", "tensor_tensor", reads=[r_po, r_rc], writes=[r_a2], out=a2[:, 0:nq, :], in0=pov[:, 1, :, 0:128],
                                  in1=rct[:, 1, 0:nq].unsqueeze(2).to_broadcast([128, nq, 128]), op=ALU.mult)
                            S.ins("pool", "tensor_tensor", reads=[r_a1, r_a2], writes=[r_a1], out=a1[:, 0:nq, :], in0=a1[:, 0:nq, :],
                                  in1=a2[:, 0:nq, :], op=ALU.add)
                            S.ins("pool", "tensor_tensor", reads=[r_a1], writes=[r_a2], out=a2[:, 0:nq, :], in0=a1[:, 0:nq, :],
                                  in1=a1[:, 0:nq, :], op=ALU.mult)
                            sst, r_ss = ssd.next()
                            S.ins("dve", "tensor_reduce", reads=[r_a2], writes=[r_ss], out=sst[:, 0:nq], in_=a2[:, 0:nq, :], axis=AX.X,
                                  op=ALU.add)
                            rsqrt_pool(k, sst[:, 0:nq], sst[:, 0:nq], 1.0 / 128, SUBLN_EPS, [r_ss], [r_ss])
                            S.ins("dve", "tensor_tensor", reads=[r_a1, r_ss], writes=[r_a1], out=a1[:, 0:nq, :], in0=a1[:, 0:nq, :],
                                  in1=sst[:, 0:nq].unsqueeze(2).to_broadcast([128, nq, 128]), op=ALU.mult)
                            S.ins("pool", "tensor_tensor", reads=[r_a1, k.r_c], writes=[r_ot], out=ot[:, 0:nq, hl * 128:(hl + 1) * 128],
                                  in0=a1[:, 0:nq, :], in1=k.G1[:].unsqueeze(1).to_broadcast([128, nq, 128]), op=ALU.mult)
                            if hl == 1:
                                t0 = tok0 + q0 * 128
                                S.dma("sp", out=k.ao1_d[t0:t0 + NQ, hp * 256:(hp + 1) * 256].rearrange("(q p) f -> p q f", p=128),
                                      in_=ot[:, 0:nq, :], reads=[r_ot])
                    steps.append((qk, mid, pv, fin))
    run_pipeline(steps, skew=2)
    S.barrier()
    A.pop()


def rope_tables():
    theta = np.float32(10000.0)
    t = np.arange(4096)

    def cs(pos, dim):
        inv = (theta ** (-(np.arange(0, dim, 2, dtype=np.float32)) / np.float32(dim))).astype(np.float32)
        ang = pos.astype(np.float32)[:, None] * inv[None, :]
        return np.cos(ang).astype(np.float32), np.sin(ang).astype(np.float32)

    rc, rs = cs(t // GRID_W, 32)
    cc, cs_ = cs(t % GRID_W, 32)
    ax = np.stack([np.stack([rc, cc], 1), np.stack([rs, cs_], 1)], 1)
    sc, ss = cs(t, 64)
    sq = np.stack([sc, ss], 1)
    return np.ascontiguousarray(ax.reshape(4096, 64)), np.ascontiguousarray(sq.reshape(4096, 64))


def host_weights(inp):
    f = lambda a: np.ascontiguousarray(np.asarray(a, dtype=np.float32))
    w0 = f(inp["w_in_mix0"][0])
    na_q, na_k, na_v = w0[:, 0:512], w0[:, 512:1024], w0[:, 1024:1536]
    gq, gk, gv = w0[:, 1536:2048], w0[:, 2048:2176], w0[:, 2176:2304]
    order = [0, 4, 1, 5, 2, 6, 3, 7]
    gqp = np.concatenate([gq[:, h * 64:(h + 1) * 64] for h in order], 1)
    w_in0 = f(np.concatenate([na_q, na_k, gqp, gk, na_v, gv], 1))
    wg = f(inp["w_ffn_gate"]).reshape(2, 8, 128, NCH, 128).transpose(0, 3, 2, 1, 4).reshape(2, NCH, 128, 1024)
    wu = f(inp["w_ffn_up"]).reshape(2, 8, 128, NCH, 128).transpose(0, 3, 2, 1, 4).reshape(2, NCH, 128, 1024)
    rpb = f(inp["rpb_na"][0])
    kc = np.arange(64)[:, None]
    qc = np.arange(64)[None, :]
    c0 = np.clip(qc - 8, 0, GRID_W - 16)
    valid = (kc >= c0) & (kc < c0 + 16)
    idx = np.clip(kc - qc + 15, 0, 30)
    tpm = np.where(valid[None, None], rpb[:, :, idx], np.float32(NEG)).astype(np.float32)
    ax, sq = rope_tables()
    return dict(
        w_in0=w_in0, w_out0=f(inp["w_out_mix0"][0]), w_in1=f(inp["w_in_mix1"][0]), w_out1=f(inp["w_out_mix1"][0]),
        wg=f(wg), wu=f(wu), wd=f(inp["w_ffn_down"]), tpm=f(tpm),
        g_q=f(inp["g_q_gqa"][0]), g_k=f(inp["g_k_gqa"][0]), lq1=f(inp["lam_q1"][0]), lk1=f(inp["lam_k1"][0]),
        lq2=f(inp["lam_q2"][0]), lk2=f(inp["lam_k2"][0]), g_sub=f(inp["g_subln"][0]),
        ln_mix_g=f(inp["ln_mix_g"]), ln_mix_b=f(inp["ln_mix_b"]), ln_ffn_g=f(inp["ln_ffn_g"]), ln_ffn_b=f(inp["ln_ffn_b"]),
        ident=np.eye(128, dtype=np.float32), rope_ax=ax, rope_seq=sq,
    )


_CACHE = {}


def kernel(**inputs):
    xp = np.asarray(inputs["x_prompt"], dtype=np.float32)
    xs = np.asarray(inputs["x_sample"], dtype=np.float32)
    hw = host_weights(inputs)
    if "nc" not in _CACHE:
        _CACHE["nc"] = build(SEQS)[0]
    nc = _CACHE["nc"]
    in_maps = []
    for c in range(N_CORES):
        xc = np.concatenate([xp[2 * c], xp[2 * c + 1], xs[c]], 0)
        m = dict(hw)
        m["x"] = np.ascontiguousarray(xc)
        in_maps.append(m)
    res = run_bass_kernel_spmd(nc, in_maps, core_ids=list(range(N_CORES)))
    yp = np.empty_like(xp)
    ys = np.empty_like(xs)
    for c in range(N_CORES):
        y = np.asarray(res.results[c]["y"], dtype=np.float32)
        yp[2 * c] = y[0:2048]
        yp[2 * c + 1] = y[2048:4096]
        ys[c] = y[4096:8192]
    return (yp, ys)
```
